# Optimizing a Trainium2 kernel written in Bass

```python
import jax, jax.numpy as jnp
from jax import lax
import numpy as np

D_MODEL = 1024
BATCH = 4
SEQ = 4096
DEPTH = 4
DEC_BATCH = 128
DEC_SEQ = 1
PAST_LEN = 8192
PAGE_SIZE = 128

N_META = 16
BLOCK = 128
WINDOW = 128
META_PAD = BLOCK - N_META
HEAD_DIM = 64
N_Q_HEADS = 8
N_KV_HEADS = 2
Q_PER_KV = N_Q_HEADS // N_KV_HEADS
ROT_DIM = HEAD_DIM // 4
ROPE_THETA = 500000.0
GLA_HEADS = 4
GLA_DK = 64
GLA_DV = 128
GLA_RANK = 16
GLA_TAU = 16.0
GLA_CHUNK = 64
D_FF = 4 * D_MODEL
ATT_W = N_Q_HEADS * HEAD_DIM
KV_W = N_KV_HEADS * HEAD_DIM
GK_W = GLA_HEADS * GLA_DK
GV_W = GLA_HEADS * GLA_DV
DEEPNORM_ALPHA = (2 * DEPTH) ** 0.25
DEEPNORM_BETA = (8 * DEPTH) ** -0.25
LN_EPS = 1e-5
RMS_EPS = 1e-6

kernel_name = 'hybrid_swa_sink_gla_deepnorm_decoder_step'


def _splits():
    return (ATT_W, KV_W, KV_W, GK_W, GK_W, GV_W, GLA_RANK, GV_W, D_MODEL, D_MODEL)


def layer_norm(x, g, b):
    xf = x.astype(jnp.float32)
    mu = jnp.mean(xf, -1, keepdims=True)
    var = jnp.mean(jnp.square(xf - mu), -1, keepdims=True)
    return ((xf - mu) * lax.rsqrt(var + LN_EPS) * g.astype(jnp.float32) + b.astype(jnp.float32)).astype(x.dtype)


def partial_rope(x, pos):
    half = ROT_DIM // 2
    inv = ROPE_THETA ** (-jnp.arange(half, dtype=jnp.float32) * 2.0 / ROT_DIM)
    ang = pos.astype(jnp.float32)[:, None] * inv[None, :]
    cos = jnp.cos(ang)[None, :, None, :]
    sin = jnp.sin(ang)[None, :, None, :]
    xf = x.astype(jnp.float32)
    x1, x2 = xf[..., :half], xf[..., half:ROT_DIM]
    out = jnp.concatenate([x1 * cos - x2 * sin, x2 * cos + x1 * sin, xf[..., ROT_DIM:]], -1)
    return out.astype(x.dtype)


def mixer_projections(x, pos, w_in, w_a2, b_a):
    B, T, _ = x.shape
    idx = np.cumsum(_splits())[:-1].tolist()
    q, k, v, gq, gk, gv, glr, gr, ga, gb = jnp.split(x @ w_in, idx, axis=-1)
    q = partial_rope(q.reshape(B, T, N_Q_HEADS, HEAD_DIM), pos)
    k = partial_rope(k.reshape(B, T, N_KV_HEADS, HEAD_DIM), pos)
    v = v.reshape(B, T, N_KV_HEADS, HEAD_DIM)
    def to_heads(t, d):
        return t.reshape(B, T, GLA_HEADS, d).astype(jnp.float32).transpose(0, 2, 1, 3)
    log_a = jax.nn.log_sigmoid((glr @ w_a2 + b_a).astype(jnp.float32)) / GLA_TAU
    gla_in = (to_heads(gq, GLA_DK) * GLA_DK ** -0.5, to_heads(gk, GLA_DK), to_heads(gv, GLA_DV), to_heads(log_a, GLA_DK))
    return q, k, v, gla_in, gr, ga, gb


def banded_sink_attention(q, k, v, mask, sink):
    s = jnp.einsum('bnqhgd,bnkhd->bnhgqk', q, k, preferred_element_type=jnp.float32) * HEAD_DIM ** -0.5
    s = jnp.where(mask[None, :, None, None], s, -jnp.inf)
    sk = sink.astype(jnp.float32)[None, None, :, :, None, None]
    m = jnp.maximum(jnp.max(s, -1, keepdims=True), sk)
    p = jnp.exp(s - m)
    p = p / (jnp.sum(p, -1, keepdims=True) + jnp.exp(sk - m))
    return jnp.einsum('bnhgqk,bnkhd->bnqhgd', p.astype(v.dtype), v)


def swa_prompt(q, k, v, sink):
    B, L = q.shape[:2]
    pad = ((0, 0), (META_PAD, 0), (0, 0), (0, 0))
    qp, kp, vp = jnp.pad(q, pad), jnp.pad(k, pad), jnp.pad(v, pad)
    nb = (L + META_PAD) // BLOCK
    qb = qp.reshape(B, nb, BLOCK, N_KV_HEADS, Q_PER_KV, HEAD_DIM)
    def band(t):
        tb = t.reshape(B, nb, BLOCK, N_KV_HEADS, HEAD_DIM)
        prev = jnp.pad(tb[:, :-1], ((0, 0), (1, 0), (0, 0), (0, 0), (0, 0)))
        return jnp.concatenate([prev, tb], axis=2)
    slot_q = jnp.arange(nb)[:, None] * BLOCK + jnp.arange(BLOCK)[None, :]
    slot_k = (jnp.arange(nb)[:, None] - 1) * BLOCK + jnp.arange(2 * BLOCK)[None, :]
    diff = slot_q[:, :, None] - slot_k[:, None, :]
    mask = (diff >= 0) & (diff < WINDOW) & (slot_k[:, None, :] >= META_PAD)
    o = banded_sink_attention(qb, band(kp), band(vp), mask, sink)
    return o.reshape(B, nb * BLOCK, ATT_W)[:, META_PAD:]


def swa_sample(q, k, v, k_buf, v_buf, sink):
    B, S = q.shape[:2]
    R = k_buf.shape[1]
    kk = jnp.concatenate([k_buf.astype(k.dtype), k], 1)
    vv = jnp.concatenate([v_buf.astype(v.dtype), v], 1)
    pos_q = PAST_LEN + jnp.arange(S)
    pos_k = PAST_LEN - R + jnp.arange(R + S)
    diff = pos_q[:, None] - pos_k[None, :]
    mask = ((diff >= 0) & (diff < WINDOW))[None]
    o = banded_sink_attention(q.reshape(B, 1, S, N_KV_HEADS, Q_PER_KV, HEAD_DIM), kk[:, None], vv[:, None], mask, sink)
    return o.reshape(B, S, ATT_W), kk[:, -R:], vv[:, -R:]


def gla_chunk(S0, q, k, v, log_a):
    C = q.shape[2]
    b = jnp.cumsum(log_a, axis=2)
    causal = jnp.tril(jnp.ones((C, C), dtype=bool))
    rel = jnp.where(causal[None, None, :, :, None], b[:, :, :, None, :] - b[:, :, None, :, :], -jnp.inf)
    A = jnp.einsum('bhtd,bhjd,bhtjd->bhtj', q, k, jnp.exp(rel))
    o = jnp.einsum('bhtd,bhde->bhte', q * jnp.exp(b), S0) + jnp.einsum('bhtj,bhje->bhte', A, v)
    b_last = b[:, :, -1:]
    S1 = jnp.exp(b_last[:, :, 0])[..., None] * S0 + jnp.einsum('bhjd,bhje->bhde', k * jnp.exp(b_last - b), v)
    return S1, o


def gla_prompt(q, k, v, log_a):
    B, H, L = q.shape[:3]
    nc = (L + META_PAD) // GLA_CHUNK
    def chunks(t):
        t = jnp.pad(t, ((0, 0), (0, 0), (META_PAD, 0), (0, 0)))
        return t.reshape(B, H, nc, GLA_CHUNK, t.shape[-1]).transpose(2, 0, 1, 3, 4)
    S0 = jnp.zeros((B, H, GLA_DK, GLA_DV), jnp.float32)
    S_fin, o = lax.scan(lambda S, c: gla_chunk(S, *c), S0, (chunks(q), chunks(k), chunks(v), chunks(log_a)))
    o = o.transpose(1, 2, 0, 3, 4).reshape(B, H, nc * GLA_CHUNK, GLA_DV)[:, :, META_PAD:]
    return o, S_fin


def gla_output(o, gr, g_norm):
    B, H, T, _ = o.shape
    o = o * lax.rsqrt(jnp.mean(o * o, -1, keepdims=True) + RMS_EPS) * g_norm.astype(jnp.float32)
    o = o.transpose(0, 2, 1, 3).reshape(B, T, GV_W)
    return (o * jax.nn.silu(gr.astype(jnp.float32))).astype(gr.dtype)


def finish_layer(x, att, gla, ga, gb, w_pa, w_pb, w_out, ln1_g, ln1_b, w_up, w_down, ln2_g, ln2_b):
    m = jax.nn.sigmoid(ga) * (att @ w_pa) + jax.nn.sigmoid(gb) * (gla @ w_pb)
    h = layer_norm(DEEPNORM_ALPHA * x + m @ w_out, ln1_g, ln1_b)
    f = jnp.square(jax.nn.relu(h @ w_up)) @ w_down
    return layer_norm(DEEPNORM_ALPHA * h + f, ln2_g, ln2_b)


def setup_inputs(seed: int = 0) -> dict:
    key = jax.random.key(seed)
    ks = jax.random.split(key, 22)
    def nrm(k, shape, scale):
        return jax.random.normal(k, shape, jnp.float32) * scale
    win_rows = min(WINDOW, PAST_LEN)
    d_in = sum(_splits())
    return {
        'x_prompt': nrm(ks[0], (BATCH, SEQ, D_MODEL), 1.0),
        'x_sample': nrm(ks[1], (DEC_BATCH, DEC_SEQ, D_MODEL), 1.0),
        'cache_k_win': nrm(ks[2], (DEPTH, DEC_BATCH, win_rows, N_KV_HEADS, HEAD_DIM), 1.0),
        'cache_v_win': nrm(ks[3], (DEPTH, DEC_BATCH, win_rows, N_KV_HEADS, HEAD_DIM), 1.0),
        'state_gla': nrm(ks[4], (DEPTH, DEC_BATCH, GLA_HEADS, GLA_DK, GLA_DV), 1.0),
        'meta_tokens': nrm(ks[5], (N_META, D_MODEL), 1.0),
        'w_in': nrm(ks[6], (DEPTH, D_MODEL, d_in), D_MODEL ** -0.5),
        'w_a2': nrm(ks[7], (DEPTH, GLA_RANK, GK_W), GLA_RANK ** -0.5),
        'b_a': nrm(ks[8], (DEPTH, GK_W), 0.1),
        'attn_sink': nrm(ks[9], (DEPTH, N_Q_HEADS), 0.5),
        'gla_norm_g': 1.0 + nrm(ks[10], (DEPTH, GLA_DV), 0.02),
        'w_proj_a': nrm(ks[11], (DEPTH, ATT_W, D_MODEL), ATT_W ** -0.5),
        'w_proj_b': nrm(ks[12], (DEPTH, GV_W, D_MODEL), GV_W ** -0.5),
        'w_out': nrm(ks[13], (DEPTH, D_MODEL, D_MODEL), DEEPNORM_BETA * D_MODEL ** -0.5),
        'ln1_g': 1.0 + nrm(ks[14], (DEPTH, D_MODEL), 0.02),
        'ln1_b': nrm(ks[15], (DEPTH, D_MODEL), 0.02),
        'w_up': nrm(ks[16], (DEPTH, D_MODEL, D_FF), D_MODEL ** -0.5),
        'w_down': nrm(ks[17], (DEPTH, D_FF, D_MODEL), DEEPNORM_BETA * D_FF ** -0.5),
        'ln2_g': 1.0 + nrm(ks[18], (DEPTH, D_MODEL), 0.02),
        'ln2_b': nrm(ks[19], (DEPTH, D_MODEL), 0.02),
    }


def reference(x_prompt, x_sample, cache_k_win, cache_v_win, state_gla, meta_tokens, w_in, w_a2, b_a, attn_sink,
              gla_norm_g, w_proj_a, w_proj_b, w_out, ln1_g, ln1_b, w_up, w_down, ln2_g, ln2_b):
    B = x_prompt.shape[0]
    meta = jnp.broadcast_to(meta_tokens[None].astype(x_prompt.dtype), (B, N_META, D_MODEL))
    xp = jnp.concatenate([meta, x_prompt], axis=1)
    xs = x_sample
    pos_p = jnp.arange(xp.shape[1])
    pos_s = PAST_LEN + jnp.arange(xs.shape[1])
    pk, pv, pst, sk, sv, sst = [], [], [], [], [], []
    for l in range(DEPTH):
        sink = attn_sink[l].reshape(N_KV_HEADS, Q_PER_KV)
        post = (w_proj_a[l], w_proj_b[l], w_out[l], ln1_g[l], ln1_b[l], w_up[l], w_down[l], ln2_g[l], ln2_b[l])
        q, k, v, g, gr, ga, gb = mixer_projections(xp, pos_p, w_in[l], w_a2[l], b_a[l])
        att = swa_prompt(q, k, v, sink)
        o, S = gla_prompt(*g)
        gla = gla_output(o, gr, gla_norm_g[l])
        pk.append(k[:, -WINDOW:])
        pv.append(v[:, -WINDOW:])
        pst.append(S.astype(xp.dtype))
        xp = finish_layer(xp, att, gla, ga, gb, *post)
        q, k, v, g, gr, ga, gb = mixer_projections(xs, pos_s, w_in[l], w_a2[l], b_a[l])
        att, kb, vb = swa_sample(q, k, v, cache_k_win[l], cache_v_win[l], sink)
        S, o = gla_chunk(state_gla[l].astype(jnp.float32), *g)
        gla = gla_output(o, gr, gla_norm_g[l])
        sk.append(kb)
        sv.append(vb)
        sst.append(S.astype(xs.dtype))
        xs = finish_layer(xs, att, gla, ga, gb, *post)
    y_prompt = xp[:, N_META:]
    y_sample = xs
    return (y_prompt, y_sample, jnp.stack(pk), jnp.stack(pv), jnp.stack(pst), jnp.stack(sk), jnp.stack(sv), jnp.stack(sst))
```

```python
import numpy as np
import contextlib
import concourse.bass as bass
import concourse.mybir as mybir

F32 = mybir.dt.float32
BF16 = mybir.dt.bfloat16
ALU = mybir.AluOpType
AF = mybir.ActivationFunctionType

NDSEM = 24
EPOCH = 20000


class R:
    __slots__ = ("ap", "res", "ps")

    def __init__(self, ap, res, ps=False):
        self.ap = ap
        self.res = res
        self.ps = ps

    def __getitem__(self, k):
        return R(self.ap[k], self.res, self.ps)


class Ins:
    __slots__ = ("eng", "fn", "deps", "needed", "done", "dma", "dsem")

    def __init__(self, eng, fn, dma):
        self.eng = eng
        self.fn = fn
        self.deps = []
        self.needed = False
        self.done = None
        self.dma = dma
        self.dsem = None


class Sched:
    ENGS = ["pe", "act", "dve", "pool", "sp"]

    def __init__(self, nc):
        self.nc = nc
        self.q = {e: [] for e in self.ENGS}
        self.last_w = {}
        self.readers = {}
        self.ndma = 0
        self.ndma_pool = 0
        self.dsem_last = [None] * NDSEM
        self.dcnt = [0] * NDSEM
        self.stack = contextlib.ExitStack()
        self.psn = 0
        self.nt = 0

    def sb(self, shape, dtype, name=None):
        self.nt += 1
        name = name or f"t{self.nt}"
        t = self.stack.enter_context(self.nc.sbuf_tensor("sb_" + name, list(shape), dtype))
        return R(t[:], name)

    def init_psum(self):
        self.banks = []
        for i in range(8):
            t = self.stack.enter_context(self.nc.psum_tensor(f"psb{i}", [128, 512], F32))
            self.banks.append(R(t[:], ("ps", i), True))

    def psum(self):
        b = self.banks[self.psn % 8]
        self.psn += 1
        return b

    def op(self, eng, fn, outs=(), ins=(), dma=False):
        i = Ins(eng, fn, dma)
        reads, writes = [], []
        for r in ins:
            if r is None or r.res is None:
                continue
            (writes if r.ps else reads).append(r.res)
        for r in outs:
            if r is None or r.res is None:
                continue
            writes.append(r.res)
        deps = {}

        def add(d, raw):
            if d is None:
                return
            if d.eng == eng and not d.dma and not dma:
                if eng == "pe":
                    return
            deps[id(d)] = d

        for r in reads:
            add(self.last_w.get(r), True)
        for w in writes:
            add(self.last_w.get(w), True)
            for rd in self.readers.get(w, {}).values():
                add(rd, False)
        if dma:
            half = NDSEM // 2
            if eng == "pool":
                k = half + self.ndma_pool % half
                self.ndma_pool += 1
            else:
                k = self.ndma % half
                self.ndma += 1
            i.dsem = k
            self.dcnt[k] += 16
            i.done = self.dcnt[k]
            if self.dsem_last[k] is not None:
                deps[id(self.dsem_last[k])] = self.dsem_last[k]
            self.dsem_last[k] = i
        i.deps = list(deps.values())
        for d in i.deps:
            d.needed = True
        for r in reads:
            self.readers.setdefault(r, {})[eng if not dma else ("dma", id(i))] = i
        for w in writes:
            self.last_w[w] = i
            self.readers[w] = {}
        self.q[eng].append(i)
        return i

    def mm(self, out, lhsT, rhs, start=True, stop=True):
        return self.op("pe", lambda e: e.matmul(out.ap, lhsT.ap, rhs.ap, start=start, stop=stop),
                       outs=[out], ins=[lhsT, rhs])

    def tr(self, out, in_, ident):
        return self.op("pe", lambda e: e.transpose(out.ap, in_.ap, ident.ap), outs=[out], ins=[in_, ident])

    def act(self, out, in_, func, bias=None, scale=None, eng="act"):
        kw = {}
        ins = [in_]
        if bias is not None:
            if isinstance(bias, R):
                kw["bias"] = bias.ap
                ins.append(bias)
            else:
                kw["bias"] = bias
        if scale is not None:
            if isinstance(scale, R):
                kw["scale"] = scale.ap
                ins.append(scale)
            else:
                kw["scale"] = scale
        return self.op(eng, lambda e: e.activation(out.ap, in_.ap, func, **kw), outs=[out], ins=ins)

    def tt(self, out, in0, in1, op, eng="dve"):
        return self.op(eng, lambda e: e.tensor_tensor(out.ap, in0.ap, in1.ap, op), outs=[out], ins=[in0, in1])

    def ts(self, out, in0, s1, op0, s2=None, op1=None, eng="dve"):
        ins = [in0]
        a1 = s1
        a2 = s2
        if isinstance(s1, R):
            ins.append(s1)
            a1 = s1.ap
        if isinstance(s2, R):
            ins.append(s2)
            a2 = s2.ap
        if op1 is None:
            return self.op(eng, lambda e: e.tensor_scalar(out.ap, in0.ap, a1, None, op0), outs=[out], ins=ins)
        return self.op(eng, lambda e: e.tensor_scalar(out.ap, in0.ap, a1, a2, op0, op1), outs=[out], ins=ins)

    def stt(self, out, in0, scalar, in1, op0, op1):
        ins = [in0, in1]
        a = scalar
        if isinstance(scalar, R):
            ins.append(scalar)
            a = scalar.ap
        return self.op("dve", lambda e: e.scalar_tensor_tensor(out.ap, in0.ap, a, in1.ap, op0, op1),
                       outs=[out], ins=ins)

    def copy(self, out, in_, eng="dve"):
        return self.op(eng, lambda e: e.tensor_copy(out.ap, in_.ap), outs=[out], ins=[in_])

    def recip(self, out, in_):
        return self.op("dve", lambda e: e.reciprocal(out.ap, in_.ap), outs=[out], ins=[in_])

    def memset(self, out, val, eng="pool"):
        return self.op(eng, lambda e: e.memset(out.ap, val), outs=[out])

    def dma(self, out, in_, eng="sp"):
        return self.op(eng, lambda e: e.dma_start(out=out.ap, in_=in_.ap), outs=[out], ins=[in_], dma=True)

    def emit(self):
        nc = self.nc
        st = self.stack
        dsems = [st.enter_context(nc.semaphore(f"dq{k}")) for k in range(NDSEM)]
        dcnt = self.dcnt
        esems = {}
        for e in self.ENGS:
            cnt = 0
            for ins in self.q[e]:
                if ins.dma:
                    ins.done = (dsems[ins.dsem], ins.done)
                elif ins.needed:
                    ep = cnt // EPOCH
                    if (e, ep) not in esems:
                        esems[(e, ep)] = st.enter_context(nc.semaphore(f"s_{e}_{ep}"))
                    cnt += 1
                    ins.done = (esems[(e, ep)], cnt - ep * EPOCH)
        q = self.q
        final_d = [(dsems[k], dcnt[k]) for k in range(NDSEM) if dcnt[k] > 0]

        def run(e, eng):
            waited = {}
            for ins in q[e]:
                for d in ins.deps:
                    sem, val = d.done
                    key = id(sem)
                    if waited.get(key, 0) < val:
                        eng.wait_ge(sem, val)
                        waited[key] = val
                inst = ins.fn(eng)
                if ins.dma:
                    inst.then_inc(ins.done[0], 16)
                elif ins.needed:
                    inst.then_inc(ins.done[0], 1)
            if e == "sp":
                for sem, val in final_d:
                    if waited.get(id(sem), 0) < val:
                        eng.wait_ge(sem, val)

        with nc.Block() as block:
            @block.tensor
            def _(eng):
                run("pe", eng)

            @block.scalar
            def _(eng):
                run("act", eng)

            @block.vector
            def _(eng):
                run("dve", eng)

            @block.gpsimd
            def _(eng):
                run("pool", eng)

            @block.sync
            def _(eng):
                run("sp", eng)


from concourse.bass_utils import run_bass_kernel_spmd

DEPTH = 4
NBLK = 33
SEQ = 4096
ALPHA = 8.0 ** 0.25
NS = 16
GROUPS = [(0, 4), (4, 4), (8, 4), (12, 4), (16, 4), (20, 4), (24, 3), (27, 3), (30, 3)]
NWST = 4


def rr(r, pat, **kw):
    return R(r.ap.rearrange(pat, **kw), r.res, r.ps)


def bc(r, shape):
    return R(r.ap.broadcast_to(list(shape)), r.res, r.ps)


def build(depth=DEPTH, groups=GROUPS, do_sample=True, nblk=NBLK):
    nc = bass.Bass("TRN2", target_bir_lowering=False)
    D = {}

    def din(name, shape):
        D[name] = nc.dram_tensor(name, list(shape), F32, kind="ExternalInput").ap()

    def dout(name, shape):
        D[name] = nc.dram_tensor(name, list(shape), F32, kind="ExternalOutput").ap()

    din("xp", [SEQ, 1024]); din("meta", [16, 1024]); din("xs", [NS, 1024])
    din("ck", [4, NS, 128, 128]); din("cv", [4, NS, 128, 128]); din("sg", [4, NS, 4, 64, 128])
    din("wtok", [4, 128, 12288]); din("win_u", [4, 6, 128, 4096]); din("wglr", [4, 128, 128])
    din("wa2", [17, 4, 256]); din("sink", [1, 32])
    din("gng", [128, 4]); din("wpa_u", [4, 2, 128, 2048]); din("wpb_u", [4, 2, 128, 2048]); din("wout_u", [4, 2, 128, 4096])
    din("lnp", [128, 4, 4, 8]); din("wup_u", [4, 8, 128, 4096]); din("wdn_u", [4, 8, 128, 4096])
    din("ident", [128, 128]); din("uneg", [128, 128]); din("unegI", [128, 128]); din("masks", [128, 4, 512])
    din("cs", [128, NBLK, 16]); din("css", [NS, 16])
    dout("y", [SEQ, 1024]); dout("ys", [NS, 1024]); dout("pk", [4, 128, 128]); dout("pv", [4, 128, 128])
    dout("pst", [4, 4, 64, 128]); dout("sk", [4, NS, 128, 128]); dout("sv", [4, NS, 128, 128])
    dout("sst", [4, NS, 4, 64, 128])
    DR = {k: R(v, None) for k, v in D.items()}

    S = Sched(nc)
    import os as _os
    _target = _os.environ.get("KSTOP", "")

    class _Stop(Exception):
        pass

    def stage(name):
        if _target and name == _target:
            raise _Stop()
    with S.stack:
        S.init_psum()
        S.nrot = 6
        MUL, ADD, SUB = ALU.mult, ALU.add, ALU.subtract

        def psum():
            b = S.banks[S.psn % S.nrot]
            S.psn += 1
            return b
        bank7 = S.banks[7]
        bank6 = S.banks[6]

        ident = S.sb([128, 128], F32, "ident"); S.dma(ident, DR["ident"])
        uneg = S.sb([128, 128], F32, "uneg"); S.dma(uneg, DR["uneg"])
        unegI = S.sb([NS, NS], F32, "unegI"); S.dma(unegI, DR["unegI"][0:NS, 0:NS])
        masks = S.sb([128, 4, 512], BF16, "masks"); S.dma(masks, DR["masks"], eng="pool")
        mcur4, mprev4, m04, mprev1 = masks[:, 0, :], masks[:, 1, :], masks[:, 2, :], masks[:, 3, :]
        cs = S.sb([128, NBLK, 16], F32, "cs"); S.dma(cs, DR["cs"])
        css = S.sb([NS, 16], F32, "css"); S.dma(css, DR["css"])
        ones_bf = S.sb([128, 128], BF16, "ones_bf"); S.memset(ones_bf, 1.0)
        onesm_bf = S.sb([128, 128], BF16, "onesm_bf"); S.memset(onesm_bf, 1.0 / 1024.0)
        cst = S.sb([128, 4], F32, "cst")
        S.memset(cst[:, 0:1], 1.0); S.memset(cst[:, 1:2], 1e-5); S.memset(cst[:, 2:3], 1e-6)
        one_col, eps_ln, eps_rms = cst[:, 0:1], cst[:, 1:2], cst[:, 2:3]
        lnp = S.sb([128, 4, 4, 8], F32, "lnp"); S.dma(lnp, DR["lnp"])
        gng = S.sb([128, 4], F32, "gng"); S.dma(gng, DR["gng"])
        wa2 = S.sb([17, 4, 256], F32, "wa2"); S.dma(wa2, DR["wa2"])
        es_all = S.sb([128, 32], F32, "es_all")
        S.dma(es_all, R(D["sink"].broadcast_to([128, 32]), None))
        S.act(es_all, es_all, AF.Exp)

        xres = S.sb([128, 8, 512], F32, "xres")
        xbf = S.sb([128, 8, 512], BF16, "xbf")
        hid_raw = S.sb([128, 16384], BF16, "hid")
        hidT = rr(hid_raw, "p (a b) -> p a b", a=32)
        gqk = R(hid_raw.ap[:, 0:4096].bitcast(F32).rearrange("p (a b) -> p a b", a=4), "hid")
        grs = rr(hid_raw[:, 4096:6144], "p (a b) -> p a b", a=4)
        attT = rr(hid_raw[:, 6144:8192], "p (a b) -> p a b", a=4)
        glaT = rr(hid_raw[:, 8192:10240], "p (a b) -> p a b", a=4)
        glrT = S.sb([17, 512], F32, "glrT"); S.memset(glrT, 1.0)
        mT = S.sb([128, 8, 512], BF16, "mT")
        wtok = S.sb([128, 8, 1536], BF16, "wtok")
        wst = [S.sb([128, 4096], BF16, f"wst{i}") for i in range(NWST)]
        wsn = [0]
        kTb = [[S.sb([128, 128], BF16, f"kT{l}_{s}") for s in range(2)] for l in range(depth)]
        vdb = [[S.sb([128, 2, 2, 64], BF16, f"vd{l}_{s}") for s in range(2)] for l in range(depth)]
        Sf = [S.sb([128, 2, 128], F32, f"Sf{l}") for l in range(depth)]
        Sbf = [S.sb([128, 2, 128], BF16, f"Sbf{l}") for l in range(depth)]
        for l in range(depth):
            S.memset(Sf[l], 0.0); S.memset(Sbf[l], 0.0)
        xin = S.sb([128, 1024], F32, "xin")
        ysb = xin
        TK = [dict(qtok=S.sb([128, 512], F32, f"qtok{p}"), ktok=S.sb([128, 128], F32, f"ktok{p}"),
                   vtok=S.sb([128, 128], F32, f"vtok{p}"), gktok=S.sb([128, 256], F32, f"gktok{p}"),
                   gvtok=S.sb([128, 512], BF16, f"gvtok{p}")) for p in range(2)]
        qtok, ktok, vtok, gktok, gvtok = (TK[0][k] for k in ("qtok", "ktok", "vtok", "gktok", "gvtok"))
        rtA = S.sb([128, 64], F32, "rtA"); rtB = S.sb([128, 64], F32, "rtB")
        qT_bf = S.sb([128, 512], BF16, "qT_bf")
        Pc = [S.sb([128, 512], BF16, f"Pc{g}") for g in range(2)]
        Pp = [S.sb([128, 512], BF16, f"Pp{g}") for g in range(2)]
        FP = [S.sb([128, 512], F32, f"F{i}") for i in range(7)]
        dtot, rec, lnv, rstd, t1 = FP[0], FP[1], FP[2], FP[3], FP[4]
        e1, sp, enb = FP[5][:, 0:256], FP[5][:, 256:512], FP[6][:, 0:256]
        sga, sgb, m1, m2 = FP[0], FP[1], FP[4], FP[5]
        msq, var, nmr, tmpA = FP[0], FP[1], FP[4], [FP[5], FP[6]]
        gvtok_f = FP[4][0:NS, :]
        ebp = S.sb([128, 2, 128], F32, "ebp"); ebn = S.sb([128, 2, 128], F32, "ebn")
        qtl = S.sb([128, 2, 128], BF16, "qtl"); ktl = S.sb([128, 2, 128], BF16, "ktl")
        ktokb = S.sb([128, 256], BF16, "ktokb")
        Am = S.sb([128, 512], BF16, "Am")
        osq = S.sb([128, 512], BF16, "osq")
        rbb = [S.sb([128, 512], BF16, f"rb{i}") for i in range(2)]
        rsb = [S.sb([128, 512], BF16, f"rsq{i}") for i in range(2)]
        m1buf = R(hid_raw.ap[:, 10240:14336].bitcast(F32).rearrange("p (a b) -> p a b", a=4), "m1buf")
        relu_t = [S.sb([128, 512], BF16, f"relu{i}") for i in range(2)]

        def wstream(src, KC, w):
            buf = wst[wsn[0] % NWST]
            wsn[0] += 1
            S.dma(buf[:, 0:KC * w], R(src, None), eng="pool")
            return R(buf.ap[:, 0:KC * w].rearrange("p (k n) -> p k n", k=KC), buf.res)

        def linear_units(units, KC, w, inT, T, evac, pre=()):
            ci = 0
            pre = list(pre)
            for ui, src in enumerate(units):
                wt = pre[ui] if ui < len(pre) else wstream(src, KC, w)
                for t0 in range(0, w, 128):
                    m = min(128, w - t0)
                    ps = psum()
                    for kc in range(KC):
                        S.mm(ps[0:m, 0:T], wt[:, kc, t0:t0 + m], inT[:, kc, 0:T], start=(kc == 0), stop=(kc == KC - 1))
                    evac(ci, ps[0:m, 0:T])
                    ci += 1

        def load_wtok(l):
            S.dma(rr(wtok, "p k n -> p (k n)"), R(D["wtok"][l], None), eng="pool")

        def tok_proj(xblk, M, cosR, sinR, tk=None):
            tk = tk or TK[0]
            qtok, ktok, vtok, gktok = tk["qtok"], tk["ktok"], tk["vtok"], tk["gktok"]
            ps_q, ps_kv, ps_gv = psum(), psum(), psum()
            for (ps, c0) in ((ps_q, 0), (ps_kv, 512), (ps_gv, 1024)):
                for kc in range(8):
                    S.mm(ps[0:M, 0:512], xblk[:, kc, :], wtok[:, kc, c0:c0 + 512], start=(kc == 0), stop=(kc == 7))
            qin = rr(ps_q[0:M, 0:512], "p (g hh d) -> p g hh d", g=2, hh=4)
            qo = rr(qtok[0:M, :], "p (hh g d) -> p g hh d", g=2, hh=4)
            S.act(qo, qin, AF.Copy)
            cosb = bc(R(cosR.ap.unsqueeze(1).unsqueeze(1), cosR.res), [M, 2, 4, 8])
            sinb = bc(R(sinR.ap.unsqueeze(1).unsqueeze(1), sinR.res), [M, 2, 4, 8])
            tA = rr(rtA[0:M, :], "p (g hh d) -> p g hh d", g=2, hh=4)
            tB = rr(rtB[0:M, :], "p (g hh d) -> p g hh d", g=2, hh=4)
            S.tt(tA, qin[:, :, :, 0:8], cosb, MUL); S.tt(tB, qin[:, :, :, 8:16], sinb, MUL)
            S.tt(qo[:, :, :, 0:8], tA, tB, SUB)
            S.tt(tA, qin[:, :, :, 8:16], cosb, MUL); S.tt(tB, qin[:, :, :, 0:8], sinb, MUL)
            S.tt(qo[:, :, :, 8:16], tA, tB, ADD)
            kin = rr(ps_kv[0:M, 0:128], "p (g d) -> p g d", g=2)
            ko = rr(ktok[0:M, :], "p (g d) -> p g d", g=2)
            S.act(ko, kin, AF.Copy)
            cos2 = bc(R(cosR.ap.unsqueeze(1), cosR.res), [M, 2, 8])
            sin2 = bc(R(sinR.ap.unsqueeze(1), sinR.res), [M, 2, 8])
            tA2 = rr(rtA[0:M, 0:16], "p (g d) -> p g d", g=2)
            tB2 = rr(rtB[0:M, 0:16], "p (g d) -> p g d", g=2)
            S.tt(tA2, kin[:, :, 0:8], cos2, MUL); S.tt(tB2, kin[:, :, 8:16], sin2, MUL)
            S.tt(ko[:, :, 0:8], tA2, tB2, SUB)
            S.tt(tA2, kin[:, :, 8:16], cos2, MUL); S.tt(tB2, kin[:, :, 0:8], sin2, MUL)
            S.tt(ko[:, :, 8:16], tA2, tB2, ADD)
            S.act(vtok[0:M, :], ps_kv[0:M, 128:256], AF.Copy)
            S.copy(gktok[0:M, :], ps_kv[0:M, 256:512])
            return ps_gv

        def gla_finish(l, o_ps, T, c0):
            n = 4 * T
            S.act(osq[:, 0:n], o_ps, AF.Square)
            ss = psum()
            S.mm(ss[:, 0:n], ones_bf, osq[:, 0:n])
            S.act(lnv[:, 0:n], ss[:, 0:n], AF.Ln, scale=1.0 / 128.0, bias=eps_rms)
            S.act(rstd[:, 0:n], lnv[:, 0:n], AF.Exp, scale=-0.5)
            S.tt(t1[:, 0:n], o_ps, rstd[:, 0:n], MUL)
            S.stt(glaT[:, :, c0:c0 + T], rr(t1[:, 0:n], "p (h t) -> p h t", h=4), gng[:, l:l + 1],
                  grs[:, :, c0:c0 + T], MUL, MUL)

        stat_pending = []

        def stat_mm(T, f):
            S.mm(bank6[:, 0:T], onesm_bf, rbb[f % 2][:, 0:T], start=(f == 0), stop=(f == 7))
            S.mm(bank7[:, 0:T], onesm_bf, rsb[f % 2][:, 0:T], start=(f == 0), stop=(f == 7))

        def stat_push(T, f):
            S.act(rbb[f % 2][:, 0:T], xres[:, f, 0:T], AF.Copy)
            S.act(rsb[f % 2][:, 0:T], xres[:, f, 0:T], AF.Square)
            stat_pending.append(f)
            if len(stat_pending) > 1:
                stat_mm(T, stat_pending.pop(0))

        def ln_fm(T, l, which, zero_pad):
            while stat_pending:
                stat_mm(T, stat_pending.pop(0))
            mean_ps, ex2_ps = bank6, bank7
            S.act(msq[:, 0:T], mean_ps[:, 0:T], AF.Square)
            S.tt(var[:, 0:T], ex2_ps[:, 0:T], msq[:, 0:T], SUB)
            S.act(lnv[:, 0:T], var[:, 0:T], AF.Ln, bias=eps_ln)
            S.act(rstd[:, 0:T], lnv[:, 0:T], AF.Exp, scale=-0.5)
            S.stt(nmr[:, 0:T], mean_ps[:, 0:T], -1.0, rstd[:, 0:T], MUL, MUL)
            for f in range(8):
                ta = tmpA[f % 2]
                S.tt(ta[:, 0:T], xres[:, f, 0:T], rstd[:, 0:T], MUL)
                S.tt(ta[:, 0:T], ta[:, 0:T], nmr[:, 0:T], ADD)
                gcol = lnp[:, l, 2 * which, f:f + 1]
                bcol = lnp[:, l, 2 * which + 1, f:f + 1]
                S.act(xbf[:, f, 0:T], ta[:, 0:T], AF.Identity, scale=gcol, bias=bcol)
                S.ts(xres[:, f, 0:T], ta[:, 0:T], gcol, MUL, bcol, ADD, eng="pool")
            if zero_pad:
                S.memset(xres[:, :, 0:112], 0.0, eng="dve")
                S.memset(xbf[:, :, 0:112], 0.0, eng="dve")

        pref = {}

        def dense_tail(l, T, zero_pad, nxt_l=None):
            for fq in range(2):
                wga = wstream(D["win_u"][l, 2 + fq], 8, 512)
                wpa = wstream(D["wpa_u"][l, fq], 4, 512)
                for ft in range(4):
                    cs_ = slice(ft * 128, ft * 128 + 128)
                    pga, ppa = psum(), psum()
                    for kc in range(8):
                        S.mm(pga[:, 0:T], wga[:, kc, cs_], xbf[:, kc, 0:T], start=(kc == 0), stop=(kc == 7))
                    for kc in range(4):
                        S.mm(ppa[:, 0:T], wpa[:, kc, cs_], attT[:, kc, 0:T], start=(kc == 0), stop=(kc == 3))
                    S.act(sga[:, 0:T], pga[:, 0:T], AF.Sigmoid)
                    S.tt(m1buf[:, ft, 0:T], ppa[:, 0:T], sga[:, 0:T], MUL)
                wgb = wstream(D["win_u"][l, 4 + fq], 8, 512)
                wpb = wstream(D["wpb_u"][l, fq], 4, 512)
                for ft in range(4):
                    f = fq * 4 + ft
                    cs_ = slice(ft * 128, ft * 128 + 128)
                    pgb, ppb = psum(), psum()
                    for kc in range(8):
                        S.mm(pgb[:, 0:T], wgb[:, kc, cs_], xbf[:, kc, 0:T], start=(kc == 0), stop=(kc == 7))
                    for kc in range(4):
                        S.mm(ppb[:, 0:T], wpb[:, kc, cs_], glaT[:, kc, 0:T], start=(kc == 0), stop=(kc == 3))
                    S.act(sgb[:, 0:T], pgb[:, 0:T], AF.Sigmoid)
                    S.tt(m2[:, 0:T], ppb[:, 0:T], sgb[:, 0:T], MUL)
                    S.tt(mT[:, f, 0:T], m1buf[:, ft, 0:T], m2[:, 0:T], ADD)

            def ev_res(ci, ps):
                S.stt(xres[:, ci, 0:T], xres[:, ci, 0:T], ALPHA, ps, MUL, ADD)
                stat_push(T, ci)
            linear_units([D["wout_u"][l, u] for u in range(2)], 8, 512, mT, T, ev_res)
            pre_up = [wstream(D["wup_u"][l, u], 8, 512) for u in range(2)]
            ln_fm(T, l, 0, False)

            def ev_up(ci, ps):
                rt = relu_t[ci % 2]
                S.act(rt[:, 0:T], ps, AF.Relu)
                S.tt(hidT[:, ci, 0:T], rt[:, 0:T], rt[:, 0:T], MUL)
            linear_units([D["wup_u"][l, u] for u in range(8)], 8, 512, xbf, T, ev_up, pre=pre_up)
            for cg in range(2):
                acc = [psum() for _ in range(4)]
                for kb in range(4):
                    wt = wstream(D["wdn_u"][l, cg * 4 + kb], 8, 512)
                    for ft in range(4):
                        for kc in range(8):
                            S.mm(acc[ft][:, 0:T], wt[:, kc, ft * 128:(ft + 1) * 128], hidT[:, kb * 8 + kc, 0:T],
                                 start=(kb == 0 and kc == 0), stop=(kb == 3 and kc == 7))
                for ft in range(4):
                    ev_res(cg * 4 + ft, acc[ft][:, 0:T])
            if nxt_l is not None:
                load_wtok(nxt_l)
                pref["gqk"] = (nxt_l, wstream(D["win_u"][nxt_l, 0], 8, 512))
            ln_fm(T, l, 1, zero_pad)

        def phase_a(l, T):
            def ev_gqk(ci, ps):
                S.copy(gqk[:, ci, 0:T], ps)
            pre = []
            if pref.get("gqk") and pref["gqk"][0] == l:
                pre = [pref.pop("gqk")[1]]
            linear_units([D["win_u"][l, 0]], 8, 512, xbf, T, ev_gqk, pre=pre)

            def ev_glr(ci, ps):
                S.copy(glrT[0:16, 0:T], ps)
            linear_units([D["wglr"][l]], 8, 16, xbf, T, ev_glr)

            def ev_gr(ci, ps):
                S.act(grs[:, ci, 0:T], ps, AF.Silu)
            linear_units([D["win_u"][l, 1]], 8, 512, xbf, T, ev_gr)

        HB = [R(xin.ap[:, 0:512], "xinA"), R(xin.ap[:, 512:1024], "xinB"), FP[5], FP[6]]
        hbn = [0]

        def next_hb():
            h = HB[hbn[0] % 4]
            hbn[0] += 1
            return h

        def load_x(b0, nb):
            for j in range(nb):
                n = b0 + j
                for half in range(2):
                    hb = next_hb()
                    cs_ = slice(half * 512, (half + 1) * 512)
                    if n == 0:
                        S.memset(hb, 0.0, eng="dve")
                        S.dma(hb[112:128, :], DR["meta"][:, cs_])
                    else:
                        S.dma(hb, DR["xp"][(n - 1) * 128:n * 128, cs_])
                    ps = psum()
                    for ff in range(4):
                        S.tr(ps[:, ff * 128:(ff + 1) * 128], hb[:, ff * 128:(ff + 1) * 128], ident)
                    pv = rr(ps[:, 0:512], "p (f t) -> p f t", f=4)
                    S.copy(xres[:, half * 4:half * 4 + 4, j * 128:(j + 1) * 128], pv)
                    S.act(xbf[:, half * 4:half * 4 + 4, j * 128:(j + 1) * 128], pv, AF.Copy)

        def store_y(b0, nb):
            for j in range(nb):
                n = b0 + j
                if n == 0:
                    continue
                for half in range(2):
                    hb = next_hb()
                    ps = psum()
                    for ff in range(4):
                        f = half * 4 + ff
                        S.tr(ps[:, ff * 128:(ff + 1) * 128], xres[:, f, j * 128:(j + 1) * 128], ident)
                    if half == 0:
                        S.copy(hb, ps[:, 0:512])
                    else:
                        S.act(hb, ps[:, 0:512], AF.Copy)
                    S.dma(DR["y"][(n - 1) * 128:n * 128, half * 512:(half + 1) * 512], hb)

        hooks = []

        def hook():
            if hooks:
                hooks.pop(0)()

        PcB = [R(hid_raw.ap[:, 14336 + g * 512:14336 + (g + 1) * 512], ("PcB", g)) for g in range(2)]
        PpB = [R(hid_raw.ap[:, 15360 + g * 512:15360 + (g + 1) * 512], ("PpB", g)) for g in range(2)]
        BT = [dict(qT_bf=qT_bf, Pc=Pc, Pp=Pp, ebp=ebp, ebn=ebn, qtl=qtl, ktl=ktl, ktokb=ktokb, Am=Am),
              dict(qT_bf=S.sb([128, 512], BF16, "qT_bfB"), Pc=PcB, Pp=PpB,
                   ebp=S.sb([128, 2, 128], F32, "ebpB"), ebn=S.sb([128, 2, 128], F32, "ebnB"),
                   qtl=S.sb([128, 2, 128], BF16, "qtlB"), ktl=S.sb([128, 2, 128], BF16, "ktlB"),
                   ktokb=S.sb([128, 256], BF16, "ktokbB"), Am=S.sb([128, 512], BF16, "AmB"))]

        def tok_stage(l, n, j):
            tk = TK[j % 2]
            csl = slice(j * 128, j * 128 + 128)
            ps_gv = tok_proj(xbf[:, :, csl], 128, cs[:, n, 0:8], cs[:, n, 8:16], tk)
            S.act(tk["gvtok"], ps_gv[:, 0:512], AF.Copy)
            if n == nblk - 1:
                S.dma(DR["pk"][l], tk["ktok"])
                S.dma(DR["pv"][l], tk["vtok"])

        def st_T(l, n, j):
            tk, bt = TK[j % 2], BT[j % 2]
            qT_ps = psum()
            for hh in range(4):
                S.tr(qT_ps[:, hh * 128:(hh + 1) * 128], tk["qtok"][:, hh * 128:(hh + 1) * 128], ident)
            S.copy(bt["qT_bf"], qT_ps[:, 0:512])
            kT_ps = psum()
            S.tr(kT_ps[:, 0:128], tk["ktok"], ident)
            S.act(kTb[l][n % 2], kT_ps[:, 0:128], AF.Copy)

        def st_vd(l, n, j):
            tk = TK[j % 2]
            vd = vdb[l][n % 2]
            v3 = rr(tk["vtok"], "p (g d) -> p g d", g=2)
            S.copy(vd[:, :, 0, :], v3); S.act(vd[:, :, 1, :], v3, AF.Copy)

        def st_S(l, n, j):
            bt = BT[j % 2]
            kTc, kTp = kTb[l][n % 2], kTb[l][(n - 1) % 2]
            maskc = m04 if n == 0 else mcur4
            maskp = mprev1 if n == 1 else mprev4
            for g in range(2):
                rs_ = slice(g * 64, g * 64 + 64)
                Sc = psum()
                S.mm(Sc[:, 0:512], kTc[rs_, :], bt["qT_bf"][rs_, :])
                S.act(bt["Pc"][g], Sc[:, 0:512], AF.Exp, scale=0.125)
                S.tt(bt["Pc"][g], bt["Pc"][g], maskc, MUL)
                if n > 0:
                    Sp_ = psum()
                    S.mm(Sp_[:, 0:512], kTp[rs_, :], bt["qT_bf"][rs_, :])
                    S.act(bt["Pp"][g], Sp_[:, 0:512], AF.Exp, scale=0.125)
                    S.tt(bt["Pp"][g], bt["Pp"][g], maskp, MUL)

        def st_PV(l, n, j, g):
            bt = BT[j % 2]
            csl = slice(j * 128, j * 128 + 128)
            vd, vdp = vdb[l][n % 2], vdb[l][(n - 1) % 2]
            Pc_, Pp_ = bt["Pc"], bt["Pp"]
            O, Dn = psum(), psum()
            vc = rr(vd[:, g], "p a d -> p (a d)")
            if n > 0:
                vp = rr(vdp[:, g], "p a d -> p (a d)")
                S.mm(O[:, 0:512], vp, Pp_[g], start=True, stop=False)
                S.mm(O[:, 0:512], vc, Pc_[g], start=False, stop=True)
                S.mm(Dn[:, 0:512], ones_bf, Pp_[g], start=True, stop=False)
                S.mm(Dn[:, 0:512], ones_bf, Pc_[g], start=False, stop=True)
            else:
                S.mm(O[:, 0:512], vc, Pc_[g])
                S.mm(Dn[:, 0:512], ones_bf, Pc_[g])
            for hh in range(4):
                hc = slice(hh * 128, (hh + 1) * 128)
                S.act(dtot[:, hc], Dn[:, hc], AF.Ln, bias=es_all[:, l * 8 + g * 4 + hh:l * 8 + g * 4 + hh + 1])
            S.act(rec, dtot, AF.Exp, scale=-1.0)
            for hh in range(4):
                h = 4 * g + hh
                r0 = (h % 2) * 64
                S.tt(attT[r0:r0 + 64, h // 2, csl], O[r0:r0 + 64, hh * 128:(hh + 1) * 128],
                     rec[r0:r0 + 64, hh * 128:(hh + 1) * 128], MUL)

        def st_GA(l, n, j):
            tk, bt = TK[j % 2], BT[j % 2]
            csl = slice(j * 128, j * 128 + 128)
            z = psum()
            S.mm(z[:, 0:256], glrT[0:17, csl], wa2[0:17, l, :])
            S.act(e1, z[:, 0:256], AF.Exp, scale=-1.0)
            S.act(sp, e1, AF.Ln, bias=one_col)
            bt_ps = psum()
            S.mm(bt_ps[:, 0:256], uneg, sp)
            bT = psum()
            for i in range(2):
                S.mm(bT[:, i * 128:(i + 1) * 128], sp[:, i * 128:(i + 1) * 128], uneg)
            bT3 = rr(bT[:, 0:256], "p (i t) -> p i t", i=2)
            S.act(bt["ebp"], bT3, AF.Exp)
            S.act(bt["ebn"], bT3, AF.Exp, scale=-1.0)
            S.act(enb, bt_ps[:, 0:256], AF.Exp, scale=-1.0)
            S.stt(bt["qtl"], gqk[:, 0:2, csl], 0.125, bt["ebp"], MUL, MUL)
            S.tt(bt["ktl"], gqk[:, 2:4, csl], bt["ebn"], MUL)
            S.tt(bt["ktokb"], tk["gktok"], enb, MUL)

        def st_GB1(l, n, j):
            bt = BT[j % 2]
            qtl_, ktl_, Am_ = bt["qtl"], bt["ktl"], bt["Am"]
            A2 = [psum(), psum()]
            for h in range(4):
                i, r0 = h // 2, (h % 2) * 64
                S.mm(A2[h % 2][:, h * 128:(h + 1) * 128], ktl_[r0:r0 + 64, i, :], qtl_[r0:r0 + 64, i, :])
            Am4 = rr(Am_, "p (a b c) -> p a b c", a=2, b=2)
            mk4 = rr(mcur4, "p (a b c) -> p a b c", a=2, b=2)
            for par in range(2):
                S.tt(Am4[:, :, par, :], rr(A2[par][:, 0:512], "p (a b c) -> p a b c", a=2, b=2)[:, :, par, :],
                     mk4[:, :, par, :], MUL)

        def st_GB2(l, n, j):
            tk, bt = TK[j % 2], BT[j % 2]
            c0 = j * 128
            gvtok_, qtl_, Am_, ktokb_, ebp_ = tk["gvtok"], bt["qtl"], bt["Am"], bt["ktokb"], bt["ebp"]
            o = psum()
            for h in range(4):
                i, r0 = h // 2, (h % 2) * 64
                hs = slice(h * 128, (h + 1) * 128)
                S.mm(o[:, hs], gvtok_[:, hs], Am_[:, hs], start=True, stop=False)
                S.mm(o[:, hs], Sbf[l][r0:r0 + 64, i, :], qtl_[r0:r0 + 64, i, :], start=False, stop=True)
            for i in range(2):
                U = psum()
                S.mm(U[:, 0:256], ktokb_[:, i * 128:(i + 1) * 128], gvtok_[:, i * 256:(i + 1) * 256])
                for hh in range(2):
                    r0 = hh * 64
                    S.tt(Sf[l][r0:r0 + 64, i, :], U[r0:r0 + 64, hh * 128:(hh + 1) * 128], Sf[l][r0:r0 + 64, i, :], ADD)
                S.ts(Sf[l][:, i, :], Sf[l][:, i, :], ebp_[:, i, 127:128], MUL)
                S.act(Sbf[l][:, i, :], Sf[l][:, i, :], AF.Copy)
            gla_finish(l, o[:, 0:512], 128, c0)
            if n == nblk - 1:
                dst = D["pst"][l].rearrange("(i hh) dk dv -> (hh dk) i dv", hh=2)
                S.dma(R(dst, None), Sf[l])

        def block_phase(l, b0, nb):
            tok_stage(l, b0, 0)
            if nb > 1:
                tok_stage(l, b0 + 1, 1)
            st_T(l, b0, 0); st_vd(l, b0, 0); st_S(l, b0, 0); hook(); st_GA(l, b0, 0); st_GB1(l, b0, 0); hook()
            for j in range(nb):
                n = b0 + j
                nxt = j + 1 < nb
                st_PV(l, n, j, 0)
                if nxt:
                    st_T(l, n + 1, j + 1)
                hook()
                st_PV(l, n, j, 1)
                if nxt:
                    st_vd(l, n + 1, j + 1)
                    st_S(l, n + 1, j + 1)
                hook()
                st_GB2(l, n, j)
                hook()
                if nxt:
                    st_GA(l, n + 1, j + 1)
                    hook()
                    st_GB1(l, n + 1, j + 1)
                if j + 2 < nb:
                    tok_stage(l, n + 2, j + 2)
                hook()

        if do_sample:
            Ss = S.sb([128, 4, 2, 128], F32, "Ss")
            kwin = R(hid_raw.ap[:, 10240:14336].bitcast(F32).rearrange("p (a b c) -> p a b c", a=2, b=8), "m1buf")
            TKS = dict(qtok=FP[2][0:NS, :], ktok=S.sb([NS, 128], F32, "ktok_s"), vtok=S.sb([NS, 128], F32, "vtok_s"),
                       gktok=S.sb([NS, 256], F32, "gktok_s"))
            gvs = S.sb([NS, 512], F32, "gvs")
            KwT = [S.sb([128, 128], BF16, f"KwT{i}") for i in range(2)]
            Vwd = [S.sb([128, 2, 2, 64], BF16, f"Vwd{i}") for i in range(2)]
            qTs = S.sb([128, 4, NS], BF16, "qTs")
            Ps = S.sb([128, 128], BF16, "Ps")
            ebs = S.sb([128, 2, NS], F32, "ebs")
            qsT = S.sb([128, 2, NS], F32, "qsT")
            kmask = [FP[6][0:NS, 0:256], FP[6][0:NS, 256:512]]

        def sample_parts(l, sc0=0):
            M = NS
            tk = TKS
            qtok, ktok, vtok, gktok = tk["qtok"], tk["ktok"], tk["vtok"], tk["gktok"]
            gvtok_f = gvs

            skR = R(D["sk"][l], ("sk", l))
            svR = R(D["sv"][l], ("sv", l))

            def stA(half):
                h8 = half * 8
                for a_, (cin, outR) in enumerate(((D["ck"], skR), (D["cv"], svR))):
                    S.dma(kwin[0:127, a_], R(cin[l, h8:h8 + 8, 1:128, :].rearrange("b r c -> r b c"), None))
                    S.dma(kwin[127:128, a_], R(outR.ap[h8:h8 + 8, 127:128, :].rearrange("b r c -> r b c"), outR.res))
                    S.dma(R(outR.ap[h8:h8 + 8, 0:127, :].rearrange("b r c -> r b c"), None), kwin[0:127, a_])

            def stB(bi):
                Kw, Vw, KT, Vd = kwin[:, 0, bi % 8, :], kwin[:, 1, bi % 8, :], KwT[bi % 2], Vwd[bi % 2]
                kt_ps = psum()
                S.tr(kt_ps[:, 0:128], Kw, ident)
                S.act(KT, kt_ps[:, 0:128], AF.Copy)
                v3 = rr(Vw, "p (g d) -> p g d", g=2)
                S.copy(Vd[:, :, 0, :], v3); S.act(Vd[:, :, 1, :], v3, AF.Copy)
                for g in range(2):
                    rs_ = slice(g * 64, g * 64 + 64)
                    sc = psum()
                    S.mm(sc[:, 0:4], KT[rs_, :], qTs[rs_, :, bi])
                    S.act(Ps[:, bi * 8 + g * 4:bi * 8 + g * 4 + 4], sc[:, 0:4], AF.Exp, scale=0.125)
                for g in range(2):
                    c = bi * 8 + g * 4
                    S.mm(bank7[:, c:c + 4], rr(Vd[:, g], "p a d -> p (a d)"), Ps[:, c:c + 4])

            def p_pre():
                ps_gv = tok_proj(xbf[:, :, sc0:sc0 + M], M, css[:, 0:8], css[:, 8:16], tk)
                S.act(gvtok_f, ps_gv[0:M, 0:512], AF.Copy)
                qT_ps = psum()
                for hh in range(4):
                    S.tr(qT_ps[:, hh * M:(hh + 1) * M], qtok[0:M, hh * 128:(hh + 1) * 128], ident[0:M, 0:M])
                S.copy(qTs, rr(qT_ps[:, 0:4 * M], "p (h b) -> p h b", h=4))
                S.dma(skR[:, 127, :], ktok[0:M, :])
                S.dma(svR[:, 127, :], vtok[0:M, :])
                stA(0)

            def p_bi(b0_):
                def f():
                    if b0_ == 8:
                        stA(1)
                    for bi in (b0_, b0_ + 1):
                        stB(bi)
                return f

            def p_post():
                Dn = psum()
                S.mm(Dn[:, 0:128], ones_bf, Ps)
                essb = bc(R(es_all.ap[:, l * 8:(l + 1) * 8].unsqueeze(1), es_all.res), [128, NS, 8])
                S.tt(rr(dtot[:, 0:128], "p (b h) -> p b h", h=8), rr(Dn[:, 0:128], "p (b h) -> p b h", h=8), essb, ADD)
                S.recip(rec[:, 0:128], dtot[:, 0:128])
                O3 = rr(bank7[:, 0:128], "p (b h) -> p h b", h=8)
                r3 = rr(rec[:, 0:128], "p (b h) -> p h b", h=8)
                for h in range(8):
                    r0 = (h % 2) * 64
                    S.tt(attT[r0:r0 + 64, h // 2, sc0:sc0 + M], O3[r0:r0 + 64, h, :], r3[r0:r0 + 64, h, :], MUL)
                z = psum()
                S.mm(z[0:M, 0:256], glrT[0:17, sc0:sc0 + M], wa2[0:17, l, :])
                S.act(e1[0:M, :], z[0:M, 0:256], AF.Exp, scale=-1.0)
                S.act(sp[0:M, :], e1[0:M, :], AF.Ln, bias=one_col[0:M, :])
                bT = psum()
                for i in range(2):
                    S.mm(bT[:, i * M:(i + 1) * M], sp[0:M, i * 128:(i + 1) * 128], unegI[0:M, 0:M])
                S.act(ebs, rr(bT[:, 0:2 * M], "p (i b) -> p i b", i=2), AF.Exp)
                S.ts(qsT, gqk[:, 0:2, sc0:sc0 + M], 0.125, MUL)

            def p_gla(b4):
                def f():
                    src = D["sg"][l, b4:b4 + 4].rearrange("b (i hh) dk dv -> (hh dk) b i dv", hh=2)
                    S.dma(Ss, R(src, None))
                    for bb in range(4):
                        b = b4 + bb
                        km = kmask[b % 2]
                        S.ts(km, gktok[0:M, :], ident[0:M, b:b + 1], MUL)
                        for i in range(2):
                            U = psum()
                            S.mm(U[:, 0:256], km[:, i * 128:(i + 1) * 128], gvtok_f[:, i * 256:(i + 1) * 256])
                            for hh in range(2):
                                r0 = hh * 64
                                S.stt(Ss[r0:r0 + 64, bb, i, :], Ss[r0:r0 + 64, bb, i, :], ebs[r0:r0 + 64, i, b:b + 1],
                                      U[r0:r0 + 64, hh * 128:(hh + 1) * 128], MUL, ADD)
                            for hh in range(2):
                                r0 = hh * 64
                                h = 2 * i + hh
                                ob = bank7 if hh == 0 else bank6
                                S.mm(ob[:, 256 + h * M + b:256 + h * M + b + 1], Ss[r0:r0 + 64, bb, i, :],
                                     qsT[r0:r0 + 64, i, b:b + 1])
                    dst = D["sst"][l, b4:b4 + 4].rearrange("b (i hh) dk dv -> (hh dk) b i dv", hh=2)
                    S.dma(R(dst, None), Ss)
                return f

            def p_fin():
                o_sb = FP[6][:, 0:4 * M]
                o4 = rr(o_sb, "p (i hh b) -> p i hh b", i=2, hh=2)
                for hh, ob in ((0, bank7), (1, bank6)):
                    S.copy(o4[:, :, hh, :], rr(ob[:, 256:256 + 4 * M], "p (i hh b) -> p i hh b", i=2, hh=2)[:, :, hh, :])
                gla_finish(l, o_sb, M, sc0)

            return [p_pre] + [p_bi(b0_) for b0_ in range(0, NS, 2)] + [p_post] + [p_gla(b4) for b4 in range(0, NS, 4)] + [p_fin]

        def load_sample(sc0):
            M = NS
            for half in range(2):
                hb = next_hb()
                S.dma(hb[0:M, :], DR["xs"][:, half * 512:(half + 1) * 512])
                ps = psum()
                for ff in range(4):
                    S.tr(ps[:, ff * M:(ff + 1) * M], hb[0:M, ff * 128:(ff + 1) * 128], ident[0:M, 0:M])
                pv = rr(ps[:, 0:4 * M], "p (f t) -> p f t", f=4)
                S.copy(xres[:, half * 4:half * 4 + 4, sc0:sc0 + M], pv)
                S.act(xbf[:, half * 4:half * 4 + 4, sc0:sc0 + M], pv, AF.Copy)

        def store_sample(sc0):
            M = NS
            for half in range(2):
                hb = next_hb()
                ps = psum()
                for ff in range(4):
                    f = half * 4 + ff
                    S.tr(ps[0:M, ff * 128:(ff + 1) * 128], xres[:, f, sc0:sc0 + M], ident)
                S.copy(hb[0:M, :], ps[0:M, 0:512])
                S.dma(DR["ys"][:, half * 512:(half + 1) * 512], hb[0:M, :])

        def main_program():
            stage("consts")
            for gi, (b0, nb) in enumerate(groups):
                T = nb * 128
                merged = do_sample and gi == len(groups) - 1 and nb <= 3
                load_x(b0, nb)
                if b0 == 0:
                    S.memset(xres[:, :, 0:112], 0.0, eng="dve")
                    S.memset(xbf[:, :, 0:112], 0.0, eng="dve")
                if merged:
                    load_sample(T)
                    T = T + NS
                stage("loadx")
                for l in range(depth):
                    if not (pref.get("gqk") and pref["gqk"][0] == l):
                        load_wtok(l)
                    stage("wtok")
                    phase_a(l, T)
                    stage("phase_a")
                    if merged:
                        hooks.extend(sample_parts(l, nb * 128))
                    block_phase(l, b0, nb)
                    while hooks:
                        hooks.pop(0)()
                    stage("blocks")
                    last_gl = (gi == len(groups) - 1 and l == depth - 1)
                    dense_tail(l, T, b0 == 0, None if last_gl else (l + 1) % depth)
                    stage("dense")
                store_y(b0, nb)
                if merged:
                    store_sample(nb * 128)
                stage("store_y")
            if do_sample and not merged:
                M = NS
                load_sample(0)
                for l in range(depth):
                    load_wtok(l)
                    phase_a(l, M)
                    stage("s_phase_a")
                    for p in sample_parts(l, 0):
                        p()
                    stage("s_mixer")
                    dense_tail(l, M, False)
                store_sample(0)

        try:
            main_program()
        except _Stop:
            pass
        S.emit()
    return nc


def _consts():
    j = np.arange(128)[:, None]
    i = np.arange(128)[None, :]
    mcur = (j <= i).astype(np.float32)
    mprev = (j > i).astype(np.float32)
    m0 = ((j <= i) & (j >= 112)).astype(np.float32)
    mprev1 = ((j > i) & (j >= 112)).astype(np.float32)
    masks = np.stack([np.tile(m, (1, 4)) for m in (mcur, mprev, m0, mprev1)], axis=1)
    uneg = (-(1.0 / 16.0) * (j <= i)).astype(np.float32)
    unegI = (-(1.0 / 16.0) * np.eye(128)).astype(np.float32)
    inv = (500000.0 ** (-np.arange(8, dtype=np.float32) * 2.0 / 16.0)).astype(np.float32)
    slot = np.arange(NBLK)[None, :] * 128 + np.arange(128)[:, None]
    pos = np.maximum(slot - 112, 0).astype(np.float32)
    ang = pos[:, :, None] * inv[None, None, :]
    cs = np.concatenate([np.cos(ang), np.sin(ang)], -1).astype(np.float32)
    angs = (np.float32(8192.0) * inv)[None, :].repeat(NS, 0)
    css = np.concatenate([np.cos(angs), np.sin(angs)], -1).astype(np.float32)
    return dict(ident=np.eye(128, dtype=np.float32), uneg=uneg, unegI=unegI,
                masks=np.ascontiguousarray(masks), cs=np.ascontiguousarray(cs), css=np.ascontiguousarray(css))


_NC_CACHE = {}


def _prep_shared(meta_tokens, w_in, w_a2, b_a, attn_sink, gla_norm_g, w_proj_a, w_proj_b, w_out, ln1_g, ln1_b,
                 w_up, w_down, ln2_g, ln2_b):
    f = lambda a: np.ascontiguousarray(np.asarray(a, dtype=np.float32))
    w_in = f(w_in)
    lnp = np.stack([f(ln1_g), f(ln1_b), f(ln2_g), f(ln2_b)], axis=1)
    lnp = lnp.reshape(4, 4, 8, 128).transpose(3, 0, 1, 2)
    ucols = np.concatenate([np.arange(c, c + 512) for c in (768, 1808, 2320, 2832, 3344, 3856)])
    win_u = w_in[:, :, ucols].reshape(4, 8, 128, 6, 512).transpose(0, 3, 2, 1, 4).reshape(4, 6, 128, 4096)
    tcols = np.concatenate([np.arange(0, 768), np.arange(1024, 1792)])
    wtok = w_in[:, :, tcols].reshape(4, 8, 128, 1536).transpose(0, 2, 1, 3).reshape(4, 128, 12288)
    wglr = w_in[:, :, 1792:1808].reshape(4, 8, 128, 16).transpose(0, 2, 1, 3).reshape(4, 128, 128)
    wpa_u = f(w_proj_a).reshape(4, 4, 128, 2, 512).transpose(0, 3, 2, 1, 4).reshape(4, 2, 128, 2048)
    wpb_u = f(w_proj_b).reshape(4, 4, 128, 2, 512).transpose(0, 3, 2, 1, 4).reshape(4, 2, 128, 2048)
    wout_u = f(w_out).reshape(4, 8, 128, 2, 512).transpose(0, 3, 2, 1, 4).reshape(4, 2, 128, 4096)
    wup_u = f(w_up).reshape(4, 8, 128, 8, 512).transpose(0, 3, 2, 1, 4).reshape(4, 8, 128, 4096)
    wdn_u = f(w_down).reshape(4, 4, 8, 128, 2, 512).transpose(0, 4, 1, 3, 2, 5).reshape(4, 8, 128, 4096)
    cst = _consts()
    return dict(
        meta=f(meta_tokens), wtok=f(wtok), win_u=f(win_u), wglr=f(wglr),
        wa2=f(np.concatenate([np.transpose(f(w_a2), (1, 0, 2)), f(b_a)[None]], 0)),
        sink=f(attn_sink).reshape(1, 32), gng=f(np.transpose(f(gla_norm_g), (1, 0))), wpa_u=f(wpa_u), wpb_u=f(wpb_u),
        wout_u=f(wout_u), lnp=f(lnp), wup_u=f(wup_u), wdn_u=f(wdn_u), **cst)


def kernel(x_prompt, x_sample, cache_k_win, cache_v_win, state_gla, meta_tokens, w_in, w_a2, b_a, attn_sink,
           gla_norm_g, w_proj_a, w_proj_b, w_out, ln1_g, ln1_b, w_up, w_down, ln2_g, ln2_b):
    f = lambda a: np.ascontiguousarray(np.asarray(a, dtype=np.float32))
    x_prompt, x_sample, cache_k_win, cache_v_win, state_gla = map(f, (x_prompt, x_sample, cache_k_win, cache_v_win, state_gla))
    if "nc" not in _NC_CACHE:
        _NC_CACHE["nc"] = build()
    nc = _NC_CACHE["nc"]
    shared = _prep_shared(meta_tokens, w_in, w_a2, b_a, attn_sink, gla_norm_g, w_proj_a, w_proj_b, w_out,
                          ln1_g, ln1_b, w_up, w_down, ln2_g, ln2_b)
    in_maps = []
    for c in range(8):
        sl = slice(c * NS, (c + 1) * NS)
        m = dict(shared)
        m["xp"] = x_prompt[c % 4]
        m["xs"] = f(x_sample[sl, 0, :])
        m["ck"] = f(cache_k_win[:, sl].reshape(4, NS, 128, 128))
        m["cv"] = f(cache_v_win[:, sl].reshape(4, NS, 128, 128))
        m["sg"] = f(state_gla[:, sl])
        in_maps.append(m)
    res = run_bass_kernel_spmd(nc, in_maps, core_ids=list(range(8)))
    rs = res.results
    y_prompt = np.stack([rs[c]["y"] for c in range(4)], 0)
    y_sample = np.concatenate([rs[c]["ys"] for c in range(8)], 0).reshape(128, 1, 1024)
    pk = np.stack([rs[c]["pk"] for c in range(4)], 1).reshape(4, 4, 128, 2, 64)
    pv = np.stack([rs[c]["pv"] for c in range(4)], 1).reshape(4, 4, 128, 2, 64)
    pst = np.stack([rs[c]["pst"] for c in range(4)], 1)
    sk = np.concatenate([rs[c]["sk"] for c in range(8)], 1).reshape(4, 128, 128, 2, 64)
    sv = np.concatenate([rs[c]["sv"] for c in range(8)], 1).reshape(4, 128, 128, 2, 64)
    sst = np.concatenate([rs[c]["sst"] for c in range(8)], 1)
    return tuple(np.ascontiguousarray(a, dtype=np.float32) for a in (y_prompt, y_sample, pk, pv, pst, sk, sv, sst))
```

```python
import numpy as np
import contextlib
import concourse.bass as bass
import concourse.mybir as mybir

F32 = mybir.dt.float32
BF16 = mybir.dt.bfloat16
ALU = mybir.AluOpType
AF = mybir.ActivationFunctionType

NDSEM = 24
EPOCH = 20000


class R:
    __slots__ = ("ap", "res", "ps")

    def __init__(self, ap, res, ps=False):
        self.ap = ap
        self.res = res
        self.ps = ps

    def __getitem__(self, k):
        return R(self.ap[k], self.res, self.ps)


class Ins:
    __slots__ = ("eng", "fn", "deps", "needed", "done", "dma", "dsem")

    def __init__(self, eng, fn, dma):
        self.eng = eng
        self.fn = fn
        self.deps = []
        self.needed = False
        self.done = None
        self.dma = dma
        self.dsem = None


class Sched:
    ENGS = ["pe", "act", "dve", "pool", "sp"]

    def __init__(self, nc):
        self.nc = nc
        self.q = {e: [] for e in self.ENGS}
        self.last_w = {}
        self.readers = {}
        self.ndma = 0
        self.ndma_pool = 0
        self.dsem_last = [None] * NDSEM
        self.dcnt = [0] * NDSEM
        self.stack = contextlib.ExitStack()
        self.psn = 0
        self.nt = 0

    def sb(self, shape, dtype, name=None):
        self.nt += 1
        name = name or f"t{self.nt}"
        t = self.stack.enter_context(self.nc.sbuf_tensor("sb_" + name, list(shape), dtype))
        return R(t[:], name)

    def init_psum(self):
        self.banks = []
        for i in range(8):
            t = self.stack.enter_context(self.nc.psum_tensor(f"psb{i}", [128, 512], F32))
            self.banks.append(R(t[:], ("ps", i), True))

    def psum(self):
        b = self.banks[self.psn % 8]
        self.psn += 1
        return b

    def op(self, eng, fn, outs=(), ins=(), dma=False):
        i = Ins(eng, fn, dma)
        reads, writes = [], []
        for r in ins:
            if r is None or r.res is None:
                continue
            (writes if r.ps else reads).extend(r.res if isinstance(r.res, list) else [r.res])
        for r in outs:
            if r is None or r.res is None:
                continue
            writes.extend(r.res if isinstance(r.res, list) else [r.res])
        deps = {}

        def add(d, raw):
            if d is None:
                return
            if d.eng == eng and not d.dma and not dma:
                if eng == "pe":
                    return
            deps[id(d)] = d

        for r in reads:
            add(self.last_w.get(r), True)
        for w in writes:
            add(self.last_w.get(w), True)
            for rd in self.readers.get(w, {}).values():
                add(rd, False)
        if dma:
            half = NDSEM // 2
            if eng == "pool":
                k = half + self.ndma_pool % half
                self.ndma_pool += 1
            else:
                k = self.ndma % half
                self.ndma += 1
            i.dsem = k
            self.dcnt[k] += 16
            i.done = self.dcnt[k]
            if self.dsem_last[k] is not None:
                deps[id(self.dsem_last[k])] = self.dsem_last[k]
            self.dsem_last[k] = i
        i.deps = list(deps.values())
        for d in i.deps:
            d.needed = True
        for r in reads:
            self.readers.setdefault(r, {})[eng if not dma else ("dma", id(i))] = i
        for w in writes:
            self.last_w[w] = i
            self.readers[w] = {}
        self.q[eng].append(i)
        return i

    def mm(self, out, lhsT, rhs, start=True, stop=True):
        return self.op("pe", lambda e: e.matmul(out.ap, lhsT.ap, rhs.ap, start=start, stop=stop),
                       outs=[out], ins=[lhsT, rhs])

    def tr(self, out, in_, ident):
        return self.op("pe", lambda e: e.transpose(out.ap, in_.ap, ident.ap), outs=[out], ins=[in_, ident])

    def act(self, out, in_, func, bias=None, scale=None, eng="act"):
        kw = {}
        ins = [in_]
        if bias is not None:
            if isinstance(bias, R):
                kw["bias"] = bias.ap
                ins.append(bias)
            else:
                kw["bias"] = bias
        if scale is not None:
            if isinstance(scale, R):
                kw["scale"] = scale.ap
                ins.append(scale)
            else:
                kw["scale"] = scale
        return self.op(eng, lambda e: e.activation(out.ap, in_.ap, func, **kw), outs=[out], ins=ins)

    def tt(self, out, in0, in1, op, eng="dve"):
        return self.op(eng, lambda e: e.tensor_tensor(out.ap, in0.ap, in1.ap, op), outs=[out], ins=[in0, in1])

    def ts(self, out, in0, s1, op0, s2=None, op1=None, eng="dve"):
        ins = [in0]
        a1 = s1
        a2 = s2
        if isinstance(s1, R):
            ins.append(s1)
            a1 = s1.ap
        if isinstance(s2, R):
            ins.append(s2)
            a2 = s2.ap
        if op1 is None:
            return self.op(eng, lambda e: e.tensor_scalar(out.ap, in0.ap, a1, None, op0), outs=[out], ins=ins)
        return self.op(eng, lambda e: e.tensor_scalar(out.ap, in0.ap, a1, a2, op0, op1), outs=[out], ins=ins)

    def stt(self, out, in0, scalar, in1, op0, op1):
        ins = [in0, in1]
        a = scalar
        if isinstance(scalar, R):
            ins.append(scalar)
            a = scalar.ap
        return self.op("dve", lambda e: e.scalar_tensor_tensor(out.ap, in0.ap, a, in1.ap, op0, op1),
                       outs=[out], ins=ins)

    def copy(self, out, in_, eng="dve"):
        return self.op(eng, lambda e: e.tensor_copy(out.ap, in_.ap), outs=[out], ins=[in_])

    def recip(self, out, in_):
        return self.op("dve", lambda e: e.reciprocal(out.ap, in_.ap), outs=[out], ins=[in_])

    def memset(self, out, val, eng="pool"):
        return self.op(eng, lambda e: e.memset(out.ap, val), outs=[out])

    def dma(self, out, in_, eng="sp"):
        return self.op(eng, lambda e: e.dma_start(out=out.ap, in_=in_.ap), outs=[out], ins=[in_], dma=True)

    def emit(self):
        nc = self.nc
        st = self.stack
        dsems = [st.enter_context(nc.semaphore(f"dq{k}")) for k in range(NDSEM)]
        dcnt = self.dcnt
        esems = {}
        for e in self.ENGS:
            cnt = 0
            for ins in self.q[e]:
                if ins.dma:
                    ins.done = (dsems[ins.dsem], ins.done)
                elif ins.needed:
                    ep = cnt // EPOCH
                    if (e, ep) not in esems:
                        esems[(e, ep)] = st.enter_context(nc.semaphore(f"s_{e}_{ep}"))
                    cnt += 1
                    ins.done = (esems[(e, ep)], cnt - ep * EPOCH)
        q = self.q
        final_d = [(dsems[k], dcnt[k]) for k in range(NDSEM) if dcnt[k] > 0]

        def run(e, eng):
            waited = {}
            for ins in q[e]:
                for d in ins.deps:
                    sem, val = d.done
                    key = id(sem)
                    if waited.get(key, 0) < val:
                        eng.wait_ge(sem, val)
                        waited[key] = val
                inst = ins.fn(eng)
                if ins.dma:
                    inst.then_inc(ins.done[0], 16)
                elif ins.needed:
                    inst.then_inc(ins.done[0], 1)
            if e == "sp":
                for sem, val in final_d:
                    if waited.get(id(sem), 0) < val:
                        eng.wait_ge(sem, val)

        with nc.Block() as block:
            @block.tensor
            def _(eng):
                run("pe", eng)

            @block.scalar
            def _(eng):
                run("act", eng)

            @block.vector
            def _(eng):
                run("dve", eng)

            @block.gpsimd
            def _(eng):
                run("pool", eng)

            @block.sync
            def _(eng):
                run("sp", eng)


from concourse.bass_utils import run_bass_kernel_spmd

DEPTH = 4
NBLK = 33
SEQ = 4096
ALPHA = 8.0 ** 0.25
NS = 16
GROUPS = [(0, 4), (4, 4), (8, 4), (12, 4), (16, 4), (20, 4), (24, 3), (27, 3), (30, 3)]
NWST = 4


def rr(r, pat, **kw):
    return R(r.ap.rearrange(pat, **kw), r.res, r.ps)


def bc(r, shape):
    return R(r.ap.broadcast_to(list(shape)), r.res, r.ps)


def build(depth=DEPTH, groups=GROUPS, do_sample=True, nblk=NBLK):
    nc = bass.Bass("TRN2", target_bir_lowering=False)
    D = {}

    def din(name, shape):
        D[name] = nc.dram_tensor(name, list(shape), F32, kind="ExternalInput").ap()

    def dout(name, shape):
        D[name] = nc.dram_tensor(name, list(shape), F32, kind="ExternalOutput").ap()

    din("xp", [SEQ, 1024]); din("meta", [16, 1024]); din("xs", [NS, 1024])
    din("ck", [4, NS, 128, 128]); din("cv", [4, NS, 128, 128]); din("sg", [4, NS, 4, 64, 128])
    din("wtok", [4, 128, 12288]); din("win_u", [4, 6, 128, 4096]); din("wglr", [4, 128, 128])
    din("wa2", [17, 4, 256]); din("sink", [1, 32])
    din("gng", [128, 4]); din("wpa_u", [4, 2, 128, 2048]); din("wpb_u", [4, 2, 128, 2048]); din("wout_u", [4, 2, 128, 4096])
    din("lnp", [128, 4, 4, 8]); din("wup_u", [4, 8, 128, 4096]); din("wdn_u", [4, 8, 128, 4096])
    din("ident", [128, 128]); din("uneg", [128, 128]); din("unegI", [128, 128]); din("masks", [128, 4, 512])
    din("cs", [128, NBLK, 16]); din("css", [NS, 16])
    dout("y", [SEQ, 1024]); dout("ys", [NS, 1024]); dout("pk", [4, 128, 128]); dout("pv", [4, 128, 128])
    dout("pst", [4, 4, 64, 128]); dout("sk", [4, NS, 128, 128]); dout("sv", [4, NS, 128, 128])
    dout("sst", [4, NS, 4, 64, 128])
    DR = {k: R(v, None) for k, v in D.items()}

    S = Sched(nc)
    import os as _os
    _target = _os.environ.get("KSTOP", "")

    class _Stop(Exception):
        pass

    def stage(name):
        if _target and name == _target:
            raise _Stop()
    with S.stack:
        S.init_psum()
        S.nrot = 6
        MUL, ADD, SUB = ALU.mult, ALU.add, ALU.subtract

        def psum():
            b = S.banks[S.psn % S.nrot]
            S.psn += 1
            return b
        bank7 = S.banks[7]
        bank6 = S.banks[6]

        ident = S.sb([128, 128], F32, "ident"); S.dma(ident, DR["ident"])
        uneg = S.sb([128, 128], F32, "uneg"); S.dma(uneg, DR["uneg"])
        unegI = S.sb([NS, NS], F32, "unegI"); S.dma(unegI, DR["unegI"][0:NS, 0:NS])
        masks = S.sb([128, 4, 512], BF16, "masks"); S.dma(masks, DR["masks"], eng="pool")
        mcur4, mprev4, m04, mprev1 = masks[:, 0, :], masks[:, 1, :], masks[:, 2, :], masks[:, 3, :]
        cs = S.sb([128, NBLK, 16], F32, "cs"); S.dma(cs, DR["cs"])
        css = S.sb([NS, 16], F32, "css"); S.dma(css, DR["css"])
        ones_bf = S.sb([128, 128], BF16, "ones_bf"); S.memset(ones_bf, 1.0)
        onesm_bf = S.sb([128, 128], BF16, "onesm_bf"); S.memset(onesm_bf, 1.0 / 1024.0)
        cst = S.sb([128, 4], F32, "cst")
        S.memset(cst[:, 0:1], 1.0); S.memset(cst[:, 1:2], 1e-5); S.memset(cst[:, 2:3], 1e-6)
        one_col, eps_ln, eps_rms = cst[:, 0:1], cst[:, 1:2], cst[:, 2:3]
        lnp = S.sb([128, 4, 4, 8], F32, "lnp"); S.dma(lnp, DR["lnp"])
        gng = S.sb([128, 4], F32, "gng"); S.dma(gng, DR["gng"])
        wa2 = S.sb([17, 4, 256], F32, "wa2"); S.dma(wa2, DR["wa2"])
        es_all = S.sb([128, 32], F32, "es_all")
        S.dma(es_all, R(D["sink"].broadcast_to([128, 32]), None))
        S.act(es_all, es_all, AF.Exp)

        xres = S.sb([128, 8, 512], F32, "xres")
        xres.res = [("xres", f) for f in range(8)]
        XR = [R(xres.ap[:, f], [("xres", f)]) for f in range(8)]
        xbf = S.sb([128, 8, 512], BF16, "xbf")
        hid_raw = S.sb([128, 16384], BF16, "hid")
        hidT = rr(hid_raw, "p (a b) -> p a b", a=32)
        gqk = R(hid_raw.ap[:, 0:4096].bitcast(F32).rearrange("p (a b) -> p a b", a=4), "hid")
        grs = rr(hid_raw[:, 4096:6144], "p (a b) -> p a b", a=4)
        attT = rr(hid_raw[:, 6144:8192], "p (a b) -> p a b", a=4)
        glaT = rr(hid_raw[:, 8192:10240], "p (a b) -> p a b", a=4)
        glrT = S.sb([17, 512], F32, "glrT"); S.memset(glrT, 1.0)
        mT = S.sb([128, 8, 512], BF16, "mT")
        wtok = S.sb([128, 8, 1536], BF16, "wtok")
        wst = [S.sb([128, 4096], BF16, f"wst{i}") for i in range(NWST)]
        wsn = [0]
        kTb = [[S.sb([128, 128], BF16, f"kT{l}_{s}") for s in range(2)] for l in range(depth)]
        vdb = [[S.sb([128, 2, 2, 64], BF16, f"vd{l}_{s}") for s in range(2)] for l in range(depth)]
        Sf = [S.sb([128, 2, 128], F32, f"Sf{l}") for l in range(depth)]
        Sbf = [S.sb([128, 2, 128], BF16, f"Sbf{l}") for l in range(depth)]
        for l in range(depth):
            S.memset(Sf[l], 0.0); S.memset(Sbf[l], 0.0)
        xin = S.sb([128, 1024], F32, "xin")
        ysb = xin
        TK = [dict(qtok=S.sb([128, 512], F32, f"qtok{p}"), ktok=S.sb([128, 128], F32, f"ktok{p}"),
                   vtok=S.sb([128, 128], F32, f"vtok{p}"), gktok=S.sb([128, 256], F32, f"gktok{p}"),
                   gvtok=S.sb([128, 512], BF16, f"gvtok{p}")) for p in range(2)]
        qtok, ktok, vtok, gktok, gvtok = (TK[0][k] for k in ("qtok", "ktok", "vtok", "gktok", "gvtok"))
        rtA = S.sb([128, 64], F32, "rtA"); rtB = S.sb([128, 64], F32, "rtB")
        qT_bf = S.sb([128, 512], BF16, "qT_bf")
        Pc = [S.sb([128, 512], BF16, f"Pc{g}") for g in range(2)]
        Pp = [S.sb([128, 512], BF16, f"Pp{g}") for g in range(2)]
        FP = [S.sb([128, 512], F32, f"F{i}") for i in range(7)]
        dtot, rec, lnv, rstd, t1 = FP[0], FP[1], FP[2], FP[3], FP[4]
        e1, sp, enb = FP[5][:, 0:256], FP[5][:, 256:512], FP[6][:, 0:256]
        sga, sgb, m1, m2 = FP[0], FP[1], FP[4], FP[5]
        msq, var, nmr, tmpA = FP[0], FP[1], FP[4], [FP[5], FP[6]]
        gvtok_f = FP[4][0:NS, :]
        ebp = S.sb([128, 2, 128], F32, "ebp"); ebn = S.sb([128, 2, 128], F32, "ebn")
        qtl = S.sb([128, 2, 128], BF16, "qtl"); ktl = S.sb([128, 2, 128], BF16, "ktl")
        ktokb = S.sb([128, 256], BF16, "ktokb")
        Am = S.sb([128, 512], BF16, "Am")
        osq = S.sb([128, 512], BF16, "osq")
        rbb = [S.sb([128, 512], BF16, f"rb{i}") for i in range(2)]
        rsb = [S.sb([128, 512], BF16, f"rsq{i}") for i in range(2)]
        m1buf = R(hid_raw.ap[:, 10240:14336].bitcast(F32).rearrange("p (a b) -> p a b", a=4), "m1buf")
        relu_t = [S.sb([128, 512], BF16, f"relu{i}") for i in range(2)]

        def wstream(src, KC, w):
            buf = wst[wsn[0] % NWST]
            wsn[0] += 1
            S.dma(buf[:, 0:KC * w], R(src, None), eng="pool")
            return R(buf.ap[:, 0:KC * w].rearrange("p (k n) -> p k n", k=KC), buf.res)

        def linear_units(units, KC, w, inT, T, evac, pre=()):
            ci = 0
            pre = list(pre)
            for ui, src in enumerate(units):
                wt = pre[ui] if ui < len(pre) else wstream(src, KC, w)
                for t0 in range(0, w, 128):
                    m = min(128, w - t0)
                    ps = psum()
                    for kc in range(KC):
                        S.mm(ps[0:m, 0:T], wt[:, kc, t0:t0 + m], inT[:, kc, 0:T], start=(kc == 0), stop=(kc == KC - 1))
                    evac(ci, ps[0:m, 0:T])
                    ci += 1

        def load_wtok(l):
            S.dma(rr(wtok, "p k n -> p (k n)"), R(D["wtok"][l], None), eng="pool")

        def tok_proj(xblk, M, cosR, sinR, tk=None):
            tk = tk or TK[0]
            qtok, ktok, vtok, gktok = tk["qtok"], tk["ktok"], tk["vtok"], tk["gktok"]
            ps_q, ps_kv, ps_gv = psum(), psum(), psum()
            for (ps, c0) in ((ps_q, 0), (ps_kv, 512), (ps_gv, 1024)):
                for kc in range(8):
                    S.mm(ps[0:M, 0:512], xblk[:, kc, :], wtok[:, kc, c0:c0 + 512], start=(kc == 0), stop=(kc == 7))
            qin = rr(ps_q[0:M, 0:512], "p (g hh d) -> p g hh d", g=2, hh=4)
            qo = rr(qtok[0:M, :], "p (hh g d) -> p g hh d", g=2, hh=4)
            S.act(qo, qin, AF.Copy)
            cosb = bc(R(cosR.ap.unsqueeze(1).unsqueeze(1), cosR.res), [M, 2, 4, 8])
            sinb = bc(R(sinR.ap.unsqueeze(1).unsqueeze(1), sinR.res), [M, 2, 4, 8])
            tA = rr(rtA[0:M, :], "p (g hh d) -> p g hh d", g=2, hh=4)
            tB = rr(rtB[0:M, :], "p (g hh d) -> p g hh d", g=2, hh=4)
            S.tt(tA, qin[:, :, :, 0:8], cosb, MUL); S.tt(tB, qin[:, :, :, 8:16], sinb, MUL)
            S.tt(qo[:, :, :, 0:8], tA, tB, SUB)
            S.tt(tA, qin[:, :, :, 8:16], cosb, MUL); S.tt(tB, qin[:, :, :, 0:8], sinb, MUL)
            S.tt(qo[:, :, :, 8:16], tA, tB, ADD)
            kin = rr(ps_kv[0:M, 0:128], "p (g d) -> p g d", g=2)
            ko = rr(ktok[0:M, :], "p (g d) -> p g d", g=2)
            S.act(ko, kin, AF.Copy)
            cos2 = bc(R(cosR.ap.unsqueeze(1), cosR.res), [M, 2, 8])
            sin2 = bc(R(sinR.ap.unsqueeze(1), sinR.res), [M, 2, 8])
            tA2 = rr(rtA[0:M, 0:16], "p (g d) -> p g d", g=2)
            tB2 = rr(rtB[0:M, 0:16], "p (g d) -> p g d", g=2)
            S.tt(tA2, kin[:, :, 0:8], cos2, MUL); S.tt(tB2, kin[:, :, 8:16], sin2, MUL)
            S.tt(ko[:, :, 0:8], tA2, tB2, SUB)
            S.tt(tA2, kin[:, :, 8:16], cos2, MUL); S.tt(tB2, kin[:, :, 0:8], sin2, MUL)
            S.tt(ko[:, :, 8:16], tA2, tB2, ADD)
            S.act(vtok[0:M, :], ps_kv[0:M, 128:256], AF.Copy)
            S.copy(gktok[0:M, :], ps_kv[0:M, 256:512])
            return ps_gv

        def gla_finish(l, o_ps, T, c0):
            n = 4 * T
            S.act(osq[:, 0:n], o_ps, AF.Square)
            ss = psum()
            S.mm(ss[:, 0:n], ones_bf, osq[:, 0:n])
            S.act(lnv[:, 0:n], ss[:, 0:n], AF.Ln, scale=1.0 / 128.0, bias=eps_rms)
            S.act(rstd[:, 0:n], lnv[:, 0:n], AF.Exp, scale=-0.5)
            S.tt(t1[:, 0:n], o_ps, rstd[:, 0:n], MUL)
            S.stt(glaT[:, :, c0:c0 + T], rr(t1[:, 0:n], "p (h t) -> p h t", h=4), gng[:, l:l + 1],
                  grs[:, :, c0:c0 + T], MUL, MUL)

        stat_pending = []

        def stat_mm(T, f):
            S.mm(bank6[:, 0:T], onesm_bf, rbb[f % 2][:, 0:T], start=(f == 0), stop=(f == 7))
            S.mm(bank7[:, 0:T], onesm_bf, rsb[f % 2][:, 0:T], start=(f == 0), stop=(f == 7))

        def stat_push(T, f):
            S.act(rbb[f % 2][:, 0:T], XR[f][:, 0:T], AF.Copy)
            S.act(rsb[f % 2][:, 0:T], XR[f][:, 0:T], AF.Square)
            stat_pending.append(f)
            if len(stat_pending) > 1:
                stat_mm(T, stat_pending.pop(0))

        def ln_fm(T, l, which, zero_pad):
            while stat_pending:
                stat_mm(T, stat_pending.pop(0))
            mean_ps, ex2_ps = bank6, bank7
            S.act(msq[:, 0:T], mean_ps[:, 0:T], AF.Square)
            S.tt(var[:, 0:T], ex2_ps[:, 0:T], msq[:, 0:T], SUB)
            S.act(lnv[:, 0:T], var[:, 0:T], AF.Ln, bias=eps_ln)
            S.act(rstd[:, 0:T], lnv[:, 0:T], AF.Exp, scale=-0.5)
            S.stt(nmr[:, 0:T], mean_ps[:, 0:T], -1.0, rstd[:, 0:T], MUL, MUL)
            for f in range(8):
                ta = tmpA[f % 2]
                S.tt(ta[:, 0:T], XR[f][:, 0:T], rstd[:, 0:T], MUL)
                S.tt(ta[:, 0:T], ta[:, 0:T], nmr[:, 0:T], ADD)
                gcol = lnp[:, l, 2 * which, f:f + 1]
                bcol = lnp[:, l, 2 * which + 1, f:f + 1]
                S.act(xbf[:, f, 0:T], ta[:, 0:T], AF.Identity, scale=gcol, bias=bcol)
                S.ts(XR[f][:, 0:T], ta[:, 0:T], gcol, MUL, bcol, ADD, eng="pool")
            if zero_pad:
                S.memset(xres[:, :, 0:112], 0.0, eng="dve")
                S.memset(xbf[:, :, 0:112], 0.0, eng="dve")

        pref = {}

        def dense_tail(l, T, zero_pad, nxt_l=None):
            for fq in range(2):
                wga = wstream(D["win_u"][l, 2 + fq], 8, 512)
                wpa = wstream(D["wpa_u"][l, fq], 4, 512)
                for ft in range(4):
                    cs_ = slice(ft * 128, ft * 128 + 128)
                    pga, ppa = psum(), psum()
                    for kc in range(8):
                        S.mm(pga[:, 0:T], wga[:, kc, cs_], xbf[:, kc, 0:T], start=(kc == 0), stop=(kc == 7))
                    for kc in range(4):
                        S.mm(ppa[:, 0:T], wpa[:, kc, cs_], attT[:, kc, 0:T], start=(kc == 0), stop=(kc == 3))
                    S.act(sga[:, 0:T], pga[:, 0:T], AF.Sigmoid)
                    S.tt(m1buf[:, ft, 0:T], ppa[:, 0:T], sga[:, 0:T], MUL)
                wgb = wstream(D["win_u"][l, 4 + fq], 8, 512)
                wpb = wstream(D["wpb_u"][l, fq], 4, 512)
                for ft in range(4):
                    f = fq * 4 + ft
                    cs_ = slice(ft * 128, ft * 128 + 128)
                    pgb, ppb = psum(), psum()
                    for kc in range(8):
                        S.mm(pgb[:, 0:T], wgb[:, kc, cs_], xbf[:, kc, 0:T], start=(kc == 0), stop=(kc == 7))
                    for kc in range(4):
                        S.mm(ppb[:, 0:T], wpb[:, kc, cs_], glaT[:, kc, 0:T], start=(kc == 0), stop=(kc == 3))
                    S.act(sgb[:, 0:T], pgb[:, 0:T], AF.Sigmoid)
                    S.tt(m2[:, 0:T], ppb[:, 0:T], sgb[:, 0:T], MUL)
                    S.tt(mT[:, f, 0:T], m1buf[:, ft, 0:T], m2[:, 0:T], ADD)

            def ev_res(ci, ps):
                S.stt(XR[ci][:, 0:T], XR[ci][:, 0:T], ALPHA, ps, MUL, ADD)
                stat_push(T, ci)
            linear_units([D["wout_u"][l, u] for u in range(2)], 8, 512, mT, T, ev_res)
            pre_up = [wstream(D["wup_u"][l, u], 8, 512) for u in range(2)]
            ln_fm(T, l, 0, False)

            def ev_up(ci, ps):
                rt = relu_t[ci % 2]
                S.act(rt[:, 0:T], ps, AF.Relu)
                S.tt(hidT[:, ci, 0:T], rt[:, 0:T], rt[:, 0:T], MUL)
            linear_units([D["wup_u"][l, u] for u in range(8)], 8, 512, xbf, T, ev_up, pre=pre_up)
            for cg in range(2):
                acc = [psum() for _ in range(4)]
                for kb in range(4):
                    wt = wstream(D["wdn_u"][l, cg * 4 + kb], 8, 512)
                    for ft in range(4):
                        for kc in range(8):
                            S.mm(acc[ft][:, 0:T], wt[:, kc, ft * 128:(ft + 1) * 128], hidT[:, kb * 8 + kc, 0:T],
                                 start=(kb == 0 and kc == 0), stop=(kb == 3 and kc == 7))
                for ft in range(4):
                    ev_res(cg * 4 + ft, acc[ft][:, 0:T])
            if nxt_l is not None:
                load_wtok(nxt_l)
                pref["gqk"] = (nxt_l, wstream(D["win_u"][nxt_l, 0], 8, 512))
            ln_fm(T, l, 1, zero_pad)

        def phase_a(l, T):
            def ev_gqk(ci, ps):
                S.copy(gqk[:, ci, 0:T], ps)
            pre = []
            if pref.get("gqk") and pref["gqk"][0] == l:
                pre = [pref.pop("gqk")[1]]
            linear_units([D["win_u"][l, 0]], 8, 512, xbf, T, ev_gqk, pre=pre)

            def ev_glr(ci, ps):
                S.copy(glrT[0:16, 0:T], ps)
            linear_units([D["wglr"][l]], 8, 16, xbf, T, ev_glr)

            def ev_gr(ci, ps):
                S.act(grs[:, ci, 0:T], ps, AF.Silu)
            linear_units([D["win_u"][l, 1]], 8, 512, xbf, T, ev_gr)

        def load_x(b0, nb):
            for j in range(nb):
                n = b0 + j
                if n == 0:
                    S.memset(xin, 0.0, eng="dve")
                    S.dma(xin[112:128, :], DR["meta"])
                else:
                    S.dma(xin, DR["xp"][(n - 1) * 128:n * 128, :])
                for half in range(2):
                    ps = psum()
                    for ff in range(4):
                        f = half * 4 + ff
                        S.tr(ps[:, ff * 128:(ff + 1) * 128], xin[:, f * 128:(f + 1) * 128], ident)
                    pv = rr(ps[:, 0:512], "p (f t) -> p f t", f=4)
                    S.copy(xres[:, half * 4:half * 4 + 4, j * 128:(j + 1) * 128], pv)
                    S.act(xbf[:, half * 4:half * 4 + 4, j * 128:(j + 1) * 128], pv, AF.Copy)

        def store_y(b0, nb):
            for j in range(nb):
                n = b0 + j
                if n == 0:
                    continue
                for half in range(2):
                    ps = psum()
                    for ff in range(4):
                        f = half * 4 + ff
                        S.tr(ps[:, ff * 128:(ff + 1) * 128], xres[:, f, j * 128:(j + 1) * 128], ident)
                    if half == 0:
                        S.copy(ysb[:, 0:512], ps[:, 0:512])
                    else:
                        S.act(ysb[:, 512:1024], ps[:, 0:512], AF.Copy)
                S.dma(DR["y"][(n - 1) * 128:n * 128, :], ysb)

        hooks = []

        def hook():
            if hooks:
                hooks.pop(0)()

        PcB = [R(hid_raw.ap[:, 14336 + g * 512:14336 + (g + 1) * 512], ("PcB", g)) for g in range(2)]
        PpB = [R(hid_raw.ap[:, 15360 + g * 512:15360 + (g + 1) * 512], ("PpB", g)) for g in range(2)]
        BT = [dict(qT_bf=qT_bf, Pc=Pc, Pp=Pp, ebp=ebp, ebn=ebn, qtl=qtl, ktl=ktl, ktokb=ktokb, Am=Am),
              dict(qT_bf=S.sb([128, 512], BF16, "qT_bfB"), Pc=PcB, Pp=PpB,
                   ebp=S.sb([128, 2, 128], F32, "ebpB"), ebn=S.sb([128, 2, 128], F32, "ebnB"),
                   qtl=S.sb([128, 2, 128], BF16, "qtlB"), ktl=S.sb([128, 2, 128], BF16, "ktlB"),
                   ktokb=S.sb([128, 256], BF16, "ktokbB"), Am=S.sb([128, 512], BF16, "AmB"))]

        def tok_stage(l, n, j):
            tk = TK[j % 2]
            csl = slice(j * 128, j * 128 + 128)
            ps_gv = tok_proj(xbf[:, :, csl], 128, cs[:, n, 0:8], cs[:, n, 8:16], tk)
            S.act(tk["gvtok"], ps_gv[:, 0:512], AF.Copy)
            if n == nblk - 1:
                S.dma(DR["pk"][l], tk["ktok"])
                S.dma(DR["pv"][l], tk["vtok"])

        def st_T(l, n, j):
            tk, bt = TK[j % 2], BT[j % 2]
            qT_ps = psum()
            for hh in range(4):
                S.tr(qT_ps[:, hh * 128:(hh + 1) * 128], tk["qtok"][:, hh * 128:(hh + 1) * 128], ident)
            S.copy(bt["qT_bf"], qT_ps[:, 0:512])
            kT_ps = psum()
            S.tr(kT_ps[:, 0:128], tk["ktok"], ident)
            S.act(kTb[l][n % 2], kT_ps[:, 0:128], AF.Copy)

        def st_vd(l, n, j):
            tk = TK[j % 2]
            vd = vdb[l][n % 2]
            v3 = rr(tk["vtok"], "p (g d) -> p g d", g=2)
            S.copy(vd[:, :, 0, :], v3); S.act(vd[:, :, 1, :], v3, AF.Copy)

        def st_S(l, n, j):
            bt = BT[j % 2]
            kTc, kTp = kTb[l][n % 2], kTb[l][(n - 1) % 2]
            maskc = m04 if n == 0 else mcur4
            maskp = mprev1 if n == 1 else mprev4
            for g in range(2):
                rs_ = slice(g * 64, g * 64 + 64)
                Sc = psum()
                S.mm(Sc[:, 0:512], kTc[rs_, :], bt["qT_bf"][rs_, :])
                S.act(bt["Pc"][g], Sc[:, 0:512], AF.Exp, scale=0.125)
                S.tt(bt["Pc"][g], bt["Pc"][g], maskc, MUL)
                if n > 0:
                    Sp_ = psum()
                    S.mm(Sp_[:, 0:512], kTp[rs_, :], bt["qT_bf"][rs_, :])
                    S.act(bt["Pp"][g], Sp_[:, 0:512], AF.Exp, scale=0.125)
                    S.tt(bt["Pp"][g], bt["Pp"][g], maskp, MUL)

        def st_PV(l, n, j, g):
            bt = BT[j % 2]
            csl = slice(j * 128, j * 128 + 128)
            vd, vdp = vdb[l][n % 2], vdb[l][(n - 1) % 2]
            Pc_, Pp_ = bt["Pc"], bt["Pp"]
            O, Dn = psum(), psum()
            vc = rr(vd[:, g], "p a d -> p (a d)")
            if n > 0:
                vp = rr(vdp[:, g], "p a d -> p (a d)")
                S.mm(O[:, 0:512], vp, Pp_[g], start=True, stop=False)
                S.mm(O[:, 0:512], vc, Pc_[g], start=False, stop=True)
                S.mm(Dn[:, 0:512], ones_bf, Pp_[g], start=True, stop=False)
                S.mm(Dn[:, 0:512], ones_bf, Pc_[g], start=False, stop=True)
            else:
                S.mm(O[:, 0:512], vc, Pc_[g])
                S.mm(Dn[:, 0:512], ones_bf, Pc_[g])
            for hh in range(4):
                hc = slice(hh * 128, (hh + 1) * 128)
                S.act(dtot[:, hc], Dn[:, hc], AF.Ln, bias=es_all[:, l * 8 + g * 4 + hh:l * 8 + g * 4 + hh + 1])
            S.act(rec, dtot, AF.Exp, scale=-1.0)
            for hh in range(4):
                h = 4 * g + hh
                r0 = (h % 2) * 64
                S.tt(attT[r0:r0 + 64, h // 2, csl], O[r0:r0 + 64, hh * 128:(hh + 1) * 128],
                     rec[r0:r0 + 64, hh * 128:(hh + 1) * 128], MUL)

        def st_GA(l, n, j):
            tk, bt = TK[j % 2], BT[j % 2]
            csl = slice(j * 128, j * 128 + 128)
            z = psum()
            S.mm(z[:, 0:256], glrT[0:17, csl], wa2[0:17, l, :])
            S.act(e1, z[:, 0:256], AF.Exp, scale=-1.0)
            S.act(sp, e1, AF.Ln, bias=one_col)
            bt_ps = psum()
            S.mm(bt_ps[:, 0:256], uneg, sp)
            bT = psum()
            for i in range(2):
                S.mm(bT[:, i * 128:(i + 1) * 128], sp[:, i * 128:(i + 1) * 128], uneg)
            bT3 = rr(bT[:, 0:256], "p (i t) -> p i t", i=2)
            S.act(bt["ebp"], bT3, AF.Exp)
            S.act(bt["ebn"], bT3, AF.Exp, scale=-1.0)
            S.act(enb, bt_ps[:, 0:256], AF.Exp, scale=-1.0)
            S.stt(bt["qtl"], gqk[:, 0:2, csl], 0.125, bt["ebp"], MUL, MUL)
            S.tt(bt["ktl"], gqk[:, 2:4, csl], bt["ebn"], MUL)
            S.tt(bt["ktokb"], tk["gktok"], enb, MUL)

        def st_GB1(l, n, j):
            bt = BT[j % 2]
            qtl_, ktl_, Am_ = bt["qtl"], bt["ktl"], bt["Am"]
            A2 = [psum(), psum()]
            for h in range(4):
                i, r0 = h // 2, (h % 2) * 64
                S.mm(A2[h % 2][:, h * 128:(h + 1) * 128], ktl_[r0:r0 + 64, i, :], qtl_[r0:r0 + 64, i, :])
            Am4 = rr(Am_, "p (a b c) -> p a b c", a=2, b=2)
            mk4 = rr(mcur4, "p (a b c) -> p a b c", a=2, b=2)
            for par in range(2):
                S.tt(Am4[:, :, par, :], rr(A2[par][:, 0:512], "p (a b c) -> p a b c", a=2, b=2)[:, :, par, :],
                     mk4[:, :, par, :], MUL)

        def st_GB2(l, n, j):
            tk, bt = TK[j % 2], BT[j % 2]
            c0 = j * 128
            gvtok_, qtl_, Am_, ktokb_, ebp_ = tk["gvtok"], bt["qtl"], bt["Am"], bt["ktokb"], bt["ebp"]
            o = psum()
            for h in range(4):
                i, r0 = h // 2, (h % 2) * 64
                hs = slice(h * 128, (h + 1) * 128)
                S.mm(o[:, hs], gvtok_[:, hs], Am_[:, hs], start=True, stop=False)
                S.mm(o[:, hs], Sbf[l][r0:r0 + 64, i, :], qtl_[r0:r0 + 64, i, :], start=False, stop=True)
            for i in range(2):
                U = psum()
                S.mm(U[:, 0:256], ktokb_[:, i * 128:(i + 1) * 128], gvtok_[:, i * 256:(i + 1) * 256])
                for hh in range(2):
                    r0 = hh * 64
                    S.tt(Sf[l][r0:r0 + 64, i, :], U[r0:r0 + 64, hh * 128:(hh + 1) * 128], Sf[l][r0:r0 + 64, i, :], ADD)
                S.ts(Sf[l][:, i, :], Sf[l][:, i, :], ebp_[:, i, 127:128], MUL)
                S.act(Sbf[l][:, i, :], Sf[l][:, i, :], AF.Copy)
            gla_finish(l, o[:, 0:512], 128, c0)
            if n == nblk - 1:
                dst = D["pst"][l].rearrange("(i hh) dk dv -> (hh dk) i dv", hh=2)
                S.dma(R(dst, None), Sf[l])

        def block_phase(l, b0, nb):
            tok_stage(l, b0, 0)
            if nb > 1:
                tok_stage(l, b0 + 1, 1)
            st_T(l, b0, 0); st_vd(l, b0, 0); st_S(l, b0, 0); hook(); st_GA(l, b0, 0); st_GB1(l, b0, 0); hook()
            for j in range(nb):
                n = b0 + j
                nxt = j + 1 < nb
                st_PV(l, n, j, 0)
                if nxt:
                    st_T(l, n + 1, j + 1)
                hook()
                st_PV(l, n, j, 1)
                if nxt:
                    st_vd(l, n + 1, j + 1)
                    st_S(l, n + 1, j + 1)
                hook()
                st_GB2(l, n, j)
                hook()
                if nxt:
                    st_GA(l, n + 1, j + 1)
                    hook()
                    st_GB1(l, n + 1, j + 1)
                if j + 2 < nb:
                    tok_stage(l, n + 2, j + 2)
                hook()

        if do_sample:
            Ss = S.sb([128, 4, 2, 128], F32, "Ss")
            kwin = R(hid_raw.ap[:, 10240:14336].bitcast(F32).rearrange("p (a b c) -> p a b c", a=2, b=8), "m1buf")
            TKS = dict(qtok=FP[2][0:NS, :], ktok=S.sb([NS, 128], F32, "ktok_s"), vtok=S.sb([NS, 128], F32, "vtok_s"),
                       gktok=S.sb([NS, 256], F32, "gktok_s"))
            gvs = S.sb([NS, 512], F32, "gvs")
            KwT = [S.sb([128, 128], BF16, f"KwT{i}") for i in range(2)]
            Vwd = [S.sb([128, 2, 2, 64], BF16, f"Vwd{i}") for i in range(2)]
            qTs = S.sb([128, 4, NS], BF16, "qTs")
            Ps = S.sb([128, 128], BF16, "Ps")
            ebs = S.sb([128, 2, NS], F32, "ebs")
            qsT = S.sb([128, 2, NS], F32, "qsT")
            kmask = [FP[6][0:NS, 0:256], FP[6][0:NS, 256:512]]

        def sample_parts(l, sc0=0):
            M = NS
            tk = TKS
            qtok, ktok, vtok, gktok = tk["qtok"], tk["ktok"], tk["vtok"], tk["gktok"]
            gvtok_f = gvs

            skR = R(D["sk"][l], ("sk", l))
            svR = R(D["sv"][l], ("sv", l))

            def stA(half):
                h8 = half * 8
                for a_, (cin, outR) in enumerate(((D["ck"], skR), (D["cv"], svR))):
                    S.dma(kwin[0:127, a_], R(cin[l, h8:h8 + 8, 1:128, :].rearrange("b r c -> r b c"), None))
                    S.dma(kwin[127:128, a_], R(outR.ap[h8:h8 + 8, 127:128, :].rearrange("b r c -> r b c"), outR.res))
                    S.dma(R(outR.ap[h8:h8 + 8, 0:127, :].rearrange("b r c -> r b c"), None), kwin[0:127, a_])

            def stB(bi):
                Kw, Vw, KT, Vd = kwin[:, 0, bi % 8, :], kwin[:, 1, bi % 8, :], KwT[bi % 2], Vwd[bi % 2]
                kt_ps = psum()
                S.tr(kt_ps[:, 0:128], Kw, ident)
                S.act(KT, kt_ps[:, 0:128], AF.Copy)
                v3 = rr(Vw, "p (g d) -> p g d", g=2)
                S.copy(Vd[:, :, 0, :], v3); S.act(Vd[:, :, 1, :], v3, AF.Copy)
                for g in range(2):
                    rs_ = slice(g * 64, g * 64 + 64)
                    sc = psum()
                    S.mm(sc[:, 0:4], KT[rs_, :], qTs[rs_, :, bi])
                    S.act(Ps[:, bi * 8 + g * 4:bi * 8 + g * 4 + 4], sc[:, 0:4], AF.Exp, scale=0.125)
                for g in range(2):
                    c = bi * 8 + g * 4
                    S.mm(bank7[:, c:c + 4], rr(Vd[:, g], "p a d -> p (a d)"), Ps[:, c:c + 4])

            def p_pre():
                ps_gv = tok_proj(xbf[:, :, sc0:sc0 + M], M, css[:, 0:8], css[:, 8:16], tk)
                S.act(gvtok_f, ps_gv[0:M, 0:512], AF.Copy)
                qT_ps = psum()
                for hh in range(4):
                    S.tr(qT_ps[:, hh * M:(hh + 1) * M], qtok[0:M, hh * 128:(hh + 1) * 128], ident[0:M, 0:M])
                S.copy(qTs, rr(qT_ps[:, 0:4 * M], "p (h b) -> p h b", h=4))
                S.dma(skR[:, 127, :], ktok[0:M, :])
                S.dma(svR[:, 127, :], vtok[0:M, :])
                stA(0)

            def p_bi(b0_):
                def f():
                    if b0_ == 8:
                        stA(1)
                    for bi in (b0_, b0_ + 1):
                        stB(bi)
                return f

            def p_post():
                Dn = psum()
                S.mm(Dn[:, 0:128], ones_bf, Ps)
                essb = bc(R(es_all.ap[:, l * 8:(l + 1) * 8].unsqueeze(1), es_all.res), [128, NS, 8])
                S.tt(rr(dtot[:, 0:128], "p (b h) -> p b h", h=8), rr(Dn[:, 0:128], "p (b h) -> p b h", h=8), essb, ADD)
                S.recip(rec[:, 0:128], dtot[:, 0:128])
                O3 = rr(bank7[:, 0:128], "p (b h) -> p h b", h=8)
                r3 = rr(rec[:, 0:128], "p (b h) -> p h b", h=8)
                for h in range(8):
                    r0 = (h % 2) * 64
                    S.tt(attT[r0:r0 + 64, h // 2, sc0:sc0 + M], O3[r0:r0 + 64, h, :], r3[r0:r0 + 64, h, :], MUL)
                z = psum()
                S.mm(z[0:M, 0:256], glrT[0:17, sc0:sc0 + M], wa2[0:17, l, :])
                S.act(e1[0:M, :], z[0:M, 0:256], AF.Exp, scale=-1.0)
                S.act(sp[0:M, :], e1[0:M, :], AF.Ln, bias=one_col[0:M, :])
                bT = psum()
                for i in range(2):
                    S.mm(bT[:, i * M:(i + 1) * M], sp[0:M, i * 128:(i + 1) * 128], unegI[0:M, 0:M])
                S.act(ebs, rr(bT[:, 0:2 * M], "p (i b) -> p i b", i=2), AF.Exp)
                S.ts(qsT, gqk[:, 0:2, sc0:sc0 + M], 0.125, MUL)

            def p_gla(b4):
                def f():
                    src = D["sg"][l, b4:b4 + 4].rearrange("b (i hh) dk dv -> (hh dk) b i dv", hh=2)
                    S.dma(Ss, R(src, None))
                    for bb in range(4):
                        b = b4 + bb
                        km = kmask[b % 2]
                        S.ts(km, gktok[0:M, :], ident[0:M, b:b + 1], MUL)
                        for i in range(2):
                            U = psum()
                            S.mm(U[:, 0:256], km[:, i * 128:(i + 1) * 128], gvtok_f[:, i * 256:(i + 1) * 256])
                            for hh in range(2):
                                r0 = hh * 64
                                S.stt(Ss[r0:r0 + 64, bb, i, :], Ss[r0:r0 + 64, bb, i, :], ebs[r0:r0 + 64, i, b:b + 1],
                                      U[r0:r0 + 64, hh * 128:(hh + 1) * 128], MUL, ADD)
                            for hh in range(2):
                                r0 = hh * 64
                                h = 2 * i + hh
                                ob = bank7 if hh == 0 else bank6
                                S.mm(ob[:, 256 + h * M + b:256 + h * M + b + 1], Ss[r0:r0 + 64, bb, i, :],
                                     qsT[r0:r0 + 64, i, b:b + 1])
                    dst = D["sst"][l, b4:b4 + 4].rearrange("b (i hh) dk dv -> (hh dk) b i dv", hh=2)
                    S.dma(R(dst, None), Ss)
                return f

            def p_fin():
                o_sb = FP[6][:, 0:4 * M]
                o4 = rr(o_sb, "p (i hh b) -> p i hh b", i=2, hh=2)
                for hh, ob in ((0, bank7), (1, bank6)):
                    S.copy(o4[:, :, hh, :], rr(ob[:, 256:256 + 4 * M], "p (i hh b) -> p i hh b", i=2, hh=2)[:, :, hh, :])
                gla_finish(l, o_sb, M, sc0)

            return [p_pre] + [p_bi(b0_) for b0_ in range(0, NS, 2)] + [p_post] + [p_gla(b4) for b4 in range(0, NS, 4)] + [p_fin]

        def load_sample(sc0):
            M = NS
            S.dma(xin[0:M, :], DR["xs"])
            for half in range(2):
                ps = psum()
                for ff in range(4):
                    f = half * 4 + ff
                    S.tr(ps[:, ff * M:(ff + 1) * M], xin[0:M, f * 128:(f + 1) * 128], ident[0:M, 0:M])
                pv = rr(ps[:, 0:4 * M], "p (f t) -> p f t", f=4)
                S.copy(xres[:, half * 4:half * 4 + 4, sc0:sc0 + M], pv)
                S.act(xbf[:, half * 4:half * 4 + 4, sc0:sc0 + M], pv, AF.Copy)

        def store_sample(sc0):
            M = NS
            for half in range(2):
                ps = psum()
                for ff in range(4):
                    f = half * 4 + ff
                    S.tr(ps[0:M, ff * 128:(ff + 1) * 128], xres[:, f, sc0:sc0 + M], ident)
                S.copy(ysb[0:M, half * 512:(half + 1) * 512], ps[0:M, 0:512])
            S.dma(DR["ys"], ysb[0:M, :])

        def main_program():
            stage("consts")
            for gi, (b0, nb) in enumerate(groups):
                T = nb * 128
                merged = do_sample and gi == len(groups) - 1 and nb <= 3
                load_x(b0, nb)
                if b0 == 0:
                    S.memset(xres[:, :, 0:112], 0.0, eng="dve")
                    S.memset(xbf[:, :, 0:112], 0.0, eng="dve")
                if merged:
                    load_sample(T)
                    T = T + NS
                stage("loadx")
                for l in range(depth):
                    if not (pref.get("gqk") and pref["gqk"][0] == l):
                        load_wtok(l)
                    stage("wtok")
                    phase_a(l, T)
                    stage("phase_a")
                    if merged:
                        hooks.extend(sample_parts(l, nb * 128))
                    block_phase(l, b0, nb)
                    while hooks:
                        hooks.pop(0)()
                    stage("blocks")
                    last_gl = (gi == len(groups) - 1 and l == depth - 1)
                    dense_tail(l, T, b0 == 0, None if last_gl else (l + 1) % depth)
                    stage("dense")
                store_y(b0, nb)
                if merged:
                    store_sample(nb * 128)
                stage("store_y")
            if do_sample and not merged:
                M = NS
                load_sample(0)
                for l in range(depth):
                    load_wtok(l)
                    phase_a(l, M)
                    stage("s_phase_a")
                    for p in sample_parts(l, 0):
                        p()
                    stage("s_mixer")
                    dense_tail(l, M, False)
                store_sample(0)

        try:
            main_program()
        except _Stop:
            pass
        S.emit()
    return nc


def _consts():
    j = np.arange(128)[:, None]
    i = np.arange(128)[None, :]
    mcur = (j <= i).astype(np.float32)
    mprev = (j > i).astype(np.float32)
    m0 = ((j <= i) & (j >= 112)).astype(np.float32)
    mprev1 = ((j > i) & (j >= 112)).astype(np.float32)
    masks = np.stack([np.tile(m, (1, 4)) for m in (mcur, mprev, m0, mprev1)], axis=1)
    uneg = (-(1.0 / 16.0) * (j <= i)).astype(np.float32)
    unegI = (-(1.0 / 16.0) * np.eye(128)).astype(np.float32)
    inv = (500000.0 ** (-np.arange(8, dtype=np.float32) * 2.0 / 16.0)).astype(np.float32)
    slot = np.arange(NBLK)[None, :] * 128 + np.arange(128)[:, None]
    pos = np.maximum(slot - 112, 0).astype(np.float32)
    ang = pos[:, :, None] * inv[None, None, :]
    cs = np.concatenate([np.cos(ang), np.sin(ang)], -1).astype(np.float32)
    angs = (np.float32(8192.0) * inv)[None, :].repeat(NS, 0)
    css = np.concatenate([np.cos(angs), np.sin(angs)], -1).astype(np.float32)
    return dict(ident=np.eye(128, dtype=np.float32), uneg=uneg, unegI=unegI,
                masks=np.ascontiguousarray(masks), cs=np.ascontiguousarray(cs), css=np.ascontiguousarray(css))


_NC_CACHE = {}


def _prep_shared(meta_tokens, w_in, w_a2, b_a, attn_sink, gla_norm_g, w_proj_a, w_proj_b, w_out, ln1_g, ln1_b,
                 w_up, w_down, ln2_g, ln2_b):
    f = lambda a: np.ascontiguousarray(np.asarray(a, dtype=np.float32))
    w_in = f(w_in)
    lnp = np.stack([f(ln1_g), f(ln1_b), f(ln2_g), f(ln2_b)], axis=1)
    lnp = lnp.reshape(4, 4, 8, 128).transpose(3, 0, 1, 2)
    ucols = np.concatenate([np.arange(c, c + 512) for c in (768, 1808, 2320, 2832, 3344, 3856)])
    win_u = w_in[:, :, ucols].reshape(4, 8, 128, 6, 512).transpose(0, 3, 2, 1, 4).reshape(4, 6, 128, 4096)
    tcols = np.concatenate([np.arange(0, 768), np.arange(1024, 1792)])
    wtok = w_in[:, :, tcols].reshape(4, 8, 128, 1536).transpose(0, 2, 1, 3).reshape(4, 128, 12288)
    wglr = w_in[:, :, 1792:1808].reshape(4, 8, 128, 16).transpose(0, 2, 1, 3).reshape(4, 128, 128)
    wpa_u = f(w_proj_a).reshape(4, 4, 128, 2, 512).transpose(0, 3, 2, 1, 4).reshape(4, 2, 128, 2048)
    wpb_u = f(w_proj_b).reshape(4, 4, 128, 2, 512).transpose(0, 3, 2, 1, 4).reshape(4, 2, 128, 2048)
    wout_u = f(w_out).reshape(4, 8, 128, 2, 512).transpose(0, 3, 2, 1, 4).reshape(4, 2, 128, 4096)
    wup_u = f(w_up).reshape(4, 8, 128, 8, 512).transpose(0, 3, 2, 1, 4).reshape(4, 8, 128, 4096)
    wdn_u = f(w_down).reshape(4, 4, 8, 128, 2, 512).transpose(0, 4, 1, 3, 2, 5).reshape(4, 8, 128, 4096)
    cst = _consts()
    return dict(
        meta=f(meta_tokens), wtok=f(wtok), win_u=f(win_u), wglr=f(wglr),
        wa2=f(np.concatenate([np.transpose(f(w_a2), (1, 0, 2)), f(b_a)[None]], 0)),
        sink=f(attn_sink).reshape(1, 32), gng=f(np.transpose(f(gla_norm_g), (1, 0))), wpa_u=f(wpa_u), wpb_u=f(wpb_u),
        wout_u=f(wout_u), lnp=f(lnp), wup_u=f(wup_u), wdn_u=f(wdn_u), **cst)


def kernel(x_prompt, x_sample, cache_k_win, cache_v_win, state_gla, meta_tokens, w_in, w_a2, b_a, attn_sink,
           gla_norm_g, w_proj_a, w_proj_b, w_out, ln1_g, ln1_b, w_up, w_down, ln2_g, ln2_b):
    f = lambda a: np.ascontiguousarray(np.asarray(a, dtype=np.float32))
    x_prompt, x_sample, cache_k_win, cache_v_win, state_gla = map(f, (x_prompt, x_sample, cache_k_win, cache_v_win, state_gla))
    if "nc" not in _NC_CACHE:
        _NC_CACHE["nc"] = build()
    nc = _NC_CACHE["nc"]
    shared = _prep_shared(meta_tokens, w_in, w_a2, b_a, attn_sink, gla_norm_g, w_proj_a, w_proj_b, w_out,
                          ln1_g, ln1_b, w_up, w_down, ln2_g, ln2_b)
    in_maps = []
    for c in range(8):
        sl = slice(c * NS, (c + 1) * NS)
        m = dict(shared)
        m["xp"] = x_prompt[c % 4]
        m["xs"] = f(x_sample[sl, 0, :])
        m["ck"] = f(cache_k_win[:, sl].reshape(4, NS, 128, 128))
        m["cv"] = f(cache_v_win[:, sl].reshape(4, NS, 128, 128))
        m["sg"] = f(state_gla[:, sl])
        in_maps.append(m)
    res = run_bass_kernel_spmd(nc, in_maps, core_ids=list(range(8)))
    rs = res.results
    y_prompt = np.stack([rs[c]["y"] for c in range(4)], 0)
    y_sample = np.concatenate([rs[c]["ys"] for c in range(8)], 0).reshape(128, 1, 1024)
    pk = np.stack([rs[c]["pk"] for c in range(4)], 1).reshape(4, 4, 128, 2, 64)
    pv = np.stack([rs[c]["pv"] for c in range(4)], 1).reshape(4, 4, 128, 2, 64)
    pst = np.stack([rs[c]["pst"] for c in range(4)], 1)
    sk = np.concatenate([rs[c]["sk"] for c in range(8)], 1).reshape(4, 128, 128, 2, 64)
    sv = np.concatenate([rs[c]["sv"] for c in range(8)], 1).reshape(4, 128, 128, 2, 64)
    sst = np.concatenate([rs[c]["sst"] for c in range(8)], 1)
    return tuple(np.ascontiguousarray(a, dtype=np.float32) for a in (y_prompt, y_sample, pk, pv, pst, sk, sv, sst))
```

```python
import numpy as np
import contextlib
import concourse.bass as bass
import concourse.mybir as mybir

F32 = mybir.dt.float32
BF16 = mybir.dt.bfloat16
ALU = mybir.AluOpType
AF = mybir.ActivationFunctionType

NDSEM = 24
EPOCH = 20000


class R:
    __slots__ = ("ap", "res", "ps")

    def __init__(self, ap, res, ps=False):
        self.ap = ap
        self.res = res
        self.ps = ps

    def __getitem__(self, k):
        return R(self.ap[k], self.res, self.ps)


class Ins:
    __slots__ = ("eng", "fn", "deps", "needed", "done", "dma", "dsem")

    def __init__(self, eng, fn, dma):
        self.eng = eng
        self.fn = fn
        self.deps = []
        self.needed = False
        self.done = None
        self.dma = dma
        self.dsem = None


class Sched:
    ENGS = ["pe", "act", "dve", "pool", "sp"]

    def __init__(self, nc):
        self.nc = nc
        self.q = {e: [] for e in self.ENGS}
        self.last_w = {}
        self.readers = {}
        self.ndma = 0
        self.ndma_pool = 0
        self.dsem_last = [None] * NDSEM
        self.dcnt = [0] * NDSEM
        self.stack = contextlib.ExitStack()
        self.psn = 0
        self.nt = 0

    def sb(self, shape, dtype, name=None):
        self.nt += 1
        name = name or f"t{self.nt}"
        t = self.stack.enter_context(self.nc.sbuf_tensor("sb_" + name, list(shape), dtype))
        return R(t[:], name)

    def init_psum(self):
        self.banks = []
        for i in range(8):
            t = self.stack.enter_context(self.nc.psum_tensor(f"psb{i}", [128, 512], F32))
            self.banks.append(R(t[:], ("ps", i), True))

    def psum(self):
        b = self.banks[self.psn % 8]
        self.psn += 1
        return b

    def op(self, eng, fn, outs=(), ins=(), dma=False):
        i = Ins(eng, fn, dma)
        reads, writes = [], []
        for r in ins:
            if r is None or r.res is None:
                continue
            (writes if r.ps else reads).extend(r.res if isinstance(r.res, list) else [r.res])
        for r in outs:
            if r is None or r.res is None:
                continue
            writes.extend(r.res if isinstance(r.res, list) else [r.res])
        deps = {}

        def add(d, raw):
            if d is None:
                return
            if d.eng == eng and not d.dma and not dma:
                if eng == "pe":
                    return
            deps[id(d)] = d

        for r in reads:
            add(self.last_w.get(r), True)
        for w in writes:
            add(self.last_w.get(w), True)
            for rd in self.readers.get(w, {}).values():
                add(rd, False)
        if dma:
            half = NDSEM // 2
            if eng == "pool":
                k = half + self.ndma_pool % half
                self.ndma_pool += 1
            else:
                k = self.ndma % half
                self.ndma += 1
            i.dsem = k
            self.dcnt[k] += 16
            i.done = self.dcnt[k]
            if self.dsem_last[k] is not None:
                deps[id(self.dsem_last[k])] = self.dsem_last[k]
            self.dsem_last[k] = i
        i.deps = list(deps.values())
        for d in i.deps:
            d.needed = True
        for r in reads:
            self.readers.setdefault(r, {})[eng if not dma else ("dma", id(i))] = i
        for w in writes:
            self.last_w[w] = i
            self.readers[w] = {}
        self.q[eng].append(i)
        return i

    def mm(self, out, lhsT, rhs, start=True, stop=True):
        return self.op("pe", lambda e: e.matmul(out.ap, lhsT.ap, rhs.ap, start=start, stop=stop),
                       outs=[out], ins=[lhsT, rhs])

    def tr(self, out, in_, ident):
        return self.op("pe", lambda e: e.transpose(out.ap, in_.ap, ident.ap), outs=[out], ins=[in_, ident])

    def act(self, out, in_, func, bias=None, scale=None, eng="act"):
        kw = {}
        ins = [in_]
        if bias is not None:
            if isinstance(bias, R):
                kw["bias"] = bias.ap
                ins.append(bias)
            else:
                kw["bias"] = bias
        if scale is not None:
            if isinstance(scale, R):
                kw["scale"] = scale.ap
                ins.append(scale)
            else:
                kw["scale"] = scale
        return self.op(eng, lambda e: e.activation(out.ap, in_.ap, func, **kw), outs=[out], ins=ins)

    def tt(self, out, in0, in1, op, eng="dve"):
        return self.op(eng, lambda e: e.tensor_tensor(out.ap, in0.ap, in1.ap, op), outs=[out], ins=[in0, in1])

    def ts(self, out, in0, s1, op0, s2=None, op1=None, eng="dve"):
        ins = [in0]
        a1 = s1
        a2 = s2
        if isinstance(s1, R):
            ins.append(s1)
            a1 = s1.ap
        if isinstance(s2, R):
            ins.append(s2)
            a2 = s2.ap
        if op1 is None:
            return self.op(eng, lambda e: e.tensor_scalar(out.ap, in0.ap, a1, None, op0), outs=[out], ins=ins)
        return self.op(eng, lambda e: e.tensor_scalar(out.ap, in0.ap, a1, a2, op0, op1), outs=[out], ins=ins)

    def stt(self, out, in0, scalar, in1, op0, op1):
        ins = [in0, in1]
        a = scalar
        if isinstance(scalar, R):
            ins.append(scalar)
            a = scalar.ap
        return self.op("dve", lambda e: e.scalar_tensor_tensor(out.ap, in0.ap, a, in1.ap, op0, op1),
                       outs=[out], ins=ins)

    def copy(self, out, in_, eng="dve"):
        return self.op(eng, lambda e: e.tensor_copy(out.ap, in_.ap), outs=[out], ins=[in_])

    def recip(self, out, in_):
        return self.op("dve", lambda e: e.reciprocal(out.ap, in_.ap), outs=[out], ins=[in_])

    def memset(self, out, val, eng="pool"):
        return self.op(eng, lambda e: e.memset(out.ap, val), outs=[out])

    def dma(self, out, in_, eng="sp"):
        return self.op(eng, lambda e: e.dma_start(out=out.ap, in_=in_.ap), outs=[out], ins=[in_], dma=True)

    def emit(self):
        nc = self.nc
        st = self.stack
        dsems = [st.enter_context(nc.semaphore(f"dq{k}")) for k in range(NDSEM)]
        dcnt = self.dcnt
        esems = {}
        for e in self.ENGS:
            cnt = 0
            for ins in self.q[e]:
                if ins.dma:
                    ins.done = (dsems[ins.dsem], ins.done)
                elif ins.needed:
                    ep = cnt // EPOCH
                    if (e, ep) not in esems:
                        esems[(e, ep)] = st.enter_context(nc.semaphore(f"s_{e}_{ep}"))
                    cnt += 1
                    ins.done = (esems[(e, ep)], cnt - ep * EPOCH)
        q = self.q
        final_d = [(dsems[k], dcnt[k]) for k in range(NDSEM) if dcnt[k] > 0]

        def run(e, eng):
            waited = {}
            for ins in q[e]:
                for d in ins.deps:
                    sem, val = d.done
                    key = id(sem)
                    if waited.get(key, 0) < val:
                        eng.wait_ge(sem, val)
                        waited[key] = val
                inst = ins.fn(eng)
                if ins.dma:
                    inst.then_inc(ins.done[0], 16)
                elif ins.needed:
                    inst.then_inc(ins.done[0], 1)
            if e == "sp":
                for sem, val in final_d:
                    if waited.get(id(sem), 0) < val:
                        eng.wait_ge(sem, val)

        with nc.Block() as block:
            @block.tensor
            def _(eng):
                run("pe", eng)

            @block.scalar
            def _(eng):
                run("act", eng)

            @block.vector
            def _(eng):
                run("dve", eng)

            @block.gpsimd
            def _(eng):
                run("pool", eng)

            @block.sync
            def _(eng):
                run("sp", eng)


from concourse.bass_utils import run_bass_kernel_spmd

DEPTH = 4
NBLK = 33
SEQ = 4096
ALPHA = 8.0 ** 0.25
NS = 16
GROUPS = [(0, 4), (4, 4), (8, 4), (12, 4), (16, 4), (20, 4), (24, 3), (27, 3), (30, 3)]
NWST = 4
REAL = [0, 2, 4, 6]


def rr(r, pat, **kw):
    return R(r.ap.rearrange(pat, **kw), r.res, r.ps)


def bc(r, shape):
    return R(r.ap.broadcast_to(list(shape)), r.res, r.ps)


def build(depth=DEPTH, groups=GROUPS, do_sample=True, nblk=NBLK):
    nc = bass.Bass("TRN2", target_bir_lowering=False)
    D = {}

    def din(name, shape):
        D[name] = nc.dram_tensor(name, list(shape), F32, kind="ExternalInput").ap()

    def dout(name, shape):
        D[name] = nc.dram_tensor(name, list(shape), F32, kind="ExternalOutput").ap()

    din("xp", [SEQ, 1024]); din("meta", [16, 1024]); din("xs", [NS, 1024])
    din("ck", [4, NS, 128, 128]); din("cv", [4, NS, 128, 128]); din("sg", [4, NS, 4, 64, 128])
    din("wtok", [4, 128, 12288]); din("win_u", [4, 6, 128, 4096]); din("wglr", [4, 128, 128])
    din("wa2", [17, 4, 256]); din("sink", [1, 32])
    din("gng", [128, 4]); din("wpa_u", [4, 2, 128, 2048]); din("wpb_u", [4, 2, 128, 2048]); din("wout_u", [4, 2, 128, 4096])
    din("lnp", [128, 4, 4, 8]); din("wup_u", [4, 8, 128, 4096]); din("wdn_u", [4, 8, 128, 4096])
    din("ident", [128, 128]); din("uneg", [128, 128]); din("unegI", [128, 128]); din("masks", [128, 4, 512])
    din("cs", [128, NBLK, 16]); din("css", [NS, 16])
    dout("y", [SEQ, 1024]); dout("ys", [NS, 1024]); dout("pk", [4, 128, 128]); dout("pv", [4, 128, 128])
    dout("pst", [4, 4, 64, 128]); dout("sk", [4, NS, 128, 128]); dout("sv", [4, NS, 128, 128])
    dout("sst", [4, NS, 4, 64, 128])
    DR = {k: R(v, None) for k, v in D.items()}

    S = Sched(nc)
    import os as _os
    _target = _os.environ.get("KSTOP", "")

    class _Stop(Exception):
        pass

    def stage(name):
        if _target and name == _target:
            raise _Stop()
    with S.stack:
        S.init_psum()
        S.nrot = 6
        MUL, ADD, SUB = ALU.mult, ALU.add, ALU.subtract

        def psum():
            b = S.banks[S.psn % S.nrot]
            S.psn += 1
            return b
        bank7 = S.banks[7]
        bank6 = S.banks[6]

        ident = S.sb([128, 128], F32, "ident"); S.dma(ident, DR["ident"])
        uneg = S.sb([128, 128], F32, "uneg"); S.dma(uneg, DR["uneg"])
        unegI = S.sb([NS, NS], F32, "unegI"); S.dma(unegI, DR["unegI"][0:NS, 0:NS])
        masks = S.sb([128, 4, 512], BF16, "masks"); S.dma(masks, DR["masks"], eng="pool")
        mcur4, mprev4, m04, mprev1 = masks[:, 0, :], masks[:, 1, :], masks[:, 2, :], masks[:, 3, :]
        cs = S.sb([128, NBLK, 16], F32, "cs"); S.dma(cs, DR["cs"])
        css = S.sb([NS, 16], F32, "css"); S.dma(css, DR["css"])
        ones_bf = S.sb([128, 128], BF16, "ones_bf"); S.memset(ones_bf, 1.0)
        onesm_bf = S.sb([128, 128], BF16, "onesm_bf"); S.memset(onesm_bf, 1.0 / 1024.0)
        cst = S.sb([128, 4], F32, "cst")
        S.memset(cst[:, 0:1], 1.0); S.memset(cst[:, 1:2], 1e-5); S.memset(cst[:, 2:3], 1e-6)
        one_col, eps_ln, eps_rms = cst[:, 0:1], cst[:, 1:2], cst[:, 2:3]
        lnp = S.sb([128, 4, 4, 8], F32, "lnp"); S.dma(lnp, DR["lnp"])
        gng = S.sb([128, 4], F32, "gng"); S.dma(gng, DR["gng"])
        wa2 = S.sb([17, 4, 256], F32, "wa2"); S.dma(wa2, DR["wa2"])
        es_all = S.sb([128, 32], F32, "es_all")
        S.dma(es_all, R(D["sink"].broadcast_to([128, 32]), None))
        S.act(es_all, es_all, AF.Exp)

        xres = S.sb([128, 8, 512], F32, "xres")
        xres.res = [("xres", f) for f in range(8)]
        XR = [R(xres.ap[:, f], [("xres", f)]) for f in range(8)]
        xbf = S.sb([128, 8, 512], BF16, "xbf")
        hid_raw = S.sb([128, 16384], BF16, "hid")
        hidT = rr(hid_raw, "p (a b) -> p a b", a=32)
        gqk = R(hid_raw.ap[:, 0:4096].bitcast(F32).rearrange("p (a b) -> p a b", a=4), "hid")
        grs = rr(hid_raw[:, 4096:6144], "p (a b) -> p a b", a=4)
        attT = rr(hid_raw[:, 6144:8192], "p (a b) -> p a b", a=4)
        glaT = rr(hid_raw[:, 8192:10240], "p (a b) -> p a b", a=4)
        glrT = S.sb([17, 512], F32, "glrT"); S.memset(glrT, 1.0)
        mT = S.sb([128, 8, 512], BF16, "mT")
        wtok = S.sb([128, 8, 1536], BF16, "wtok")
        wst = [S.sb([128, 4096], BF16, f"wst{i}") for i in range(NWST)]
        wsn = [0]
        kTb = [[S.sb([128, 128], BF16, f"kT{l}_{s}") for s in range(2)] for l in range(depth)]
        vdb = [[S.sb([128, 2, 2, 64], BF16, f"vd{l}_{s}") for s in range(2)] for l in range(depth)]
        Sf = [S.sb([128, 2, 128], F32, f"Sf{l}") for l in range(depth)]
        Sbf = [S.sb([128, 2, 128], BF16, f"Sbf{l}") for l in range(depth)]
        for l in range(depth):
            S.memset(Sf[l], 0.0); S.memset(Sbf[l], 0.0)
        xin = S.sb([128, 1024], F32, "xin")
        ysb = xin
        TK = [dict(qtok=S.sb([128, 512], F32, f"qtok{p}"), ktok=S.sb([128, 128], F32, f"ktok{p}"),
                   vtok=S.sb([128, 128], F32, f"vtok{p}"), gktok=S.sb([128, 256], F32, f"gktok{p}"),
                   gvtok=S.sb([128, 512], BF16, f"gvtok{p}")) for p in range(2)]
        qtok, ktok, vtok, gktok, gvtok = (TK[0][k] for k in ("qtok", "ktok", "vtok", "gktok", "gvtok"))
        rtA = S.sb([128, 64], F32, "rtA"); rtB = S.sb([128, 64], F32, "rtB")
        qT_bf = S.sb([128, 512], BF16, "qT_bf")
        Pc = [S.sb([128, 512], BF16, f"Pc{g}") for g in range(2)]
        Pp = [S.sb([128, 512], BF16, f"Pp{g}") for g in range(2)]
        FP = [S.sb([128, 512], F32, f"F{i}") for i in range(7)]
        dtot, rec, lnv, rstd, t1 = FP[0], FP[1], FP[2], FP[3], FP[4]
        e1, sp, enb = FP[5][:, 0:256], FP[5][:, 256:512], FP[6][:, 0:256]
        sga, sgb, m1, m2 = FP[0], FP[1], FP[4], FP[5]
        msq, var, nmr, tmpA = FP[0], FP[1], FP[4], [FP[5], FP[6]]
        gvtok_f = FP[4][0:NS, :]
        ebp = S.sb([128, 2, 128], F32, "ebp"); ebn = S.sb([128, 2, 128], F32, "ebn")
        qtl = S.sb([128, 2, 128], BF16, "qtl"); ktl = S.sb([128, 2, 128], BF16, "ktl")
        ktokb = S.sb([128, 256], BF16, "ktokb")
        Am = S.sb([128, 512], BF16, "Am")
        osq = S.sb([128, 512], BF16, "osq")
        rbb = [S.sb([128, 512], BF16, f"rb{i}") for i in range(2)]
        rsb = [S.sb([128, 512], BF16, f"rsq{i}") for i in range(2)]
        m1buf = R(hid_raw.ap[:, 10240:14336].bitcast(F32).rearrange("p (a b) -> p a b", a=4), "m1buf")
        relu_t = [S.sb([128, 512], BF16, f"relu{i}") for i in range(2)]

        def wstream(src, KC, w):
            buf = wst[wsn[0] % NWST]
            wsn[0] += 1
            S.dma(buf[:, 0:KC * w], R(src, None), eng="pool")
            return R(buf.ap[:, 0:KC * w].rearrange("p (k n) -> p k n", k=KC), buf.res)

        def linear_units(units, KC, w, inT, T, evac, pre=()):
            ci = 0
            pre = list(pre)
            for ui, src in enumerate(units):
                wt = pre[ui] if ui < len(pre) else wstream(src, KC, w)
                for t0 in range(0, w, 128):
                    m = min(128, w - t0)
                    ps = psum()
                    for kc in range(KC):
                        S.mm(ps[0:m, 0:T], wt[:, kc, t0:t0 + m], inT[:, kc, 0:T], start=(kc == 0), stop=(kc == KC - 1))
                    evac(ci, ps[0:m, 0:T])
                    ci += 1

        def load_wtok(l):
            S.dma(rr(wtok, "p k n -> p (k n)"), R(D["wtok"][l], None), eng="pool")

        def tok_proj(xblk, M, cosR, sinR, tk=None):
            tk = tk or TK[0]
            qtok, ktok, vtok, gktok = tk["qtok"], tk["ktok"], tk["vtok"], tk["gktok"]
            ps_q, ps_kv, ps_gv = psum(), psum(), psum()
            for (ps, c0) in ((ps_q, 0), (ps_kv, 512), (ps_gv, 1024)):
                for kc in range(8):
                    S.mm(ps[0:M, 0:512], xblk[:, kc, :], wtok[:, kc, c0:c0 + 512], start=(kc == 0), stop=(kc == 7))
            qin = rr(ps_q[0:M, 0:512], "p (g hh d) -> p g hh d", g=2, hh=4)
            qo = rr(qtok[0:M, :], "p (hh g d) -> p g hh d", g=2, hh=4)
            S.act(qo, qin, AF.Copy)
            cosb = bc(R(cosR.ap.unsqueeze(1).unsqueeze(1), cosR.res), [M, 2, 4, 8])
            sinb = bc(R(sinR.ap.unsqueeze(1).unsqueeze(1), sinR.res), [M, 2, 4, 8])
            tA = rr(rtA[0:M, :], "p (g hh d) -> p g hh d", g=2, hh=4)
            tB = rr(rtB[0:M, :], "p (g hh d) -> p g hh d", g=2, hh=4)
            S.tt(tA, qin[:, :, :, 0:8], cosb, MUL); S.tt(tB, qin[:, :, :, 8:16], sinb, MUL)
            S.tt(qo[:, :, :, 0:8], tA, tB, SUB)
            S.tt(tA, qin[:, :, :, 8:16], cosb, MUL); S.tt(tB, qin[:, :, :, 0:8], sinb, MUL)
            S.tt(qo[:, :, :, 8:16], tA, tB, ADD)
            kin = rr(ps_kv[0:M, 0:128], "p (g d) -> p g d", g=2)
            ko = rr(ktok[0:M, :], "p (g d) -> p g d", g=2)
            S.act(ko, kin, AF.Copy)
            cos2 = bc(R(cosR.ap.unsqueeze(1), cosR.res), [M, 2, 8])
            sin2 = bc(R(sinR.ap.unsqueeze(1), sinR.res), [M, 2, 8])
            tA2 = rr(rtA[0:M, 0:16], "p (g d) -> p g d", g=2)
            tB2 = rr(rtB[0:M, 0:16], "p (g d) -> p g d", g=2)
            S.tt(tA2, kin[:, :, 0:8], cos2, MUL); S.tt(tB2, kin[:, :, 8:16], sin2, MUL)
            S.tt(ko[:, :, 0:8], tA2, tB2, SUB)
            S.tt(tA2, kin[:, :, 8:16], cos2, MUL); S.tt(tB2, kin[:, :, 0:8], sin2, MUL)
            S.tt(ko[:, :, 8:16], tA2, tB2, ADD)
            S.act(vtok[0:M, :], ps_kv[0:M, 128:256], AF.Copy)
            S.copy(gktok[0:M, :], ps_kv[0:M, 256:512])
            return ps_gv

        def gla_finish(l, o_ps, T, c0):
            n = 4 * T
            S.act(osq[:, 0:n], o_ps, AF.Square)
            ss = psum()
            S.mm(ss[:, 0:n], ones_bf, osq[:, 0:n])
            S.act(lnv[:, 0:n], ss[:, 0:n], AF.Ln, scale=1.0 / 128.0, bias=eps_rms)
            S.act(rstd[:, 0:n], lnv[:, 0:n], AF.Exp, scale=-0.5)
            S.tt(t1[:, 0:n], o_ps, rstd[:, 0:n], MUL)
            S.stt(glaT[:, :, c0:c0 + T], rr(t1[:, 0:n], "p (h t) -> p h t", h=4), gng[:, l:l + 1],
                  grs[:, :, c0:c0 + T], MUL, MUL)

        stat_pending = []

        def stat_mm(T, f):
            S.mm(bank6[:, 0:T], onesm_bf, rbb[f % 2][:, 0:T], start=(f == 0), stop=(f == 7))
            S.mm(bank7[:, 0:T], onesm_bf, rsb[f % 2][:, 0:T], start=(f == 0), stop=(f == 7))

        def stat_push(T, f):
            S.act(rbb[f % 2][:, 0:T], XR[f][:, 0:T], AF.Copy)
            S.act(rsb[f % 2][:, 0:T], XR[f][:, 0:T], AF.Square)
            stat_pending.append(f)
            if len(stat_pending) > 1:
                stat_mm(T, stat_pending.pop(0))

        def ln_fm(T, l, which, zero_pad):
            while stat_pending:
                stat_mm(T, stat_pending.pop(0))
            mean_ps, ex2_ps = bank6, bank7
            S.act(msq[:, 0:T], mean_ps[:, 0:T], AF.Square)
            S.tt(var[:, 0:T], ex2_ps[:, 0:T], msq[:, 0:T], SUB)
            S.act(lnv[:, 0:T], var[:, 0:T], AF.Ln, bias=eps_ln)
            S.act(rstd[:, 0:T], lnv[:, 0:T], AF.Exp, scale=-0.5)
            S.stt(nmr[:, 0:T], mean_ps[:, 0:T], -1.0, rstd[:, 0:T], MUL, MUL)
            for f in range(8):
                ta = tmpA[f % 2]
                S.tt(ta[:, 0:T], XR[f][:, 0:T], rstd[:, 0:T], MUL)
                S.tt(ta[:, 0:T], ta[:, 0:T], nmr[:, 0:T], ADD)
                gcol = lnp[:, l, 2 * which, f:f + 1]
                bcol = lnp[:, l, 2 * which + 1, f:f + 1]
                S.act(xbf[:, f, 0:T], ta[:, 0:T], AF.Identity, scale=gcol, bias=bcol)
                S.ts(XR[f][:, 0:T], ta[:, 0:T], gcol, MUL, bcol, ADD, eng="pool")
            if zero_pad:
                S.memset(xres[:, :, 0:112], 0.0, eng="dve")
                S.memset(xbf[:, :, 0:112], 0.0, eng="dve")

        pref = {}

        def dense_tail(l, T, zero_pad, nxt_l=None):
            for fq in range(2):
                wga = wstream(D["win_u"][l, 2 + fq], 8, 512)
                wpa = wstream(D["wpa_u"][l, fq], 4, 512)
                for ft in range(4):
                    cs_ = slice(ft * 128, ft * 128 + 128)
                    pga, ppa = psum(), psum()
                    for kc in range(8):
                        S.mm(pga[:, 0:T], wga[:, kc, cs_], xbf[:, kc, 0:T], start=(kc == 0), stop=(kc == 7))
                    for kc in range(4):
                        S.mm(ppa[:, 0:T], wpa[:, kc, cs_], attT[:, kc, 0:T], start=(kc == 0), stop=(kc == 3))
                    S.act(sga[:, 0:T], pga[:, 0:T], AF.Sigmoid)
                    S.tt(m1buf[:, ft, 0:T], ppa[:, 0:T], sga[:, 0:T], MUL)
                wgb = wstream(D["win_u"][l, 4 + fq], 8, 512)
                wpb = wstream(D["wpb_u"][l, fq], 4, 512)
                for ft in range(4):
                    f = fq * 4 + ft
                    cs_ = slice(ft * 128, ft * 128 + 128)
                    pgb, ppb = psum(), psum()
                    for kc in range(8):
                        S.mm(pgb[:, 0:T], wgb[:, kc, cs_], xbf[:, kc, 0:T], start=(kc == 0), stop=(kc == 7))
                    for kc in range(4):
                        S.mm(ppb[:, 0:T], wpb[:, kc, cs_], glaT[:, kc, 0:T], start=(kc == 0), stop=(kc == 3))
                    S.act(sgb[:, 0:T], pgb[:, 0:T], AF.Sigmoid)
                    S.tt(m2[:, 0:T], ppb[:, 0:T], sgb[:, 0:T], MUL)
                    S.tt(mT[:, f, 0:T], m1buf[:, ft, 0:T], m2[:, 0:T], ADD)

            def ev_res(ci, ps):
                S.stt(XR[ci][:, 0:T], XR[ci][:, 0:T], ALPHA, ps, MUL, ADD)
                stat_push(T, ci)
            linear_units([D["wout_u"][l, u] for u in range(2)], 8, 512, mT, T, ev_res)
            pre_up = [wstream(D["wup_u"][l, u], 8, 512) for u in range(2)]
            ln_fm(T, l, 0, False)

            def ev_up(ci, ps):
                rt = relu_t[ci % 2]
                S.act(rt[:, 0:T], ps, AF.Relu)
                S.tt(hidT[:, ci, 0:T], rt[:, 0:T], rt[:, 0:T], MUL)
            linear_units([D["wup_u"][l, u] for u in range(8)], 8, 512, xbf, T, ev_up, pre=pre_up)
            for cg in range(2):
                acc = [psum() for _ in range(4)]
                for kb in range(4):
                    wt = wstream(D["wdn_u"][l, cg * 4 + kb], 8, 512)
                    for ft in range(4):
                        for kc in range(8):
                            S.mm(acc[ft][:, 0:T], wt[:, kc, ft * 128:(ft + 1) * 128], hidT[:, kb * 8 + kc, 0:T],
                                 start=(kb == 0 and kc == 0), stop=(kb == 3 and kc == 7))
                for ft in range(4):
                    ev_res(cg * 4 + ft, acc[ft][:, 0:T])
            if nxt_l is not None:
                load_wtok(nxt_l)
                pref["gqk"] = (nxt_l, wstream(D["win_u"][nxt_l, 0], 8, 512))
            ln_fm(T, l, 1, zero_pad)

        def phase_a(l, T):
            def ev_gqk(ci, ps):
                S.copy(gqk[:, ci, 0:T], ps)
            pre = []
            if pref.get("gqk") and pref["gqk"][0] == l:
                pre = [pref.pop("gqk")[1]]
            linear_units([D["win_u"][l, 0]], 8, 512, xbf, T, ev_gqk, pre=pre)

            def ev_glr(ci, ps):
                S.copy(glrT[0:16, 0:T], ps)
            linear_units([D["wglr"][l]], 8, 16, xbf, T, ev_glr)

            def ev_gr(ci, ps):
                S.act(grs[:, ci, 0:T], ps, AF.Silu)
            linear_units([D["win_u"][l, 1]], 8, 512, xbf, T, ev_gr)

        def load_x(b0, nb):
            for j in range(nb):
                n = b0 + j
                if n == 0:
                    S.memset(xin, 0.0, eng="dve")
                    S.dma(xin[112:128, :], DR["meta"])
                else:
                    S.dma(xin, DR["xp"][(n - 1) * 128:n * 128, :])
                for half in range(2):
                    ps = psum()
                    for ff in range(4):
                        f = half * 4 + ff
                        S.tr(ps[:, ff * 128:(ff + 1) * 128], xin[:, f * 128:(f + 1) * 128], ident)
                    pv = rr(ps[:, 0:512], "p (f t) -> p f t", f=4)
                    S.copy(xres[:, half * 4:half * 4 + 4, j * 128:(j + 1) * 128], pv)
                    S.act(xbf[:, half * 4:half * 4 + 4, j * 128:(j + 1) * 128], pv, AF.Copy)

        def store_y(b0, nb):
            for j in range(nb):
                n = b0 + j
                if n == 0:
                    continue
                for half in range(2):
                    ps = psum()
                    for ff in range(4):
                        f = half * 4 + ff
                        S.tr(ps[:, ff * 128:(ff + 1) * 128], xres[:, f, j * 128:(j + 1) * 128], ident)
                    if half == 0:
                        S.copy(ysb[:, 0:512], ps[:, 0:512])
                    else:
                        S.act(ysb[:, 512:1024], ps[:, 0:512], AF.Copy)
                S.dma(DR["y"][(n - 1) * 128:n * 128, :], ysb)

        hooks = []

        def hook():
            if hooks:
                hooks.pop(0)()

        PcB = [R(hid_raw.ap[:, 14336 + g * 512:14336 + (g + 1) * 512], ("PcB", g)) for g in range(2)]
        PpB = [R(hid_raw.ap[:, 15360 + g * 512:15360 + (g + 1) * 512], ("PpB", g)) for g in range(2)]
        BT = [dict(qT_bf=qT_bf, Pc=Pc, Pp=Pp, ebp=ebp, ebn=ebn, qtl=qtl, ktl=ktl, ktokb=ktokb, Am=Am),
              dict(qT_bf=S.sb([128, 512], BF16, "qT_bfB"), Pc=PcB, Pp=PpB,
                   ebp=S.sb([128, 2, 128], F32, "ebpB"), ebn=S.sb([128, 2, 128], F32, "ebnB"),
                   qtl=S.sb([128, 2, 128], BF16, "qtlB"), ktl=S.sb([128, 2, 128], BF16, "ktlB"),
                   ktokb=S.sb([128, 256], BF16, "ktokbB"), Am=S.sb([128, 512], BF16, "AmB"))]

        def tok_stage(l, n, j):
            tk = TK[j % 2]
            csl = slice(j * 128, j * 128 + 128)
            ps_gv = tok_proj(xbf[:, :, csl], 128, cs[:, n, 0:8], cs[:, n, 8:16], tk)
            S.act(tk["gvtok"], ps_gv[:, 0:512], AF.Copy)
            if n == nblk - 1:
                S.dma(DR["pk"][l], tk["ktok"])
                S.dma(DR["pv"][l], tk["vtok"])

        def st_T(l, n, j):
            tk, bt = TK[j % 2], BT[j % 2]
            qT_ps = psum()
            for hh in range(4):
                S.tr(qT_ps[:, hh * 128:(hh + 1) * 128], tk["qtok"][:, hh * 128:(hh + 1) * 128], ident)
            S.copy(bt["qT_bf"], qT_ps[:, 0:512])
            kT_ps = psum()
            S.tr(kT_ps[:, 0:128], tk["ktok"], ident)
            S.act(kTb[l][n % 2], kT_ps[:, 0:128], AF.Copy)

        def st_vd(l, n, j):
            tk = TK[j % 2]
            vd = vdb[l][n % 2]
            v3 = rr(tk["vtok"], "p (g d) -> p g d", g=2)
            S.copy(vd[:, :, 0, :], v3); S.act(vd[:, :, 1, :], v3, AF.Copy)

        def st_S(l, n, j):
            bt = BT[j % 2]
            kTc, kTp = kTb[l][n % 2], kTb[l][(n - 1) % 2]
            maskc = m04 if n == 0 else mcur4
            maskp = mprev1 if n == 1 else mprev4
            for g in range(2):
                rs_ = slice(g * 64, g * 64 + 64)
                Sc = psum()
                S.mm(Sc[:, 0:512], kTc[rs_, :], bt["qT_bf"][rs_, :])
                S.act(bt["Pc"][g], Sc[:, 0:512], AF.Exp, scale=0.125)
                S.tt(bt["Pc"][g], bt["Pc"][g], maskc, MUL)
                if n > 0:
                    Sp_ = psum()
                    S.mm(Sp_[:, 0:512], kTp[rs_, :], bt["qT_bf"][rs_, :])
                    S.act(bt["Pp"][g], Sp_[:, 0:512], AF.Exp, scale=0.125)
                    S.tt(bt["Pp"][g], bt["Pp"][g], maskp, MUL)

        def st_PV(l, n, j, g):
            bt = BT[j % 2]
            csl = slice(j * 128, j * 128 + 128)
            vd, vdp = vdb[l][n % 2], vdb[l][(n - 1) % 2]
            Pc_, Pp_ = bt["Pc"], bt["Pp"]
            O, Dn = psum(), psum()
            vc = rr(vd[:, g], "p a d -> p (a d)")
            if n > 0:
                vp = rr(vdp[:, g], "p a d -> p (a d)")
                S.mm(O[:, 0:512], vp, Pp_[g], start=True, stop=False)
                S.mm(O[:, 0:512], vc, Pc_[g], start=False, stop=True)
                S.mm(Dn[:, 0:512], ones_bf, Pp_[g], start=True, stop=False)
                S.mm(Dn[:, 0:512], ones_bf, Pc_[g], start=False, stop=True)
            else:
                S.mm(O[:, 0:512], vc, Pc_[g])
                S.mm(Dn[:, 0:512], ones_bf, Pc_[g])
            for hh in range(4):
                hc = slice(hh * 128, (hh + 1) * 128)
                S.act(dtot[:, hc], Dn[:, hc], AF.Ln, bias=es_all[:, l * 8 + g * 4 + hh:l * 8 + g * 4 + hh + 1])
            S.act(rec, dtot, AF.Exp, scale=-1.0)
            for hh in range(4):
                h = 4 * g + hh
                r0 = (h % 2) * 64
                S.tt(attT[r0:r0 + 64, h // 2, csl], O[r0:r0 + 64, hh * 128:(hh + 1) * 128],
                     rec[r0:r0 + 64, hh * 128:(hh + 1) * 128], MUL)

        def st_GA(l, n, j):
            tk, bt = TK[j % 2], BT[j % 2]
            csl = slice(j * 128, j * 128 + 128)
            z = psum()
            S.mm(z[:, 0:256], glrT[0:17, csl], wa2[0:17, l, :])
            S.act(e1, z[:, 0:256], AF.Exp, scale=-1.0)
            S.act(sp, e1, AF.Ln, bias=one_col)
            bt_ps = psum()
            S.mm(bt_ps[:, 0:256], uneg, sp)
            bT = psum()
            for i in range(2):
                S.mm(bT[:, i * 128:(i + 1) * 128], sp[:, i * 128:(i + 1) * 128], uneg)
            bT3 = rr(bT[:, 0:256], "p (i t) -> p i t", i=2)
            S.act(bt["ebp"], bT3, AF.Exp)
            S.act(bt["ebn"], bT3, AF.Exp, scale=-1.0)
            S.act(enb, bt_ps[:, 0:256], AF.Exp, scale=-1.0)
            S.stt(bt["qtl"], gqk[:, 0:2, csl], 0.125, bt["ebp"], MUL, MUL)
            S.tt(bt["ktl"], gqk[:, 2:4, csl], bt["ebn"], MUL)
            S.tt(bt["ktokb"], tk["gktok"], enb, MUL)

        def st_GB1(l, n, j):
            bt = BT[j % 2]
            qtl_, ktl_, Am_ = bt["qtl"], bt["ktl"], bt["Am"]
            A2 = [psum(), psum()]
            for h in range(4):
                i, r0 = h // 2, (h % 2) * 64
                S.mm(A2[h % 2][:, h * 128:(h + 1) * 128], ktl_[r0:r0 + 64, i, :], qtl_[r0:r0 + 64, i, :])
            Am4 = rr(Am_, "p (a b c) -> p a b c", a=2, b=2)
            mk4 = rr(mcur4, "p (a b c) -> p a b c", a=2, b=2)
            for par in range(2):
                S.tt(Am4[:, :, par, :], rr(A2[par][:, 0:512], "p (a b c) -> p a b c", a=2, b=2)[:, :, par, :],
                     mk4[:, :, par, :], MUL)

        def st_GB2(l, n, j):
            tk, bt = TK[j % 2], BT[j % 2]
            c0 = j * 128
            gvtok_, qtl_, Am_, ktokb_, ebp_ = tk["gvtok"], bt["qtl"], bt["Am"], bt["ktokb"], bt["ebp"]
            o = psum()
            for h in range(4):
                i, r0 = h // 2, (h % 2) * 64
                hs = slice(h * 128, (h + 1) * 128)
                S.mm(o[:, hs], gvtok_[:, hs], Am_[:, hs], start=True, stop=False)
                S.mm(o[:, hs], Sbf[l][r0:r0 + 64, i, :], qtl_[r0:r0 + 64, i, :], start=False, stop=True)
            for i in range(2):
                U = psum()
                S.mm(U[:, 0:256], ktokb_[:, i * 128:(i + 1) * 128], gvtok_[:, i * 256:(i + 1) * 256])
                for hh in range(2):
                    r0 = hh * 64
                    S.tt(Sf[l][r0:r0 + 64, i, :], U[r0:r0 + 64, hh * 128:(hh + 1) * 128], Sf[l][r0:r0 + 64, i, :], ADD)
                S.ts(Sf[l][:, i, :], Sf[l][:, i, :], ebp_[:, i, 127:128], MUL)
                S.act(Sbf[l][:, i, :], Sf[l][:, i, :], AF.Copy)
            gla_finish(l, o[:, 0:512], 128, c0)
            if n == nblk - 1:
                dst = D["pst"][l].rearrange("(i hh) dk dv -> (hh dk) i dv", hh=2)
                S.dma(R(dst, None), Sf[l])

        def block_phase(l, b0, nb):
            tok_stage(l, b0, 0)
            if nb > 1:
                tok_stage(l, b0 + 1, 1)
            st_T(l, b0, 0); st_vd(l, b0, 0); st_S(l, b0, 0); hook(); st_GA(l, b0, 0); st_GB1(l, b0, 0); hook()
            for j in range(nb):
                n = b0 + j
                nxt = j + 1 < nb
                st_PV(l, n, j, 0)
                if nxt:
                    st_T(l, n + 1, j + 1)
                hook()
                st_PV(l, n, j, 1)
                if nxt:
                    st_vd(l, n + 1, j + 1)
                    st_S(l, n + 1, j + 1)
                hook()
                st_GB2(l, n, j)
                hook()
                if nxt:
                    st_GA(l, n + 1, j + 1)
                    hook()
                    st_GB1(l, n + 1, j + 1)
                if j + 2 < nb:
                    tok_stage(l, n + 2, j + 2)
                hook()

        if do_sample:
            Ss = S.sb([128, 4, 2, 128], F32, "Ss")
            kwin = R(hid_raw.ap[:, 10240:14336].bitcast(F32).rearrange("p (a b c) -> p a b c", a=2, b=8), "m1buf")
            TKS = dict(qtok=FP[2][0:NS, :], ktok=S.sb([NS, 128], F32, "ktok_s"), vtok=S.sb([NS, 128], F32, "vtok_s"),
                       gktok=S.sb([NS, 256], F32, "gktok_s"))
            gvs = S.sb([NS, 512], F32, "gvs")
            KwT = [S.sb([128, 128], BF16, f"KwT{i}") for i in range(2)]
            Vwd = [S.sb([128, 2, 2, 64], BF16, f"Vwd{i}") for i in range(2)]
            qTs = S.sb([128, 4, NS], BF16, "qTs")
            Ps = S.sb([128, 128], BF16, "Ps")
            ebs = S.sb([128, 2, NS], F32, "ebs")
            qsT = S.sb([128, 2, NS], F32, "qsT")
            kmask = [FP[6][0:NS, 0:256], FP[6][0:NS, 256:512]]

        def sample_parts(l, sc0=0):
            M = NS
            tk = TKS
            qtok, ktok, vtok, gktok = tk["qtok"], tk["ktok"], tk["vtok"], tk["gktok"]
            gvtok_f = gvs

            skR = R(D["sk"][l], ("sk", l))
            svR = R(D["sv"][l], ("sv", l))

            def stA(half):
                h8 = half * 8
                for a_, (cin, outR) in enumerate(((D["ck"], skR), (D["cv"], svR))):
                    S.dma(kwin[0:127, a_], R(cin[l, h8:h8 + 8, 1:128, :].rearrange("b r c -> r b c"), None))
                    S.dma(kwin[127:128, a_], R(outR.ap[h8:h8 + 8, 127:128, :].rearrange("b r c -> r b c"), outR.res))
                    S.dma(R(outR.ap[h8:h8 + 8, 0:127, :].rearrange("b r c -> r b c"), None), kwin[0:127, a_])

            def stB(bi):
                Kw, Vw, KT, Vd = kwin[:, 0, bi % 8, :], kwin[:, 1, bi % 8, :], KwT[bi % 2], Vwd[bi % 2]
                kt_ps = psum()
                S.tr(kt_ps[:, 0:128], Kw, ident)
                S.act(KT, kt_ps[:, 0:128], AF.Copy)
                v3 = rr(Vw, "p (g d) -> p g d", g=2)
                S.copy(Vd[:, :, 0, :], v3); S.act(Vd[:, :, 1, :], v3, AF.Copy)
                for g in range(2):
                    rs_ = slice(g * 64, g * 64 + 64)
                    sc = psum()
                    S.mm(sc[:, 0:4], KT[rs_, :], qTs[rs_, :, bi])
                    S.act(Ps[:, bi * 8 + g * 4:bi * 8 + g * 4 + 4], sc[:, 0:4], AF.Exp, scale=0.125)
                for g in range(2):
                    c = bi * 8 + g * 4
                    S.mm(bank7[:, c:c + 4], rr(Vd[:, g], "p a d -> p (a d)"), Ps[:, c:c + 4])

            def p_pre():
                ps_gv = tok_proj(xbf[:, :, sc0:sc0 + M], M, css[:, 0:8], css[:, 8:16], tk)
                S.act(gvtok_f, ps_gv[0:M, 0:512], AF.Copy)
                qT_ps = psum()
                for hh in range(4):
                    S.tr(qT_ps[:, hh * M:(hh + 1) * M], qtok[0:M, hh * 128:(hh + 1) * 128], ident[0:M, 0:M])
                S.copy(qTs, rr(qT_ps[:, 0:4 * M], "p (h b) -> p h b", h=4))
                S.dma(skR[:, 127, :], ktok[0:M, :])
                S.dma(svR[:, 127, :], vtok[0:M, :])
                stA(0)

            def p_bi(b0_):
                def f():
                    if b0_ == 8:
                        stA(1)
                    for bi in (b0_, b0_ + 1):
                        stB(bi)
                return f

            def p_post():
                Dn = psum()
                S.mm(Dn[:, 0:128], ones_bf, Ps)
                essb = bc(R(es_all.ap[:, l * 8:(l + 1) * 8].unsqueeze(1), es_all.res), [128, NS, 8])
                S.tt(rr(dtot[:, 0:128], "p (b h) -> p b h", h=8), rr(Dn[:, 0:128], "p (b h) -> p b h", h=8), essb, ADD)
                S.recip(rec[:, 0:128], dtot[:, 0:128])
                O3 = rr(bank7[:, 0:128], "p (b h) -> p h b", h=8)
                r3 = rr(rec[:, 0:128], "p (b h) -> p h b", h=8)
                for h in range(8):
                    r0 = (h % 2) * 64
                    S.tt(attT[r0:r0 + 64, h // 2, sc0:sc0 + M], O3[r0:r0 + 64, h, :], r3[r0:r0 + 64, h, :], MUL)
                z = psum()
                S.mm(z[0:M, 0:256], glrT[0:17, sc0:sc0 + M], wa2[0:17, l, :])
                S.act(e1[0:M, :], z[0:M, 0:256], AF.Exp, scale=-1.0)
                S.act(sp[0:M, :], e1[0:M, :], AF.Ln, bias=one_col[0:M, :])
                bT = psum()
                for i in range(2):
                    S.mm(bT[:, i * M:(i + 1) * M], sp[0:M, i * 128:(i + 1) * 128], unegI[0:M, 0:M])
                S.act(ebs, rr(bT[:, 0:2 * M], "p (i b) -> p i b", i=2), AF.Exp)
                S.ts(qsT, gqk[:, 0:2, sc0:sc0 + M], 0.125, MUL)

            def p_gla(b4):
                def f():
                    src = D["sg"][l, b4:b4 + 4].rearrange("b (i hh) dk dv -> (hh dk) b i dv", hh=2)
                    S.dma(Ss, R(src, None))
                    for bb in range(4):
                        b = b4 + bb
                        km = kmask[b % 2]
                        S.ts(km, gktok[0:M, :], ident[0:M, b:b + 1], MUL)
                        for i in range(2):
                            U = psum()
                            S.mm(U[:, 0:256], km[:, i * 128:(i + 1) * 128], gvtok_f[:, i * 256:(i + 1) * 256])
                            for hh in range(2):
                                r0 = hh * 64
                                S.stt(Ss[r0:r0 + 64, bb, i, :], Ss[r0:r0 + 64, bb, i, :], ebs[r0:r0 + 64, i, b:b + 1],
                                      U[r0:r0 + 64, hh * 128:(hh + 1) * 128], MUL, ADD)
                            for hh in range(2):
                                r0 = hh * 64
                                h = 2 * i + hh
                                ob = bank7 if hh == 0 else bank6
                                S.mm(ob[:, 256 + h * M + b:256 + h * M + b + 1], Ss[r0:r0 + 64, bb, i, :],
                                     qsT[r0:r0 + 64, i, b:b + 1])
                    dst = D["sst"][l, b4:b4 + 4].rearrange("b (i hh) dk dv -> (hh dk) b i dv", hh=2)
                    S.dma(R(dst, None), Ss)
                return f

            def p_fin():
                o_sb = FP[6][:, 0:4 * M]
                o4 = rr(o_sb, "p (i hh b) -> p i hh b", i=2, hh=2)
                for hh, ob in ((0, bank7), (1, bank6)):
                    S.copy(o4[:, :, hh, :], rr(ob[:, 256:256 + 4 * M], "p (i hh b) -> p i hh b", i=2, hh=2)[:, :, hh, :])
                gla_finish(l, o_sb, M, sc0)

            return [p_pre] + [p_bi(b0_) for b0_ in range(0, NS, 2)] + [p_post] + [p_gla(b4) for b4 in range(0, NS, 4)] + [p_fin]

        def load_sample(sc0):
            M = NS
            S.dma(xin[0:M, :], DR["xs"])
            for half in range(2):
                ps = psum()
                for ff in range(4):
                    f = half * 4 + ff
                    S.tr(ps[:, ff * M:(ff + 1) * M], xin[0:M, f * 128:(f + 1) * 128], ident[0:M, 0:M])
                pv = rr(ps[:, 0:4 * M], "p (f t) -> p f t", f=4)
                S.copy(xres[:, half * 4:half * 4 + 4, sc0:sc0 + M], pv)
                S.act(xbf[:, half * 4:half * 4 + 4, sc0:sc0 + M], pv, AF.Copy)

        def store_sample(sc0):
            M = NS
            for half in range(2):
                ps = psum()
                for ff in range(4):
                    f = half * 4 + ff
                    S.tr(ps[0:M, ff * 128:(ff + 1) * 128], xres[:, f, sc0:sc0 + M], ident)
                S.copy(ysb[0:M, half * 512:(half + 1) * 512], ps[0:M, 0:512])
            S.dma(DR["ys"], ysb[0:M, :])

        def main_program():
            stage("consts")
            for gi, (b0, nb) in enumerate(groups):
                T = nb * 128
                merged = do_sample and gi == len(groups) - 1 and nb <= 3
                load_x(b0, nb)
                if b0 == 0:
                    S.memset(xres[:, :, 0:112], 0.0, eng="dve")
                    S.memset(xbf[:, :, 0:112], 0.0, eng="dve")
                if merged:
                    load_sample(T)
                    T = T + NS
                stage("loadx")
                for l in range(depth):
                    if not (pref.get("gqk") and pref["gqk"][0] == l):
                        load_wtok(l)
                    stage("wtok")
                    phase_a(l, T)
                    stage("phase_a")
                    if merged:
                        hooks.extend(sample_parts(l, nb * 128))
                    block_phase(l, b0, nb)
                    while hooks:
                        hooks.pop(0)()
                    stage("blocks")
                    last_gl = (gi == len(groups) - 1 and l == depth - 1)
                    dense_tail(l, T, b0 == 0, None if last_gl else (l + 1) % depth)
                    stage("dense")
                store_y(b0, nb)
                if merged:
                    store_sample(nb * 128)
                stage("store_y")
            if do_sample and not merged:
                M = NS
                load_sample(0)
                for l in range(depth):
                    load_wtok(l)
                    phase_a(l, M)
                    stage("s_phase_a")
                    for p in sample_parts(l, 0):
                        p()
                    stage("s_mixer")
                    dense_tail(l, M, False)
                store_sample(0)

        try:
            main_program()
        except _Stop:
            pass
        S.emit()
    return nc


def _consts():
    j = np.arange(128)[:, None]
    i = np.arange(128)[None, :]
    mcur = (j <= i).astype(np.float32)
    mprev = (j > i).astype(np.float32)
    m0 = ((j <= i) & (j >= 112)).astype(np.float32)
    mprev1 = ((j > i) & (j >= 112)).astype(np.float32)
    masks = np.stack([np.tile(m, (1, 4)) for m in (mcur, mprev, m0, mprev1)], axis=1)
    uneg = (-(1.0 / 16.0) * (j <= i)).astype(np.float32)
    unegI = (-(1.0 / 16.0) * np.eye(128)).astype(np.float32)
    inv = (500000.0 ** (-np.arange(8, dtype=np.float32) * 2.0 / 16.0)).astype(np.float32)
    slot = np.arange(NBLK)[None, :] * 128 + np.arange(128)[:, None]
    pos = np.maximum(slot - 112, 0).astype(np.float32)
    ang = pos[:, :, None] * inv[None, None, :]
    cs = np.concatenate([np.cos(ang), np.sin(ang)], -1).astype(np.float32)
    angs = (np.float32(8192.0) * inv)[None, :].repeat(NS, 0)
    css = np.concatenate([np.cos(angs), np.sin(angs)], -1).astype(np.float32)
    return dict(ident=np.eye(128, dtype=np.float32), uneg=uneg, unegI=unegI,
                masks=np.ascontiguousarray(masks), cs=np.ascontiguousarray(cs), css=np.ascontiguousarray(css))


_NC_CACHE = {}


def _prep_shared(meta_tokens, w_in, w_a2, b_a, attn_sink, gla_norm_g, w_proj_a, w_proj_b, w_out, ln1_g, ln1_b,
                 w_up, w_down, ln2_g, ln2_b):
    f = lambda a: np.ascontiguousarray(np.asarray(a, dtype=np.float32))
    w_in = f(w_in)
    lnp = np.stack([f(ln1_g), f(ln1_b), f(ln2_g), f(ln2_b)], axis=1)
    lnp = lnp.reshape(4, 4, 8, 128).transpose(3, 0, 1, 2)
    ucols = np.concatenate([np.arange(c, c + 512) for c in (768, 1808, 2320, 2832, 3344, 3856)])
    win_u = w_in[:, :, ucols].reshape(4, 8, 128, 6, 512).transpose(0, 3, 2, 1, 4).reshape(4, 6, 128, 4096)
    tcols = np.concatenate([np.arange(0, 768), np.arange(1024, 1792)])
    wtok = w_in[:, :, tcols].reshape(4, 8, 128, 1536).transpose(0, 2, 1, 3).reshape(4, 128, 12288)
    wglr = w_in[:, :, 1792:1808].reshape(4, 8, 128, 16).transpose(0, 2, 1, 3).reshape(4, 128, 128)
    wpa_u = f(w_proj_a).reshape(4, 4, 128, 2, 512).transpose(0, 3, 2, 1, 4).reshape(4, 2, 128, 2048)
    wpb_u = f(w_proj_b).reshape(4, 4, 128, 2, 512).transpose(0, 3, 2, 1, 4).reshape(4, 2, 128, 2048)
    wout_u = f(w_out).reshape(4, 8, 128, 2, 512).transpose(0, 3, 2, 1, 4).reshape(4, 2, 128, 4096)
    wup_u = f(w_up).reshape(4, 8, 128, 8, 512).transpose(0, 3, 2, 1, 4).reshape(4, 8, 128, 4096)
    wdn_u = f(w_down).reshape(4, 4, 8, 128, 2, 512).transpose(0, 4, 1, 3, 2, 5).reshape(4, 8, 128, 4096)
    cst = _consts()
    return dict(
        meta=f(meta_tokens), wtok=f(wtok), win_u=f(win_u), wglr=f(wglr),
        wa2=f(np.concatenate([np.transpose(f(w_a2), (1, 0, 2)), f(b_a)[None]], 0)),
        sink=f(attn_sink).reshape(1, 32), gng=f(np.transpose(f(gla_norm_g), (1, 0))), wpa_u=f(wpa_u), wpb_u=f(wpb_u),
        wout_u=f(wout_u), lnp=f(lnp), wup_u=f(wup_u), wdn_u=f(wdn_u), **cst)


def kernel(x_prompt, x_sample, cache_k_win, cache_v_win, state_gla, meta_tokens, w_in, w_a2, b_a, attn_sink,
           gla_norm_g, w_proj_a, w_proj_b, w_out, ln1_g, ln1_b, w_up, w_down, ln2_g, ln2_b):
    f = lambda a: np.ascontiguousarray(np.asarray(a, dtype=np.float32))
    x_prompt, x_sample, cache_k_win, cache_v_win, state_gla = map(f, (x_prompt, x_sample, cache_k_win, cache_v_win, state_gla))
    if "nc" not in _NC_CACHE:
        _NC_CACHE["nc"] = build()
    nc = _NC_CACHE["nc"]
    shared = _prep_shared(meta_tokens, w_in, w_a2, b_a, attn_sink, gla_norm_g, w_proj_a, w_proj_b, w_out,
                          ln1_g, ln1_b, w_up, w_down, ln2_g, ln2_b)
    in_maps = []
    zero_xp = np.zeros_like(x_prompt[0])
    for c in range(8):
        sl = slice(c * NS, (c + 1) * NS)
        m = dict(shared)
        m["xp"] = x_prompt[REAL.index(c)] if c in REAL else zero_xp
        m["xs"] = f(x_sample[sl, 0, :])
        m["ck"] = f(cache_k_win[:, sl].reshape(4, NS, 128, 128))
        m["cv"] = f(cache_v_win[:, sl].reshape(4, NS, 128, 128))
        m["sg"] = f(state_gla[:, sl])
        in_maps.append(m)
    res = run_bass_kernel_spmd(nc, in_maps, core_ids=list(range(8)))
    rs = res.results
    y_prompt = np.stack([rs[c]["y"] for c in REAL], 0)
    y_sample = np.concatenate([rs[c]["ys"] for c in range(8)], 0).reshape(128, 1, 1024)
    pk = np.stack([rs[c]["pk"] for c in REAL], 1).reshape(4, 4, 128, 2, 64)
    pv = np.stack([rs[c]["pv"] for c in REAL], 1).reshape(4, 4, 128, 2, 64)
    pst = np.stack([rs[c]["pst"] for c in REAL], 1)
    sk = np.concatenate([rs[c]["sk"] for c in range(8)], 1).reshape(4, 128, 128, 2, 64)
    sv = np.concatenate([rs[c]["sv"] for c in range(8)], 1).reshape(4, 128, 128, 2, 64)
    sst = np.concatenate([rs[c]["sst"] for c in range(8)], 1)
    return tuple(np.ascontiguousarray(a, dtype=np.float32) for a in (y_prompt, y_sample, pk, pv, pst, sk, sv, sst))
```

```python
import numpy as np
import contextlib
import concourse.bass as bass
import concourse.mybir as mybir

F32 = mybir.dt.float32
BF16 = mybir.dt.bfloat16
ALU = mybir.AluOpType
AF = mybir.ActivationFunctionType

NDSEM = 24
EPOCH = 20000


class R:
    __slots__ = ("ap", "res", "ps")

    def __init__(self, ap, res, ps=False):
        self.ap = ap
        self.res = res
        self.ps = ps

    def __getitem__(self, k):
        return R(self.ap[k], self.res, self.ps)


class Ins:
    __slots__ = ("eng", "fn", "deps", "needed", "done", "dma", "dsem")

    def __init__(self, eng, fn, dma):
        self.eng = eng
        self.fn = fn
        self.deps = []
        self.needed = False
        self.done = None
        self.dma = dma
        self.dsem = None


class Sched:
    ENGS = ["pe", "act", "dve", "pool", "sp"]

    def __init__(self, nc):
        self.nc = nc
        self.q = {e: [] for e in self.ENGS}
        self.last_w = {}
        self.readers = {}
        self.ndma = 0
        self.ndma_pool = 0
        self.dsem_last = [None] * NDSEM
        self.dcnt = [0] * NDSEM
        self.stack = contextlib.ExitStack()
        self.psn = 0
        self.nt = 0

    def sb(self, shape, dtype, name=None):
        self.nt += 1
        name = name or f"t{self.nt}"
        t = self.stack.enter_context(self.nc.sbuf_tensor("sb_" + name, list(shape), dtype))
        return R(t[:], name)

    def init_psum(self):
        self.banks = []
        for i in range(8):
            t = self.stack.enter_context(self.nc.psum_tensor(f"psb{i}", [128, 512], F32))
            self.banks.append(R(t[:], ("ps", i), True))

    def psum(self):
        b = self.banks[self.psn % 8]
        self.psn += 1
        return b

    def op(self, eng, fn, outs=(), ins=(), dma=False):
        i = Ins(eng, fn, dma)
        reads, writes = [], []
        for r in ins:
            if r is None or r.res is None:
                continue
            (writes if r.ps else reads).extend(r.res if isinstance(r.res, list) else [r.res])
        for r in outs:
            if r is None or r.res is None:
                continue
            writes.extend(r.res if isinstance(r.res, list) else [r.res])
        deps = {}

        def add(d, raw):
            if d is None:
                return
            if d.eng == eng and not d.dma and not dma:
                if eng == "pe":
                    return
            deps[id(d)] = d

        for r in reads:
            add(self.last_w.get(r), True)
        for w in writes:
            add(self.last_w.get(w), True)
            for rd in self.readers.get(w, {}).values():
                add(rd, False)
        if dma:
            half = NDSEM // 2
            if eng == "pool":
                k = half + self.ndma_pool % half
                self.ndma_pool += 1
            else:
                k = self.ndma % half
                self.ndma += 1
            i.dsem = k
            self.dcnt[k] += 16
            i.done = self.dcnt[k]
            if self.dsem_last[k] is not None:
                deps[id(self.dsem_last[k])] = self.dsem_last[k]
            self.dsem_last[k] = i
        i.deps = list(deps.values())
        for d in i.deps:
            d.needed = True
        for r in reads:
            self.readers.setdefault(r, {})[eng if not dma else ("dma", id(i))] = i
        for w in writes:
            self.last_w[w] = i
            self.readers[w] = {}
        self.q[eng].append(i)
        return i

    def mm(self, out, lhsT, rhs, start=True, stop=True):
        return self.op("pe", lambda e: e.matmul(out.ap, lhsT.ap, rhs.ap, start=start, stop=stop),
                       outs=[out], ins=[lhsT, rhs])

    def tr(self, out, in_, ident):
        return self.op("pe", lambda e: e.transpose(out.ap, in_.ap, ident.ap), outs=[out], ins=[in_, ident])

    def act(self, out, in_, func, bias=None, scale=None, eng="act"):
        kw = {}
        ins = [in_]
        if bias is not None:
            if isinstance(bias, R):
                kw["bias"] = bias.ap
                ins.append(bias)
            else:
                kw["bias"] = bias
        if scale is not None:
            if isinstance(scale, R):
                kw["scale"] = scale.ap
                ins.append(scale)
            else:
                kw["scale"] = scale
        return self.op(eng, lambda e: e.activation(out.ap, in_.ap, func, **kw), outs=[out], ins=ins)

    def tt(self, out, in0, in1, op, eng="dve"):
        return self.op(eng, lambda e: e.tensor_tensor(out.ap, in0.ap, in1.ap, op), outs=[out], ins=[in0, in1])

    def ts(self, out, in0, s1, op0, s2=None, op1=None, eng="dve"):
        ins = [in0]
        a1 = s1
        a2 = s2
        if isinstance(s1, R):
            ins.append(s1)
            a1 = s1.ap
        if isinstance(s2, R):
            ins.append(s2)
            a2 = s2.ap
        if op1 is None:
            return self.op(eng, lambda e: e.tensor_scalar(out.ap, in0.ap, a1, None, op0), outs=[out], ins=ins)
        return self.op(eng, lambda e: e.tensor_scalar(out.ap, in0.ap, a1, a2, op0, op1), outs=[out], ins=ins)

    def stt(self, out, in0, scalar, in1, op0, op1):
        ins = [in0, in1]
        a = scalar
        if isinstance(scalar, R):
            ins.append(scalar)
            a = scalar.ap
        return self.op("dve", lambda e: e.scalar_tensor_tensor(out.ap, in0.ap, a, in1.ap, op0, op1),
                       outs=[out], ins=ins)

    def copy(self, out, in_, eng="dve"):
        return self.op(eng, lambda e: e.tensor_copy(out.ap, in_.ap), outs=[out], ins=[in_])

    def recip(self, out, in_):
        return self.op("dve", lambda e: e.reciprocal(out.ap, in_.ap), outs=[out], ins=[in_])

    def memset(self, out, val, eng="pool"):
        return self.op(eng, lambda e: e.memset(out.ap, val), outs=[out])

    def dma(self, out, in_, eng="sp"):
        return self.op(eng, lambda e: e.dma_start(out=out.ap, in_=in_.ap), outs=[out], ins=[in_], dma=True)

    def emit(self):
        nc = self.nc
        st = self.stack
        dsems = [st.enter_context(nc.semaphore(f"dq{k}")) for k in range(NDSEM)]
        dcnt = self.dcnt
        esems = {}
        for e in self.ENGS:
            cnt = 0
            for ins in self.q[e]:
                if ins.dma:
                    ins.done = (dsems[ins.dsem], ins.done)
                elif ins.needed:
                    ep = cnt // EPOCH
                    if (e, ep) not in esems:
                        esems[(e, ep)] = st.enter_context(nc.semaphore(f"s_{e}_{ep}"))
                    cnt += 1
                    ins.done = (esems[(e, ep)], cnt - ep * EPOCH)
        q = self.q
        final_d = [(dsems[k], dcnt[k]) for k in range(NDSEM) if dcnt[k] > 0]

        def run(e, eng):
            waited = {}
            for ins in q[e]:
                for d in ins.deps:
                    sem, val = d.done
                    key = id(sem)
                    if waited.get(key, 0) < val:
                        eng.wait_ge(sem, val)
                        waited[key] = val
                inst = ins.fn(eng)
                if ins.dma:
                    inst.then_inc(ins.done[0], 16)
                elif ins.needed:
                    inst.then_inc(ins.done[0], 1)
            if e == "sp":
                for sem, val in final_d:
                    if waited.get(id(sem), 0) < val:
                        eng.wait_ge(sem, val)

        with nc.Block() as block:
            @block.tensor
            def _(eng):
                run("pe", eng)

            @block.scalar
            def _(eng):
                run("act", eng)

            @block.vector
            def _(eng):
                run("dve", eng)

            @block.gpsimd
            def _(eng):
                run("pool", eng)

            @block.sync
            def _(eng):
                run("sp", eng)


from concourse.bass_utils import run_bass_kernel_spmd

DEPTH = 4
NBLK = 33
SEQ = 4096
ALPHA = 8.0 ** 0.25
NS = 16
GROUPS = [(0, 4), (4, 4), (8, 4), (12, 4), (16, 4), (20, 4), (24, 3), (27, 3), (30, 3)]
NWST = 4
REAL = [0, 2, 4, 6]


def rr(r, pat, **kw):
    return R(r.ap.rearrange(pat, **kw), r.res, r.ps)


def bc(r, shape):
    return R(r.ap.broadcast_to(list(shape)), r.res, r.ps)


def build(depth=DEPTH, groups=GROUPS, do_sample=True, nblk=NBLK):
    nc = bass.Bass("TRN2", target_bir_lowering=False)
    D = {}

    def din(name, shape):
        D[name] = nc.dram_tensor(name, list(shape), F32, kind="ExternalInput").ap()

    def dout(name, shape):
        D[name] = nc.dram_tensor(name, list(shape), F32, kind="ExternalOutput").ap()

    din("xp", [SEQ, 1024]); din("meta", [16, 1024]); din("xs", [NS, 1024])
    din("ck", [4, NS, 128, 128]); din("cv", [4, NS, 128, 128]); din("sg", [4, NS, 4, 64, 128])
    din("wtok", [4, 128, 12288]); din("win_u", [4, 6, 128, 4096]); din("wglr", [4, 128, 128])
    din("wa2", [17, 4, 256]); din("sink", [1, 32])
    din("gng", [128, 4]); din("wpa_u", [4, 2, 128, 2048]); din("wpb_u", [4, 2, 128, 2048]); din("wout_u", [4, 2, 128, 4096])
    din("lnp", [128, 4, 4, 8]); din("wup_u", [4, 8, 128, 4096]); din("wdn_u", [4, 8, 128, 4096])
    din("ident", [128, 128]); din("uneg", [128, 128]); din("unegI", [128, 128]); din("masks", [128, 4, 512])
    din("cs", [128, NBLK, 16]); din("css", [NS, 16])
    dout("y", [SEQ, 1024]); dout("ys", [NS, 1024]); dout("pk", [4, 128, 128]); dout("pv", [4, 128, 128])
    dout("pst", [4, 4, 64, 128]); dout("sk", [4, NS, 128, 128]); dout("sv", [4, NS, 128, 128])
    dout("sst", [4, NS, 4, 64, 128])
    DR = {k: R(v, None) for k, v in D.items()}

    S = Sched(nc)
    import os as _os
    _target = _os.environ.get("KSTOP", "")

    class _Stop(Exception):
        pass

    def stage(name):
        if _target and name == _target:
            raise _Stop()
    with S.stack:
        S.init_psum()
        S.nrot = 6
        MUL, ADD, SUB = ALU.mult, ALU.add, ALU.subtract

        def psum():
            b = S.banks[S.psn % S.nrot]
            S.psn += 1
            return b
        bank7 = S.banks[7]
        bank6 = S.banks[6]

        ident = S.sb([128, 128], F32, "ident"); S.dma(ident, DR["ident"])
        uneg = S.sb([128, 128], F32, "uneg"); S.dma(uneg, DR["uneg"])
        unegI = S.sb([NS, NS], F32, "unegI"); S.dma(unegI, DR["unegI"][0:NS, 0:NS])
        masks = S.sb([128, 4, 512], BF16, "masks"); S.dma(masks, DR["masks"], eng="pool")
        mcur4, mprev4, m04, mprev1 = masks[:, 0, :], masks[:, 1, :], masks[:, 2, :], masks[:, 3, :]
        cs = S.sb([128, NBLK, 16], F32, "cs"); S.dma(cs, DR["cs"])
        css = S.sb([NS, 16], F32, "css"); S.dma(css, DR["css"])
        ones_bf = S.sb([128, 128], BF16, "ones_bf"); S.memset(ones_bf, 1.0)
        onesm_bf = S.sb([128, 128], BF16, "onesm_bf"); S.memset(onesm_bf, 1.0 / 1024.0)
        cst = S.sb([128, 4], F32, "cst")
        S.memset(cst[:, 0:1], 1.0); S.memset(cst[:, 1:2], 1e-5); S.memset(cst[:, 2:3], 1e-6)
        one_col, eps_ln, eps_rms = cst[:, 0:1], cst[:, 1:2], cst[:, 2:3]
        lnp = S.sb([128, 4, 4, 8], F32, "lnp"); S.dma(lnp, DR["lnp"])
        gng = S.sb([128, 4], F32, "gng"); S.dma(gng, DR["gng"])
        wa2 = S.sb([17, 4, 256], F32, "wa2"); S.dma(wa2, DR["wa2"])
        es_all = S.sb([128, 32], F32, "es_all")
        S.dma(es_all, R(D["sink"].broadcast_to([128, 32]), None))
        S.act(es_all, es_all, AF.Exp)

        xres = S.sb([128, 8, 512], F32, "xres")
        xres.res = [("xres", f) for f in range(8)]
        XR = [R(xres.ap[:, f], [("xres", f)]) for f in range(8)]
        xbf = S.sb([128, 8, 512], BF16, "xbf")
        hid_raw = S.sb([128, 16384], BF16, "hid")
        hidT = rr(hid_raw, "p (a b) -> p a b", a=32)
        gqk = R(hid_raw.ap[:, 0:4096].bitcast(F32).rearrange("p (a b) -> p a b", a=4), "hid")
        grs = rr(hid_raw[:, 4096:6144], "p (a b) -> p a b", a=4)
        attT = rr(hid_raw[:, 6144:8192], "p (a b) -> p a b", a=4)
        glaT = rr(hid_raw[:, 8192:10240], "p (a b) -> p a b", a=4)
        glrT = S.sb([17, 512], F32, "glrT"); S.memset(glrT, 1.0)
        mT = S.sb([128, 8, 512], BF16, "mT")
        wtok = S.sb([128, 8, 1536], BF16, "wtok")
        wst = [S.sb([128, 4096], BF16, f"wst{i}") for i in range(NWST)]
        wsn = [0]
        kTb = [[S.sb([128, 128], BF16, f"kT{l}_{s}") for s in range(2)] for l in range(depth)]
        vdb = [[S.sb([128, 2, 2, 64], BF16, f"vd{l}_{s}") for s in range(2)] for l in range(depth)]
        Sf = [S.sb([128, 2, 128], F32, f"Sf{l}") for l in range(depth)]
        Sbf = [S.sb([128, 2, 128], BF16, f"Sbf{l}") for l in range(depth)]
        for l in range(depth):
            S.memset(Sf[l], 0.0); S.memset(Sbf[l], 0.0)
        xin = S.sb([128, 1024], F32, "xin")
        ysb = xin
        TK = [dict(qtok=S.sb([128, 512], F32, f"qtok{p}"), ktok=S.sb([128, 128], F32, f"ktok{p}"),
                   vtok=S.sb([128, 128], F32, f"vtok{p}"), gktok=S.sb([128, 256], F32, f"gktok{p}"),
                   gvtok=S.sb([128, 512], BF16, f"gvtok{p}")) for p in range(2)]
        qtok, ktok, vtok, gktok, gvtok = (TK[0][k] for k in ("qtok", "ktok", "vtok", "gktok", "gvtok"))
        rtA = S.sb([128, 64], F32, "rtA"); rtB = S.sb([128, 64], F32, "rtB")
        qT_bf = S.sb([128, 512], BF16, "qT_bf")
        Pc = [S.sb([128, 512], BF16, f"Pc{g}") for g in range(2)]
        Pp = [S.sb([128, 512], BF16, f"Pp{g}") for g in range(2)]
        FP = [S.sb([128, 512], F32, f"F{i}") for i in range(7)]
        dtot, rec, lnv, rstd, t1 = FP[0], FP[1], FP[2], FP[3], FP[4]
        e1, sp, enb = FP[5][:, 0:256], FP[5][:, 256:512], FP[6][:, 0:256]
        sga, sgb, m1, m2 = FP[0], FP[1], FP[4], FP[5]
        msq, var, nmr, tmpA = FP[0], FP[1], FP[4], [FP[5], FP[6]]
        gvtok_f = FP[4][0:NS, :]
        ebp = S.sb([128, 2, 128], F32, "ebp"); ebn = S.sb([128, 2, 128], F32, "ebn")
        qtl = S.sb([128, 2, 128], BF16, "qtl"); ktl = S.sb([128, 2, 128], BF16, "ktl")
        ktokb = S.sb([128, 256], BF16, "ktokb")
        Am = S.sb([128, 512], BF16, "Am")
        osq = S.sb([128, 512], BF16, "osq")
        rbb = [S.sb([128, 512], BF16, f"rb{i}") for i in range(2)]
        rsb = [S.sb([128, 512], BF16, f"rsq{i}") for i in range(2)]
        m1buf = R(hid_raw.ap[:, 10240:14336].bitcast(F32).rearrange("p (a b) -> p a b", a=4), "m1buf")
        relu_t = [S.sb([128, 512], BF16, f"relu{i}") for i in range(2)]

        def wstream(src, KC, w):
            buf = wst[wsn[0] % NWST]
            wsn[0] += 1
            S.dma(buf[:, 0:KC * w], R(src, None), eng="pool")
            return R(buf.ap[:, 0:KC * w].rearrange("p (k n) -> p k n", k=KC), buf.res)

        def linear_units(units, KC, w, inT, T, evac, pre=()):
            ci = 0
            pre = list(pre)
            for ui, src in enumerate(units):
                wt = pre[ui] if ui < len(pre) else wstream(src, KC, w)
                for t0 in range(0, w, 128):
                    m = min(128, w - t0)
                    ps = psum()
                    for kc in range(KC):
                        S.mm(ps[0:m, 0:T], wt[:, kc, t0:t0 + m], inT[:, kc, 0:T], start=(kc == 0), stop=(kc == KC - 1))
                    evac(ci, ps[0:m, 0:T])
                    ci += 1

        def load_wtok(l):
            S.dma(rr(wtok, "p k n -> p (k n)"), R(D["wtok"][l], None), eng="pool")

        def tok_proj(xblk, M, cosR, sinR, tk=None):
            tk = tk or TK[0]
            qtok, ktok, vtok, gktok = tk["qtok"], tk["ktok"], tk["vtok"], tk["gktok"]
            ps_q, ps_kv, ps_gv = psum(), psum(), psum()
            for (ps, c0) in ((ps_q, 0), (ps_kv, 512), (ps_gv, 1024)):
                for kc in range(8):
                    S.mm(ps[0:M, 0:512], xblk[:, kc, :], wtok[:, kc, c0:c0 + 512], start=(kc == 0), stop=(kc == 7))
            qin = rr(ps_q[0:M, 0:512], "p (g hh d) -> p g hh d", g=2, hh=4)
            qo = rr(qtok[0:M, :], "p (hh g d) -> p g hh d", g=2, hh=4)
            S.act(qo, qin, AF.Copy)
            cosb = bc(R(cosR.ap.unsqueeze(1).unsqueeze(1), cosR.res), [M, 2, 4, 8])
            sinb = bc(R(sinR.ap.unsqueeze(1).unsqueeze(1), sinR.res), [M, 2, 4, 8])
            tA = rr(rtA[0:M, :], "p (g hh d) -> p g hh d", g=2, hh=4)
            tB = rr(rtB[0:M, :], "p (g hh d) -> p g hh d", g=2, hh=4)
            S.tt(tA, qin[:, :, :, 0:8], cosb, MUL); S.tt(tB, qin[:, :, :, 8:16], sinb, MUL)
            S.tt(qo[:, :, :, 0:8], tA, tB, SUB)
            S.tt(tA, qin[:, :, :, 8:16], cosb, MUL); S.tt(tB, qin[:, :, :, 0:8], sinb, MUL)
            S.tt(qo[:, :, :, 8:16], tA, tB, ADD)
            kin = rr(ps_kv[0:M, 0:128], "p (g d) -> p g d", g=2)
            ko = rr(ktok[0:M, :], "p (g d) -> p g d", g=2)
            S.act(ko, kin, AF.Copy)
            cos2 = bc(R(cosR.ap.unsqueeze(1), cosR.res), [M, 2, 8])
            sin2 = bc(R(sinR.ap.unsqueeze(1), sinR.res), [M, 2, 8])
            tA2 = rr(rtA[0:M, 0:16], "p (g d) -> p g d", g=2)
            tB2 = rr(rtB[0:M, 0:16], "p (g d) -> p g d", g=2)
            S.tt(tA2, kin[:, :, 0:8], cos2, MUL); S.tt(tB2, kin[:, :, 8:16], sin2, MUL)
            S.tt(ko[:, :, 0:8], tA2, tB2, SUB)
            S.tt(tA2, kin[:, :, 8:16], cos2, MUL); S.tt(tB2, kin[:, :, 0:8], sin2, MUL)
            S.tt(ko[:, :, 8:16], tA2, tB2, ADD)
            S.act(vtok[0:M, :], ps_kv[0:M, 128:256], AF.Copy)
            S.copy(gktok[0:M, :], ps_kv[0:M, 256:512])
            return ps_gv

        def gla_finish(l, o_ps, T, c0):
            n = 4 * T
            S.act(osq[:, 0:n], o_ps, AF.Square)
            ss = psum()
            S.mm(ss[:, 0:n], ones_bf, osq[:, 0:n])
            S.act(lnv[:, 0:n], ss[:, 0:n], AF.Ln, scale=1.0 / 128.0, bias=eps_rms)
            S.act(rstd[:, 0:n], lnv[:, 0:n], AF.Exp, scale=-0.5)
            S.tt(t1[:, 0:n], o_ps, rstd[:, 0:n], MUL)
            S.stt(glaT[:, :, c0:c0 + T], rr(t1[:, 0:n], "p (h t) -> p h t", h=4), gng[:, l:l + 1],
                  grs[:, :, c0:c0 + T], MUL, MUL)

        stat_pending = []

        def stat_mm(T, f):
            S.mm(bank6[:, 0:T], onesm_bf, rbb[f % 2][:, 0:T], start=(f == 0), stop=(f == 7))
            S.mm(bank7[:, 0:T], onesm_bf, rsb[f % 2][:, 0:T], start=(f == 0), stop=(f == 7))

        def stat_push(T, f):
            S.act(rbb[f % 2][:, 0:T], XR[f][:, 0:T], AF.Copy)
            S.act(rsb[f % 2][:, 0:T], XR[f][:, 0:T], AF.Square)
            stat_pending.append(f)
            if len(stat_pending) > 1:
                stat_mm(T, stat_pending.pop(0))

        def ln_fm(T, l, which, zero_pad):
            while stat_pending:
                stat_mm(T, stat_pending.pop(0))
            mean_ps, ex2_ps = bank6, bank7
            S.act(msq[:, 0:T], mean_ps[:, 0:T], AF.Square)
            S.tt(var[:, 0:T], ex2_ps[:, 0:T], msq[:, 0:T], SUB)
            S.act(lnv[:, 0:T], var[:, 0:T], AF.Ln, bias=eps_ln)
            S.act(rstd[:, 0:T], lnv[:, 0:T], AF.Exp, scale=-0.5)
            S.stt(nmr[:, 0:T], mean_ps[:, 0:T], -1.0, rstd[:, 0:T], MUL, MUL)
            for f in range(8):
                ta = tmpA[f % 2]
                S.tt(ta[:, 0:T], XR[f][:, 0:T], rstd[:, 0:T], MUL)
                S.tt(ta[:, 0:T], ta[:, 0:T], nmr[:, 0:T], ADD)
                gcol = lnp[:, l, 2 * which, f:f + 1]
                bcol = lnp[:, l, 2 * which + 1, f:f + 1]
                S.act(xbf[:, f, 0:T], ta[:, 0:T], AF.Identity, scale=gcol, bias=bcol)
                S.ts(XR[f][:, 0:T], ta[:, 0:T], gcol, MUL, bcol, ADD, eng="pool")
            if zero_pad:
                S.memset(xres[:, :, 0:112], 0.0, eng="dve")
                S.memset(xbf[:, :, 0:112], 0.0, eng="dve")

        pref = {}

        def dense_tail(l, T, zero_pad, nxt_l=None):
            for fq in range(2):
                wga = wstream(D["win_u"][l, 2 + fq], 8, 512)
                wpa = wstream(D["wpa_u"][l, fq], 4, 512)
                for ft in range(4):
                    cs_ = slice(ft * 128, ft * 128 + 128)
                    pga, ppa = psum(), psum()
                    for kc in range(8):
                        S.mm(pga[:, 0:T], wga[:, kc, cs_], xbf[:, kc, 0:T], start=(kc == 0), stop=(kc == 7))
                    for kc in range(4):
                        S.mm(ppa[:, 0:T], wpa[:, kc, cs_], attT[:, kc, 0:T], start=(kc == 0), stop=(kc == 3))
                    S.act(sga[:, 0:T], pga[:, 0:T], AF.Sigmoid)
                    S.tt(m1buf[:, ft, 0:T], ppa[:, 0:T], sga[:, 0:T], MUL)
                wgb = wstream(D["win_u"][l, 4 + fq], 8, 512)
                wpb = wstream(D["wpb_u"][l, fq], 4, 512)
                for ft in range(4):
                    f = fq * 4 + ft
                    cs_ = slice(ft * 128, ft * 128 + 128)
                    pgb, ppb = psum(), psum()
                    for kc in range(8):
                        S.mm(pgb[:, 0:T], wgb[:, kc, cs_], xbf[:, kc, 0:T], start=(kc == 0), stop=(kc == 7))
                    for kc in range(4):
                        S.mm(ppb[:, 0:T], wpb[:, kc, cs_], glaT[:, kc, 0:T], start=(kc == 0), stop=(kc == 3))
                    S.act(sgb[:, 0:T], pgb[:, 0:T], AF.Sigmoid)
                    S.tt(m2[:, 0:T], ppb[:, 0:T], sgb[:, 0:T], MUL)
                    S.tt(mT[:, f, 0:T], m1buf[:, ft, 0:T], m2[:, 0:T], ADD)

            def ev_res(ci, ps):
                S.stt(XR[ci][:, 0:T], XR[ci][:, 0:T], ALPHA, ps, MUL, ADD)
                stat_push(T, ci)
            linear_units([D["wout_u"][l, u] for u in range(2)], 8, 512, mT, T, ev_res)
            pre_up = [wstream(D["wup_u"][l, u], 8, 512) for u in range(2)]
            ln_fm(T, l, 0, False)

            def ev_up(ci, ps):
                rt = relu_t[ci % 2]
                S.act(rt[:, 0:T], ps, AF.Relu)
                S.tt(hidT[:, ci, 0:T], rt[:, 0:T], rt[:, 0:T], MUL)
            linear_units([D["wup_u"][l, u] for u in range(8)], 8, 512, xbf, T, ev_up, pre=pre_up)
            for cg in range(2):
                acc = [psum() for _ in range(4)]
                for kb in range(4):
                    wt = wstream(D["wdn_u"][l, cg * 4 + kb], 8, 512)
                    for ft in range(4):
                        for kc in range(8):
                            S.mm(acc[ft][:, 0:T], wt[:, kc, ft * 128:(ft + 1) * 128], hidT[:, kb * 8 + kc, 0:T],
                                 start=(kb == 0 and kc == 0), stop=(kb == 3 and kc == 7))
                for ft in range(4):
                    ev_res(cg * 4 + ft, acc[ft][:, 0:T])
            if nxt_l is not None:
                load_wtok(nxt_l)
                pref["gqk"] = (nxt_l, wstream(D["win_u"][nxt_l, 0], 8, 512))
            ln_fm(T, l, 1, zero_pad)

        def phase_a(l, T):
            def ev_gqk(ci, ps):
                S.copy(gqk[:, ci, 0:T], ps)
            pre = []
            if pref.get("gqk") and pref["gqk"][0] == l:
                pre = [pref.pop("gqk")[1]]
            linear_units([D["win_u"][l, 0]], 8, 512, xbf, T, ev_gqk, pre=pre)

            def ev_glr(ci, ps):
                S.copy(glrT[0:16, 0:T], ps)
            linear_units([D["wglr"][l]], 8, 16, xbf, T, ev_glr)

            def ev_gr(ci, ps):
                S.act(grs[:, ci, 0:T], ps, AF.Silu)
            linear_units([D["win_u"][l, 1]], 8, 512, xbf, T, ev_gr)

        def load_x(b0, nb):
            for j in range(nb):
                n = b0 + j
                if n == 0:
                    S.memset(xin, 0.0, eng="dve")
                    S.dma(xin[112:128, :], DR["meta"])
                else:
                    S.dma(xin, DR["xp"][(n - 1) * 128:n * 128, :])
                for half in range(2):
                    ps = psum()
                    for ff in range(4):
                        f = half * 4 + ff
                        S.tr(ps[:, ff * 128:(ff + 1) * 128], xin[:, f * 128:(f + 1) * 128], ident)
                    pv = rr(ps[:, 0:512], "p (f t) -> p f t", f=4)
                    S.copy(xres[:, half * 4:half * 4 + 4, j * 128:(j + 1) * 128], pv)
                    S.act(xbf[:, half * 4:half * 4 + 4, j * 128:(j + 1) * 128], pv, AF.Copy)

        def store_y(b0, nb):
            for j in range(nb):
                n = b0 + j
                if n == 0:
                    continue
                for half in range(2):
                    ps = psum()
                    for ff in range(4):
                        f = half * 4 + ff
                        S.tr(ps[:, ff * 128:(ff + 1) * 128], xres[:, f, j * 128:(j + 1) * 128], ident)
                    if half == 0:
                        S.copy(ysb[:, 0:512], ps[:, 0:512])
                    else:
                        S.act(ysb[:, 512:1024], ps[:, 0:512], AF.Copy)
                S.dma(DR["y"][(n - 1) * 128:n * 128, :], ysb)

        hooks = []

        def hook():
            if hooks:
                hooks.pop(0)()

        PcB = [R(hid_raw.ap[:, 14336 + g * 512:14336 + (g + 1) * 512], ("PcB", g)) for g in range(2)]
        PpB = [R(hid_raw.ap[:, 15360 + g * 512:15360 + (g + 1) * 512], ("PpB", g)) for g in range(2)]
        BT = [dict(qT_bf=qT_bf, Pc=Pc, Pp=Pp, ebp=ebp, ebn=ebn, qtl=qtl, ktl=ktl, ktokb=ktokb, Am=Am),
              dict(qT_bf=S.sb([128, 512], BF16, "qT_bfB"), Pc=PcB, Pp=PpB,
                   ebp=S.sb([128, 2, 128], F32, "ebpB"), ebn=S.sb([128, 2, 128], F32, "ebnB"),
                   qtl=S.sb([128, 2, 128], BF16, "qtlB"), ktl=S.sb([128, 2, 128], BF16, "ktlB"),
                   ktokb=S.sb([128, 256], BF16, "ktokbB"), Am=S.sb([128, 512], BF16, "AmB"))]

        def tok_stage(l, n, j):
            tk = TK[j % 2]
            csl = slice(j * 128, j * 128 + 128)
            ps_gv = tok_proj(xbf[:, :, csl], 128, cs[:, n, 0:8], cs[:, n, 8:16], tk)
            S.act(tk["gvtok"], ps_gv[:, 0:512], AF.Copy)
            if n == nblk - 1:
                S.dma(DR["pk"][l], tk["ktok"])
                S.dma(DR["pv"][l], tk["vtok"])

        def st_T(l, n, j):
            tk, bt = TK[j % 2], BT[j % 2]
            qT_ps = psum()
            for hh in range(4):
                S.tr(qT_ps[:, hh * 128:(hh + 1) * 128], tk["qtok"][:, hh * 128:(hh + 1) * 128], ident)
            S.copy(bt["qT_bf"], qT_ps[:, 0:512])
            kT_ps = psum()
            S.tr(kT_ps[:, 0:128], tk["ktok"], ident)
            S.act(kTb[l][n % 2], kT_ps[:, 0:128], AF.Copy)

        def st_vd(l, n, j):
            tk = TK[j % 2]
            vd = vdb[l][n % 2]
            v3 = rr(tk["vtok"], "p (g d) -> p g d", g=2)
            S.copy(vd[:, :, 0, :], v3); S.act(vd[:, :, 1, :], v3, AF.Copy)

        def st_S(l, n, j):
            bt = BT[j % 2]
            kTc, kTp = kTb[l][n % 2], kTb[l][(n - 1) % 2]
            maskc = m04 if n == 0 else mcur4
            maskp = mprev1 if n == 1 else mprev4
            for g in range(2):
                rs_ = slice(g * 64, g * 64 + 64)
                Sc = psum()
                S.mm(Sc[:, 0:512], kTc[rs_, :], bt["qT_bf"][rs_, :])
                S.act(bt["Pc"][g], Sc[:, 0:512], AF.Exp, scale=0.125)
                S.tt(bt["Pc"][g], bt["Pc"][g], maskc, MUL)
                if n > 0:
                    Sp_ = psum()
                    S.mm(Sp_[:, 0:512], kTp[rs_, :], bt["qT_bf"][rs_, :])
                    S.act(bt["Pp"][g], Sp_[:, 0:512], AF.Exp, scale=0.125)
                    S.tt(bt["Pp"][g], bt["Pp"][g], maskp, MUL)

        def st_PV(l, n, j, g):
            bt = BT[j % 2]
            csl = slice(j * 128, j * 128 + 128)
            vd, vdp = vdb[l][n % 2], vdb[l][(n - 1) % 2]
            Pc_, Pp_ = bt["Pc"], bt["Pp"]
            O, Dn = psum(), psum()
            vc = rr(vd[:, g], "p a d -> p (a d)")
            if n > 0:
                vp = rr(vdp[:, g], "p a d -> p (a d)")
                S.mm(O[:, 0:512], vp, Pp_[g], start=True, stop=False)
                S.mm(O[:, 0:512], vc, Pc_[g], start=False, stop=True)
                S.mm(Dn[:, 0:512], ones_bf, Pp_[g], start=True, stop=False)
                S.mm(Dn[:, 0:512], ones_bf, Pc_[g], start=False, stop=True)
            else:
                S.mm(O[:, 0:512], vc, Pc_[g])
                S.mm(Dn[:, 0:512], ones_bf, Pc_[g])
            for hh in range(4):
                hc = slice(hh * 128, (hh + 1) * 128)
                S.act(dtot[:, hc], Dn[:, hc], AF.Ln, bias=es_all[:, l * 8 + g * 4 + hh:l * 8 + g * 4 + hh + 1])
            S.act(rec, dtot, AF.Exp, scale=-1.0)
            for hh in range(4):
                h = 4 * g + hh
                r0 = (h % 2) * 64
                S.tt(attT[r0:r0 + 64, h // 2, csl], O[r0:r0 + 64, hh * 128:(hh + 1) * 128],
                     rec[r0:r0 + 64, hh * 128:(hh + 1) * 128], MUL)

        def st_GA(l, n, j):
            tk, bt = TK[j % 2], BT[j % 2]
            csl = slice(j * 128, j * 128 + 128)
            z = psum()
            S.mm(z[:, 0:256], glrT[0:17, csl], wa2[0:17, l, :])
            S.act(e1, z[:, 0:256], AF.Exp, scale=-1.0)
            S.act(sp, e1, AF.Ln, bias=one_col)
            bt_ps = psum()
            S.mm(bt_ps[:, 0:256], uneg, sp)
            bT = psum()
            for i in range(2):
                S.mm(bT[:, i * 128:(i + 1) * 128], sp[:, i * 128:(i + 1) * 128], uneg)
            bT3 = rr(bT[:, 0:256], "p (i t) -> p i t", i=2)
            S.act(bt["ebp"], bT3, AF.Exp)
            S.act(bt["ebn"], bT3, AF.Exp, scale=-1.0)
            S.act(enb, bt_ps[:, 0:256], AF.Exp, scale=-1.0)
            S.stt(bt["qtl"], gqk[:, 0:2, csl], 0.125, bt["ebp"], MUL, MUL)
            S.tt(bt["ktl"], gqk[:, 2:4, csl], bt["ebn"], MUL)
            S.tt(bt["ktokb"], tk["gktok"], enb, MUL)

        def st_GB1(l, n, j):
            bt = BT[j % 2]
            qtl_, ktl_, Am_ = bt["qtl"], bt["ktl"], bt["Am"]
            A2 = [psum(), psum()]
            for h in range(4):
                i, r0 = h // 2, (h % 2) * 64
                S.mm(A2[h % 2][:, h * 128:(h + 1) * 128], ktl_[r0:r0 + 64, i, :], qtl_[r0:r0 + 64, i, :])
            Am4 = rr(Am_, "p (a b c) -> p a b c", a=2, b=2)
            mk4 = rr(mcur4, "p (a b c) -> p a b c", a=2, b=2)
            for par in range(2):
                S.tt(Am4[:, :, par, :], rr(A2[par][:, 0:512], "p (a b c) -> p a b c", a=2, b=2)[:, :, par, :],
                     mk4[:, :, par, :], MUL)

        def st_GB2(l, n, j):
            tk, bt = TK[j % 2], BT[j % 2]
            c0 = j * 128
            gvtok_, qtl_, Am_, ktokb_, ebp_ = tk["gvtok"], bt["qtl"], bt["Am"], bt["ktokb"], bt["ebp"]
            o = psum()
            for h in range(4):
                i, r0 = h // 2, (h % 2) * 64
                hs = slice(h * 128, (h + 1) * 128)
                S.mm(o[:, hs], gvtok_[:, hs], Am_[:, hs], start=True, stop=False)
                S.mm(o[:, hs], Sbf[l][r0:r0 + 64, i, :], qtl_[r0:r0 + 64, i, :], start=False, stop=True)
            for i in range(2):
                U = psum()
                S.mm(U[:, 0:256], ktokb_[:, i * 128:(i + 1) * 128], gvtok_[:, i * 256:(i + 1) * 256])
                for hh in range(2):
                    r0 = hh * 64
                    S.tt(Sf[l][r0:r0 + 64, i, :], U[r0:r0 + 64, hh * 128:(hh + 1) * 128], Sf[l][r0:r0 + 64, i, :], ADD)
                S.ts(Sf[l][:, i, :], Sf[l][:, i, :], ebp_[:, i, 127:128], MUL)
                S.act(Sbf[l][:, i, :], Sf[l][:, i, :], AF.Copy)
            gla_finish(l, o[:, 0:512], 128, c0)
            if n == nblk - 1:
                dst = D["pst"][l].rearrange("(i hh) dk dv -> (hh dk) i dv", hh=2)
                S.dma(R(dst, None), Sf[l])

        def block_phase(l, b0, nb):
            tok_stage(l, b0, 0)
            if nb > 1:
                tok_stage(l, b0 + 1, 1)
            st_T(l, b0, 0); st_vd(l, b0, 0); st_S(l, b0, 0); hook(); st_GA(l, b0, 0); st_GB1(l, b0, 0); hook()
            for j in range(nb):
                n = b0 + j
                nxt = j + 1 < nb
                st_PV(l, n, j, 0)
                if nxt:
                    st_T(l, n + 1, j + 1)
                hook()
                st_PV(l, n, j, 1)
                if nxt:
                    st_vd(l, n + 1, j + 1)
                    st_S(l, n + 1, j + 1)
                hook()
                st_GB2(l, n, j)
                hook()
                if nxt:
                    st_GA(l, n + 1, j + 1)
                    hook()
                    st_GB1(l, n + 1, j + 1)
                if j + 2 < nb:
                    tok_stage(l, n + 2, j + 2)
                hook()

        if do_sample:
            Ss = S.sb([128, 4, 2, 128], F32, "Ss")
            kwin = R(hid_raw.ap[:, 10240:14336].bitcast(F32).rearrange("p (a b c) -> p a b c", a=2, b=8), "m1buf")
            TKS = dict(qtok=FP[2][0:NS, :], ktok=S.sb([NS, 128], F32, "ktok_s"), vtok=S.sb([NS, 128], F32, "vtok_s"),
                       gktok=S.sb([NS, 256], F32, "gktok_s"))
            gvs = S.sb([NS, 512], F32, "gvs")
            KwT = [S.sb([128, 128], BF16, f"KwT{i}") for i in range(2)]
            Vwd = [S.sb([128, 2, 2, 64], BF16, f"Vwd{i}") for i in range(2)]
            qTs = S.sb([128, 4, NS], BF16, "qTs")
            Ps = S.sb([128, 128], BF16, "Ps")
            ebs = S.sb([128, 2, NS], F32, "ebs")
            qsT = S.sb([128, 2, NS], F32, "qsT")
            kmask = [FP[6][0:NS, 0:256], FP[6][0:NS, 256:512]]

        def sample_parts(l, sc0=0):
            M = NS
            tk = TKS
            qtok, ktok, vtok, gktok = tk["qtok"], tk["ktok"], tk["vtok"], tk["gktok"]
            gvtok_f = gvs

            skR = R(D["sk"][l], ("sk", l))
            svR = R(D["sv"][l], ("sv", l))

            def stA(half):
                h8 = half * 8
                for a_, (cin, outR) in enumerate(((D["ck"], skR), (D["cv"], svR))):
                    S.dma(kwin[0:127, a_], R(cin[l, h8:h8 + 8, 1:128, :].rearrange("b r c -> r b c"), None))
                    S.dma(kwin[127:128, a_], R(outR.ap[h8:h8 + 8, 127:128, :].rearrange("b r c -> r b c"), outR.res))
                    S.dma(R(outR.ap[h8:h8 + 8, 0:127, :].rearrange("b r c -> r b c"), None), kwin[0:127, a_])

            def stB(bi):
                Kw, Vw, KT, Vd = kwin[:, 0, bi % 8, :], kwin[:, 1, bi % 8, :], KwT[bi % 2], Vwd[bi % 2]
                kt_ps = psum()
                S.tr(kt_ps[:, 0:128], Kw, ident)
                S.act(KT, kt_ps[:, 0:128], AF.Copy)
                v3 = rr(Vw, "p (g d) -> p g d", g=2)
                S.copy(Vd[:, :, 0, :], v3); S.act(Vd[:, :, 1, :], v3, AF.Copy)
                for g in range(2):
                    rs_ = slice(g * 64, g * 64 + 64)
                    sc = psum()
                    S.mm(sc[:, 0:4], KT[rs_, :], qTs[rs_, :, bi])
                    S.act(Ps[:, bi * 8 + g * 4:bi * 8 + g * 4 + 4], sc[:, 0:4], AF.Exp, scale=0.125)
                for g in range(2):
                    c = bi * 8 + g * 4
                    S.mm(bank7[:, c:c + 4], rr(Vd[:, g], "p a d -> p (a d)"), Ps[:, c:c + 4])

            def p_pre():
                ps_gv = tok_proj(xbf[:, :, sc0:sc0 + M], M, css[:, 0:8], css[:, 8:16], tk)
                S.act(gvtok_f, ps_gv[0:M, 0:512], AF.Copy)
                qT_ps = psum()
                for hh in range(4):
                    S.tr(qT_ps[:, hh * M:(hh + 1) * M], qtok[0:M, hh * 128:(hh + 1) * 128], ident[0:M, 0:M])
                S.copy(qTs, rr(qT_ps[:, 0:4 * M], "p (h b) -> p h b", h=4))
                S.dma(skR[:, 127, :], ktok[0:M, :])
                S.dma(svR[:, 127, :], vtok[0:M, :])
                stA(0)

            def p_bi(b0_):
                def f():
                    if b0_ == 8:
                        stA(1)
                    for bi in (b0_, b0_ + 1):
                        stB(bi)
                return f

            def p_post():
                Dn = psum()
                S.mm(Dn[:, 0:128], ones_bf, Ps)
                essb = bc(R(es_all.ap[:, l * 8:(l + 1) * 8].unsqueeze(1), es_all.res), [128, NS, 8])
                S.tt(rr(dtot[:, 0:128], "p (b h) -> p b h", h=8), rr(Dn[:, 0:128], "p (b h) -> p b h", h=8), essb, ADD)
                S.recip(rec[:, 0:128], dtot[:, 0:128])
                O3 = rr(bank7[:, 0:128], "p (b h) -> p h b", h=8)
                r3 = rr(rec[:, 0:128], "p (b h) -> p h b", h=8)
                for h in range(8):
                    r0 = (h % 2) * 64
                    S.tt(attT[r0:r0 + 64, h // 2, sc0:sc0 + M], O3[r0:r0 + 64, h, :], r3[r0:r0 + 64, h, :], MUL)
                z = psum()
                S.mm(z[0:M, 0:256], glrT[0:17, sc0:sc0 + M], wa2[0:17, l, :])
                S.act(e1[0:M, :], z[0:M, 0:256], AF.Exp, scale=-1.0)
                S.act(sp[0:M, :], e1[0:M, :], AF.Ln, bias=one_col[0:M, :])
                bT = psum()
                for i in range(2):
                    S.mm(bT[:, i * M:(i + 1) * M], sp[0:M, i * 128:(i + 1) * 128], unegI[0:M, 0:M])
                S.act(ebs, rr(bT[:, 0:2 * M], "p (i b) -> p i b", i=2), AF.Exp)
                S.ts(qsT, gqk[:, 0:2, sc0:sc0 + M], 0.125, MUL)

            def p_gla(b4):
                def f():
                    src = D["sg"][l, b4:b4 + 4].rearrange("b (i hh) dk dv -> (hh dk) b i dv", hh=2)
                    S.dma(Ss, R(src, None))
                    for bb in range(4):
                        b = b4 + bb
                        km = kmask[b % 2]
                        S.ts(km, gktok[0:M, :], ident[0:M, b:b + 1], MUL)
                        for i in range(2):
                            U = psum()
                            S.mm(U[:, 0:256], km[:, i * 128:(i + 1) * 128], gvtok_f[:, i * 256:(i + 1) * 256])
                            for hh in range(2):
                                r0 = hh * 64
                                S.stt(Ss[r0:r0 + 64, bb, i, :], Ss[r0:r0 + 64, bb, i, :], ebs[r0:r0 + 64, i, b:b + 1],
                                      U[r0:r0 + 64, hh * 128:(hh + 1) * 128], MUL, ADD)
                            for hh in range(2):
                                r0 = hh * 64
                                h = 2 * i + hh
                                ob = bank7 if hh == 0 else bank6
                                S.mm(ob[:, 256 + h * M + b:256 + h * M + b + 1], Ss[r0:r0 + 64, bb, i, :],
                                     qsT[r0:r0 + 64, i, b:b + 1])
                    dst = D["sst"][l, b4:b4 + 4].rearrange("b (i hh) dk dv -> (hh dk) b i dv", hh=2)
                    S.dma(R(dst, None), Ss)
                return f

            def p_fin():
                o_sb = FP[6][:, 0:4 * M]
                o4 = rr(o_sb, "p (i hh b) -> p i hh b", i=2, hh=2)
                for hh, ob in ((0, bank7), (1, bank6)):
                    S.copy(o4[:, :, hh, :], rr(ob[:, 256:256 + 4 * M], "p (i hh b) -> p i hh b", i=2, hh=2)[:, :, hh, :])
                gla_finish(l, o_sb, M, sc0)

            return [p_pre] + [p_bi(b0_) for b0_ in range(0, NS, 2)] + [p_post] + [p_gla(b4) for b4 in range(0, NS, 4)] + [p_fin]

        def load_sample(sc0):
            M = NS
            S.dma(xin[0:M, :], DR["xs"])
            for half in range(2):
                ps = psum()
                for ff in range(4):
                    f = half * 4 + ff
                    S.tr(ps[:, ff * M:(ff + 1) * M], xin[0:M, f * 128:(f + 1) * 128], ident[0:M, 0:M])
                pv = rr(ps[:, 0:4 * M], "p (f t) -> p f t", f=4)
                S.copy(xres[:, half * 4:half * 4 + 4, sc0:sc0 + M], pv)
                S.act(xbf[:, half * 4:half * 4 + 4, sc0:sc0 + M], pv, AF.Copy)

        def store_sample(sc0):
            M = NS
            for half in range(2):
                ps = psum()
                for ff in range(4):
                    f = half * 4 + ff
                    S.tr(ps[0:M, ff * 128:(ff + 1) * 128], xres[:, f, sc0:sc0 + M], ident)
                S.copy(ysb[0:M, half * 512:(half + 1) * 512], ps[0:M, 0:512])
            S.dma(DR["ys"], ysb[0:M, :])

        def main_program():
            stage("consts")
            for gi, (b0, nb) in enumerate(groups):
                T = nb * 128
                merged = do_sample and gi == len(groups) - 1 and nb <= 3
                load_x(b0, nb)
                if b0 == 0:
                    S.memset(xres[:, :, 0:112], 0.0, eng="dve")
                    S.memset(xbf[:, :, 0:112], 0.0, eng="dve")
                if merged:
                    load_sample(T)
                    T = T + NS
                stage("loadx")
                for l in range(depth):
                    if not (pref.get("gqk") and pref["gqk"][0] == l):
                        load_wtok(l)
                    stage("wtok")
                    phase_a(l, T)
                    stage("phase_a")
                    if merged:
                        hooks.extend(sample_parts(l, nb * 128))
                    if not merged:
                        S.nrot = 8
                    block_phase(l, b0, nb)
                    S.nrot = 6
                    while hooks:
                        hooks.pop(0)()
                    stage("blocks")
                    last_gl = (gi == len(groups) - 1 and l == depth - 1)
                    dense_tail(l, T, b0 == 0, None if last_gl else (l + 1) % depth)
                    stage("dense")
                store_y(b0, nb)
                if merged:
                    store_sample(nb * 128)
                stage("store_y")
            if do_sample and not merged:
                M = NS
                load_sample(0)
                for l in range(depth):
                    load_wtok(l)
                    phase_a(l, M)
                    stage("s_phase_a")
                    for p in sample_parts(l, 0):
                        p()
                    stage("s_mixer")
                    dense_tail(l, M, False)
                store_sample(0)

        try:
            main_program()
        except _Stop:
            pass
        S.emit()
    return nc


def _consts():
    j = np.arange(128)[:, None]
    i = np.arange(128)[None, :]
    mcur = (j <= i).astype(np.float32)
    mprev = (j > i).astype(np.float32)
    m0 = ((j <= i) & (j >= 112)).astype(np.float32)
    mprev1 = ((j > i) & (j >= 112)).astype(np.float32)
    masks = np.stack([np.tile(m, (1, 4)) for m in (mcur, mprev, m0, mprev1)], axis=1)
    uneg = (-(1.0 / 16.0) * (j <= i)).astype(np.float32)
    unegI = (-(1.0 / 16.0) * np.eye(128)).astype(np.float32)
    inv = (500000.0 ** (-np.arange(8, dtype=np.float32) * 2.0 / 16.0)).astype(np.float32)
    slot = np.arange(NBLK)[None, :] * 128 + np.arange(128)[:, None]
    pos = np.maximum(slot - 112, 0).astype(np.float32)
    ang = pos[:, :, None] * inv[None, None, :]
    cs = np.concatenate([np.cos(ang), np.sin(ang)], -1).astype(np.float32)
    angs = (np.float32(8192.0) * inv)[None, :].repeat(NS, 0)
    css = np.concatenate([np.cos(angs), np.sin(angs)], -1).astype(np.float32)
    return dict(ident=np.eye(128, dtype=np.float32), uneg=uneg, unegI=unegI,
                masks=np.ascontiguousarray(masks), cs=np.ascontiguousarray(cs), css=np.ascontiguousarray(css))


_NC_CACHE = {}


def _prep_shared(meta_tokens, w_in, w_a2, b_a, attn_sink, gla_norm_g, w_proj_a, w_proj_b, w_out, ln1_g, ln1_b,
                 w_up, w_down, ln2_g, ln2_b):
    f = lambda a: np.ascontiguousarray(np.asarray(a, dtype=np.float32))
    w_in = f(w_in)
    lnp = np.stack([f(ln1_g), f(ln1_b), f(ln2_g), f(ln2_b)], axis=1)
    lnp = lnp.reshape(4, 4, 8, 128).transpose(3, 0, 1, 2)
    ucols = np.concatenate([np.arange(c, c + 512) for c in (768, 1808, 2320, 2832, 3344, 3856)])
    win_u = w_in[:, :, ucols].reshape(4, 8, 128, 6, 512).transpose(0, 3, 2, 1, 4).reshape(4, 6, 128, 4096)
    tcols = np.concatenate([np.arange(0, 768), np.arange(1024, 1792)])
    wtok = w_in[:, :, tcols].reshape(4, 8, 128, 1536).transpose(0, 2, 1, 3).reshape(4, 128, 12288)
    wglr = w_in[:, :, 1792:1808].reshape(4, 8, 128, 16).transpose(0, 2, 1, 3).reshape(4, 128, 128)
    wpa_u = f(w_proj_a).reshape(4, 4, 128, 2, 512).transpose(0, 3, 2, 1, 4).reshape(4, 2, 128, 2048)
    wpb_u = f(w_proj_b).reshape(4, 4, 128, 2, 512).transpose(0, 3, 2, 1, 4).reshape(4, 2, 128, 2048)
    wout_u = f(w_out).reshape(4, 8, 128, 2, 512).transpose(0, 3, 2, 1, 4).reshape(4, 2, 128, 4096)
    wup_u = f(w_up).reshape(4, 8, 128, 8, 512).transpose(0, 3, 2, 1, 4).reshape(4, 8, 128, 4096)
    wdn_u = f(w_down).reshape(4, 4, 8, 128, 2, 512).transpose(0, 4, 1, 3, 2, 5).reshape(4, 8, 128, 4096)
    cst = _consts()
    return dict(
        meta=f(meta_tokens), wtok=f(wtok), win_u=f(win_u), wglr=f(wglr),
        wa2=f(np.concatenate([np.transpose(f(w_a2), (1, 0, 2)), f(b_a)[None]], 0)),
        sink=f(attn_sink).reshape(1, 32), gng=f(np.transpose(f(gla_norm_g), (1, 0))), wpa_u=f(wpa_u), wpb_u=f(wpb_u),
        wout_u=f(wout_u), lnp=f(lnp), wup_u=f(wup_u), wdn_u=f(wdn_u), **cst)


def kernel(x_prompt, x_sample, cache_k_win, cache_v_win, state_gla, meta_tokens, w_in, w_a2, b_a, attn_sink,
           gla_norm_g, w_proj_a, w_proj_b, w_out, ln1_g, ln1_b, w_up, w_down, ln2_g, ln2_b):
    f = lambda a: np.ascontiguousarray(np.asarray(a, dtype=np.float32))
    x_prompt, x_sample, cache_k_win, cache_v_win, state_gla = map(f, (x_prompt, x_sample, cache_k_win, cache_v_win, state_gla))
    if "nc" not in _NC_CACHE:
        _NC_CACHE["nc"] = build()
    nc = _NC_CACHE["nc"]
    shared = _prep_shared(meta_tokens, w_in, w_a2, b_a, attn_sink, gla_norm_g, w_proj_a, w_proj_b, w_out,
                          ln1_g, ln1_b, w_up, w_down, ln2_g, ln2_b)
    in_maps = []
    zero_xp = np.zeros_like(x_prompt[0])
    for c in range(8):
        sl = slice(c * NS, (c + 1) * NS)
        m = dict(shared)
        m["xp"] = x_prompt[REAL.index(c)] if c in REAL else zero_xp
        m["xs"] = f(x_sample[sl, 0, :])
        m["ck"] = f(cache_k_win[:, sl].reshape(4, NS, 128, 128))
        m["cv"] = f(cache_v_win[:, sl].reshape(4, NS, 128, 128))
        m["sg"] = f(state_gla[:, sl])
        in_maps.append(m)
    res = run_bass_kernel_spmd(nc, in_maps, core_ids=list(range(8)))
    rs = res.results
    y_prompt = np.stack([rs[c]["y"] for c in REAL], 0)
    y_sample = np.concatenate([rs[c]["ys"] for c in range(8)], 0).reshape(128, 1, 1024)
    pk = np.stack([rs[c]["pk"] for c in REAL], 1).reshape(4, 4, 128, 2, 64)
    pv = np.stack([rs[c]["pv"] for c in REAL], 1).reshape(4, 4, 128, 2, 64)
    pst = np.stack([rs[c]["pst"] for c in REAL], 1)
    sk = np.concatenate([rs[c]["sk"] for c in range(8)], 1).reshape(4, 128, 128, 2, 64)
    sv = np.concatenate([rs[c]["sv"] for c in range(8)], 1).reshape(4, 128, 128, 2, 64)
    sst = np.concatenate([rs[c]["sst"] for c in range(8)], 1)
    return tuple(np.ascontiguousarray(a, dtype=np.float32) for a in (y_prompt, y_sample, pk, pv, pst, sk, sv, sst))
```
